# Optimizing a Trainium2 kernel written in Bass

```python
import math
import jax, jax.numpy as jnp
from jax import lax
import numpy as np

D_MODEL = 1024
BATCH = 2
SEQ = 8192
DEPTH = 2

GRID_W = 64
N_MEM = 256
D_MIX = D_MODEL
EPS = 1e-6

ATTN_HEADS = 8
ATTN_KV_HEADS = 2
ATTN_GROUP = ATTN_HEADS // ATTN_KV_HEADS
HEAD_DIM = 64
ATTN_W = ATTN_HEADS * HEAD_DIM
Q_BLOCK = 128
ROPE_THETA = 10000.0

CONV_CH = D_MIX // 4
CONV_WIDTH = 31

MLSTM_HEADS = 4
MLSTM_HEAD_DIM = 64
MLSTM_W = MLSTM_HEADS * MLSTM_HEAD_DIM
MLSTM_CHUNK = 128
N_DIRS = 2
N_GATES = N_DIRS * 2 * MLSTM_HEADS

XATTN_HEADS = 4
XATTN_HEAD_DIM = D_MODEL // XATTN_HEADS

D_FF = 2816

OFF_Q = 0
OFF_K = OFF_Q + ATTN_W
OFF_V = OFF_K + ATTN_KV_HEADS * HEAD_DIM
OFF_CONV = OFF_V + ATTN_KV_HEADS * HEAD_DIM
OFF_MQ = OFF_CONV + 2 * CONV_CH
OFF_MK = OFF_MQ + MLSTM_W
OFF_MV = OFF_MK + MLSTM_W
OFF_MO = OFF_MV + MLSTM_W
OFF_MG = OFF_MO + MLSTM_W
N_IN = OFF_MG + N_GATES

kernel_name = "hybrid_parallel_attn_conv_mlstm_encoder"


def rms_norm(x, g):
    xf = x.astype(jnp.float32)
    y = xf * lax.rsqrt(jnp.mean(xf * xf, axis=-1, keepdims=True) + EPS)
    return (y * g.astype(jnp.float32)).astype(x.dtype)


def swiglu_ffn(x, g, w13, w2):
    h = rms_norm(x, g)
    a, b = jnp.split(h @ w13, 2, axis=-1)
    return (jax.nn.silu(a) * b) @ w2


def axial_rope_table(seq_len):
    rows = seq_len // GRID_W
    row_idx = jnp.repeat(jnp.arange(rows, dtype=jnp.int32), GRID_W).astype(jnp.float32)
    col_idx = jnp.tile(jnp.arange(GRID_W, dtype=jnp.int32), rows).astype(jnp.float32)
    n_freq = HEAD_DIM // 4
    inv_freq = jnp.float32(ROPE_THETA) ** (-jnp.arange(n_freq, dtype=jnp.float32) / n_freq)
    ang = jnp.concatenate([row_idx[:, None] * inv_freq, col_idx[:, None] * inv_freq], axis=-1)
    return jnp.cos(ang), jnp.sin(ang)


def apply_rope(x, cos, sin):
    xf = x.astype(jnp.float32).reshape(x.shape[:-1] + (x.shape[-1] // 2, 2))
    x0, x1 = xf[..., 0], xf[..., 1]
    c = cos[None, :, None, :]
    s = sin[None, :, None, :]
    out = jnp.stack([x0 * c - x1 * s, x0 * s + x1 * c], axis=-1).reshape(x.shape)
    return out.astype(x.dtype)


def attention_group(hq, hk, hv, q_gain, k_gain, cos, sin):
    B, S, _ = hq.shape
    q = hq.reshape(B, S, ATTN_HEADS, HEAD_DIM)
    k = hk.reshape(B, S, ATTN_KV_HEADS, HEAD_DIM)
    v = hv.reshape(B, S, ATTN_KV_HEADS, HEAD_DIM)
    q = apply_rope(rms_norm(q, q_gain), cos, sin) * (HEAD_DIM ** -0.5)
    k = apply_rope(rms_norm(k, k_gain), cos, sin)
    nb = S // Q_BLOCK
    qb_all = q.reshape(B, nb, Q_BLOCK, ATTN_KV_HEADS, ATTN_GROUP, HEAD_DIM).transpose(1, 0, 2, 3, 4, 5)

    def one_block(qb):
        s = jnp.einsum('bqhgd,bkhd->bhgqk', qb, k, preferred_element_type=jnp.float32)
        p = jax.nn.softmax(s, axis=-1).astype(v.dtype)
        return jnp.einsum('bhgqk,bkhd->bqhgd', p, v)

    o = lax.map(one_block, qb_all)
    return o.transpose(1, 0, 2, 3, 4, 5).reshape(B, S, ATTN_W)


def conv_group(hc, dw_w, dw_b, ln_g, ln_b):
    a, gate = jnp.split(hc, 2, axis=-1)
    u = a * jax.nn.sigmoid(gate)
    pad = CONV_WIDTH // 2
    u = lax.conv_general_dilated(
        u, dw_w[:, None, :].astype(u.dtype), window_strides=(1,), padding=[(pad, pad)],
        dimension_numbers=('NWC', 'WIO', 'NWC'), feature_group_count=CONV_CH) + dw_b
    uf = u.astype(jnp.float32)
    mu = jnp.mean(uf, axis=-1, keepdims=True)
    var = jnp.mean(jnp.square(uf - mu), axis=-1, keepdims=True)
    un = (uf - mu) * lax.rsqrt(var + EPS) * ln_g.astype(jnp.float32) + ln_b.astype(jnp.float32)
    return jax.nn.silu(un).astype(hc.dtype)


def mlstm_scan(q, k, v, i_pre, log_f):
    B, S, H, D = q.shape
    L = MLSTM_CHUNK
    nc = S // L

    def chunk4(t):
        return t.reshape(B, nc, L, H, D).transpose(1, 0, 3, 2, 4)

    def chunk3(t):
        return t.reshape(B, nc, L, H).transpose(1, 0, 3, 2)

    mask = jnp.tril(jnp.ones((L, L), dtype=bool))
    neg_inf = jnp.float32(-jnp.inf)

    def step(carry, inp):
        C, n, m = carry
        qc, kc, vc, ic, fc = inp
        b = jnp.cumsum(fc, axis=-1)
        logw = b[..., :, None] - b[..., None, :] + ic[..., None, :]
        logw = jnp.where(mask, logw, neg_inf)
        m_inter = b + m[..., None]
        m_t = jnp.maximum(jnp.max(logw, axis=-1), m_inter)
        w = jnp.exp(logw - m_t[..., None])
        s = jnp.einsum('bhtd,bhsd->bhts', qc, kc) * w
        inter = jnp.exp(m_inter - m_t)
        num = jnp.einsum('bhts,bhsd->bhtd', s, vc) + inter[..., None] * jnp.einsum('bhed,bhtd->bhte', C, qc)
        den = jnp.sum(s, axis=-1) + inter * jnp.einsum('bhd,bhtd->bht', n, qc)
        h = num / jnp.maximum(jnp.abs(den), jnp.exp(-m_t))[..., None]
        b_end = b[..., -1]
        logw_end = b_end[..., None] - b + ic
        m_new = jnp.maximum(b_end + m, jnp.max(logw_end, axis=-1))
        w_end = jnp.exp(logw_end - m_new[..., None])
        decay = jnp.exp(b_end + m - m_new)
        C_new = decay[..., None, None] * C + jnp.einsum('bhs,bhse,bhsd->bhed', w_end, vc, kc)
        n_new = decay[..., None] * n + jnp.einsum('bhs,bhsd->bhd', w_end, kc)
        return (C_new, n_new, m_new), h

    init = (jnp.zeros((B, H, D, D), jnp.float32), jnp.zeros((B, H, D), jnp.float32),
            jnp.zeros((B, H), jnp.float32))
    _, hs = lax.scan(step, init, (chunk4(q), chunk4(k), chunk4(v), chunk3(i_pre), chunk3(log_f)))
    return hs.transpose(1, 0, 3, 2, 4).reshape(B, S, H, D)


def mlstm_group(hq, hk, hv, ho, hg, gate_b, out_gain):
    B, S, _ = hq.shape
    shp = (B, S, MLSTM_HEADS, MLSTM_HEAD_DIM)
    q = hq.reshape(shp).astype(jnp.float32)
    k = hk.reshape(shp).astype(jnp.float32) * (MLSTM_HEAD_DIM ** -0.5)
    v = hv.reshape(shp).astype(jnp.float32)
    g = hg.astype(jnp.float32).reshape(B, S, N_DIRS, 2, MLSTM_HEADS) + gate_b.astype(jnp.float32)
    i_pre = g[:, :, :, 0, :]
    log_f = jax.nn.log_sigmoid(g[:, :, :, 1, :])
    h_fwd = mlstm_scan(q, k, v, i_pre[:, :, 0], log_f[:, :, 0])
    rev = lambda t: jnp.flip(t, axis=1)
    h_bwd = rev(mlstm_scan(rev(q), rev(k), rev(v), rev(i_pre[:, :, 1]), rev(log_f[:, :, 1])))
    h = rms_norm(h_fwd + h_bwd, out_gain)
    return (jax.nn.sigmoid(ho.astype(jnp.float32)) * h.reshape(B, S, MLSTM_W)).astype(hq.dtype)


def memory_cross_attention(x, mem, x_gain, mem_gain, wq, wkv, q_gain, k_gain, wo):
    B, S, _ = x.shape
    M = mem.shape[1]
    q = (rms_norm(x, x_gain) @ wq).reshape(B, S, XATTN_HEADS, XATTN_HEAD_DIM)
    k, v = jnp.split(rms_norm(mem, mem_gain) @ wkv, 2, axis=-1)
    k = k.reshape(B, M, XATTN_HEADS, XATTN_HEAD_DIM)
    v = v.reshape(B, M, XATTN_HEADS, XATTN_HEAD_DIM)
    q = rms_norm(q, q_gain) * (XATTN_HEAD_DIM ** -0.5)
    k = rms_norm(k, k_gain)
    s = jnp.einsum('bqhd,bmhd->bhqm', q, k, preferred_element_type=jnp.float32)
    p = jax.nn.softmax(s, axis=-1).astype(v.dtype)
    o = jnp.einsum('bhqm,bmhd->bqhd', p, v).reshape(B, S, XATTN_HEADS * XATTN_HEAD_DIM)
    return o @ wo


def setup_inputs(seed: int = 0) -> dict:
    key = jax.random.key(seed)
    ks = iter(jax.random.split(key, 32))
    L = DEPTH

    def nrm(shape, scale):
        return jax.random.normal(next(ks), shape, jnp.float32) * scale

    def gain(shape):
        return 1.0 + nrm(shape, 0.02)

    x = nrm((BATCH, SEQ, D_MODEL), 1.0)
    mem = nrm((BATCH, N_MEM, D_MODEL), 1.0)
    ffn1_norm = gain((L, D_MODEL))
    ffn1_w13 = nrm((L, D_MODEL, 2 * D_FF), D_MODEL ** -0.5)
    ffn1_w2 = nrm((L, D_FF, D_MODEL), D_FF ** -0.5)
    mix_norm = gain((L, D_MODEL))
    w_in = nrm((L, D_MODEL, N_IN), D_MODEL ** -0.5)
    attn_q_norm = gain((L, HEAD_DIM))
    attn_k_norm = gain((L, HEAD_DIM))
    conv_dw_w = nrm((L, CONV_WIDTH, CONV_CH), CONV_WIDTH ** -0.5)
    conv_dw_b = nrm((L, CONV_CH), 0.02)
    conv_ln_g = gain((L, CONV_CH))
    conv_ln_b = nrm((L, CONV_CH), 0.02)
    i_bias = nrm((L, N_DIRS, 1, MLSTM_HEADS), 0.1)
    f_bias = jnp.linspace(3.0, 6.0, MLSTM_HEADS, dtype=jnp.float32) + nrm((L, N_DIRS, 1, MLSTM_HEADS), 0.1)
    mlstm_gate_b = jnp.concatenate([i_bias, f_bias], axis=2)
    mlstm_out_norm = gain((L, MLSTM_HEADS, MLSTM_HEAD_DIM))
    w_out = nrm((L, D_MIX, D_MODEL), D_MIX ** -0.5)
    xattn_norm = gain((L, D_MODEL))
    mem_norm = gain((L, D_MODEL))
    xattn_wq = nrm((L, D_MODEL, XATTN_HEADS * XATTN_HEAD_DIM), D_MODEL ** -0.5)
    xattn_wkv = nrm((L, D_MODEL, 2 * XATTN_HEADS * XATTN_HEAD_DIM), D_MODEL ** -0.5)
    xattn_q_norm = gain((L, XATTN_HEAD_DIM))
    xattn_k_norm = gain((L, XATTN_HEAD_DIM))
    xattn_wo = nrm((L, XATTN_HEADS * XATTN_HEAD_DIM, D_MODEL), (XATTN_HEADS * XATTN_HEAD_DIM) ** -0.5)
    ffn2_norm = gain((L, D_MODEL))
    ffn2_w13 = nrm((L, D_MODEL, 2 * D_FF), D_MODEL ** -0.5)
    ffn2_w2 = nrm((L, D_FF, D_MODEL), D_FF ** -0.5)
    return {
        "x": x, "mem": mem,
        "ffn1_norm": ffn1_norm, "ffn1_w13": ffn1_w13, "ffn1_w2": ffn1_w2,
        "mix_norm": mix_norm, "w_in": w_in,
        "attn_q_norm": attn_q_norm, "attn_k_norm": attn_k_norm,
        "conv_dw_w": conv_dw_w, "conv_dw_b": conv_dw_b, "conv_ln_g": conv_ln_g, "conv_ln_b": conv_ln_b,
        "mlstm_gate_b": mlstm_gate_b, "mlstm_out_norm": mlstm_out_norm,
        "w_out": w_out,
        "xattn_norm": xattn_norm, "mem_norm": mem_norm, "xattn_wq": xattn_wq, "xattn_wkv": xattn_wkv,
        "xattn_q_norm": xattn_q_norm, "xattn_k_norm": xattn_k_norm, "xattn_wo": xattn_wo,
        "ffn2_norm": ffn2_norm, "ffn2_w13": ffn2_w13, "ffn2_w2": ffn2_w2,
    }


def reference(x, mem, ffn1_norm, ffn1_w13, ffn1_w2, mix_norm, w_in, attn_q_norm, attn_k_norm,
              conv_dw_w, conv_dw_b, conv_ln_g, conv_ln_b, mlstm_gate_b, mlstm_out_norm, w_out,
              xattn_norm, mem_norm, xattn_wq, xattn_wkv, xattn_q_norm, xattn_k_norm, xattn_wo,
              ffn2_norm, ffn2_w13, ffn2_w2):
    cos, sin = axial_rope_table(x.shape[1])
    for l in range(DEPTH):
        x = x + 0.5 * swiglu_ffn(x, ffn1_norm[l], ffn1_w13[l], ffn1_w2[l])
        h = rms_norm(x, mix_norm[l]) @ w_in[l]
        y_attn = attention_group(h[..., OFF_Q:OFF_K], h[..., OFF_K:OFF_V], h[..., OFF_V:OFF_CONV],
                                 attn_q_norm[l], attn_k_norm[l], cos, sin)
        y_conv = conv_group(h[..., OFF_CONV:OFF_MQ], conv_dw_w[l], conv_dw_b[l], conv_ln_g[l], conv_ln_b[l])
        y_mlstm = mlstm_group(h[..., OFF_MQ:OFF_MK], h[..., OFF_MK:OFF_MV], h[..., OFF_MV:OFF_MO],
                              h[..., OFF_MO:OFF_MG], h[..., OFF_MG:N_IN], mlstm_gate_b[l], mlstm_out_norm[l])
        x = x + jnp.concatenate([y_attn, y_conv, y_mlstm], axis=-1) @ w_out[l]
        x = x + memory_cross_attention(x, mem, xattn_norm[l], mem_norm[l], xattn_wq[l], xattn_wkv[l],
                                       xattn_q_norm[l], xattn_k_norm[l], xattn_wo[l])
        x = x + 0.5 * swiglu_ffn(x, ffn2_norm[l], ffn2_w13[l], ffn2_w2[l])
    return x
```

```python
import numpy as np
import ml_dtypes
from contextlib import ExitStack
import concourse.bass as bass
import concourse.mybir as mybir
from concourse.bass_utils import run_bass_kernel_spmd

F32 = mybir.dt.float32
BF16 = mybir.dt.bfloat16
AF = mybir.ActivationFunctionType
ALU = mybir.AluOpType
AX = mybir.AxisListType

NCORES = 8
NT = 2048
NTILE = 16
DM = 1024
DFF = 2816
NFC = 22
EPS = 1e-6
SEQ = 8192
GRID_W = 64
N_MEM = 256
CONV_W = 31
PAD = 15
NEG = -30000.0
SKIP = set()
PASSA_INTERLEAVE = True
EXP2 = True


class Eng:
    def __init__(self, name, h, sem):
        self.name = name
        self.h = h
        self.sem = sem
        self.n = 0
        self.pending = False
        self.clock = {}


class K:
    NDS = 24

    def __init__(self, nc, es):
        self.nc = nc
        self.eng = {}
        for name, h in (("pe", nc.tensor), ("act", nc.scalar), ("dve", nc.vector),
                        ("pool", nc.gpsimd), ("sp", nc.sync)):
            sem = es.enter_context(nc.semaphore("s_" + name))
            self.eng[name] = Eng(name, h, sem)
        self.dsem = [es.enter_context(nc.semaphore("d%d" % i)) for i in range(self.NDS)]
        self.dcount = [0] * self.NDS
        self.dnext = 0
        self.dnext_p = 0
        self.last_w = {}
        self.readers = {}
        self.nwaits = 0
        self.nops = 0
        self.dumps = []

    def _merge(self, eng, clk):
        c = eng.clock
        for kk, v in clk.items():
            if c.get(kk, 0) < v:
                c[kk] = v

    def _wait(self, eng, t):
        if t[0] == 'E':
            _, en, n, clk = t
            if en == eng.name and en == 'pe':
                return
            if eng.clock.get(en, 0) >= n:
                return
            src = self.eng[en]
            assert src.n >= n, "waiting on unsignalled %s op (n=%d have %d)" % (en, n, src.n)
            eng.h.wait_ge(src.sem, n)
            self.nwaits += 1
            eng.clock[en] = n
            self._merge(eng, clk)
        else:
            _, si, val, clk = t
            key = ('d', si)
            if eng.clock.get(key, 0) >= val:
                return
            eng.h.wait_ge(self.dsem[si], val)
            self.nwaits += 1
            eng.clock[key] = val
            self._merge(eng, clk)

    def _deps(self, eng, reads, writes):
        for r in reads:
            t = self.last_w.get(r)
            if t is not None:
                self._wait(eng, t)
        for w in writes:
            t = self.last_w.get(w)
            if t is not None:
                self._wait(eng, t)
            rd = self.readers.get(w)
            if rd:
                for t in rd.values():
                    self._wait(eng, t)

    def _record(self, t, rkey, reads, writes):
        for r in reads:
            self.readers.setdefault(r, {})[rkey] = t
        for w in writes:
            self.last_w[w] = t
            self.readers[w] = {}

    @staticmethod
    def _is_ps(key):
        return isinstance(key, str) and key.startswith('ps') and key[2:].isdigit()

    def op(self, en, fn, reads=(), writes=(), signal=True):
        psr = [r for r in reads if self._is_ps(r)]
        if psr:
            reads = [r for r in reads if not self._is_ps(r)]
            writes = list(writes) + [r for r in psr if r not in writes]
        eng = self.eng[en]
        self._deps(eng, reads, writes)
        ins = fn(eng.h)
        self.nops += 1
        if signal:
            eng.n += 1
            ins.then_inc(eng.sem, 1)
            eng.pending = False
            t = ('E', en, eng.n, dict(eng.clock))
        else:
            eng.pending = True
            t = ('E', en, eng.n + 1, dict(eng.clock))
        self._record(t, en, reads, writes)
        return t

    def dma(self, qn, out, in_, reads=(), writes=()):
        eng = self.eng[qn]
        self._deps(eng, reads, writes)
        if qn == 'pool':
            si = self.dnext_p % 8
            self.dnext_p += 1
        else:
            si = 8 + self.dnext % (self.NDS - 8)
            self.dnext += 1
        cnt = self.dcount[si]
        if cnt > 0:
            self._wait(eng, ('D', si, 16 * cnt, {}))
        eng.h.dma_start(out=out, in_=in_).then_inc(self.dsem[si], 16)
        self.nops += 1
        self.dcount[si] = cnt + 1
        t = ('D', si, 16 * (cnt + 1), dict(eng.clock))
        self._record(t, ('d', si), reads, writes)
        return t

    def barrier(self):
        ts = []
        for e in self.eng.values():
            assert not e.pending, "pending unsignalled ops on " + e.name
            if e.n > 0:
                ts.append(('E', e.name, e.n, {}))
        for si in range(self.NDS):
            if self.dcount[si] > 0:
                ts.append(('D', si, 16 * self.dcount[si], {}))
        for e in self.eng.values():
            for t in ts:
                self._wait(e, t)
        self.last_w = {}
        self.readers = {}

    def dump(self, name, ap, shape, dtype, reads):
        d = self.nc.dram_tensor("dbg_" + name, list(shape), dtype, kind="ExternalOutput").ap()
        self.dma('sp', d, ap, reads=reads)
        self.dumps.append("dbg_" + name)


class Ctx:
    pass


def drain(g):
    for _ in g:
        pass


def interleave(gens):
    gens = list(gens)
    while gens:
        for g in list(gens):
            try:
                next(g)
            except StopIteration:
                gens.remove(g)


_UNIQ = [0]


def sbuf(nc, es, name, shape, dt):
    _UNIQ[0] += 1
    return es.enter_context(nc.sbuf_tensor("%s_%d" % (name, _UNIQ[0]), list(shape), dt))


def rms_block(k, C, xT, t0, gcol, dst_fn, dst_key, xkeys, bank=7):
    ps, sq, rstd = C.ps, C.sq, C.rstd
    pk = 'ps%d' % bank
    for kc in range(8):
        j = kc % 2
        k.op('act', lambda e: e.activation(out=sq[:, j, :], in_=xT[:, kc, t0:t0 + 512], func=AF.Square),
             reads=xkeys, writes=[('sq', j)])
        k.op('pe', lambda e: e.matmul(ps[bank][:], lhsT=C.ones_bf[:], rhs=sq[:, j, :], start=(kc == 0), stop=(kc == 7)),
             reads=[('sq', j)], writes=[pk])
    k.op('act', lambda e: e.activation(out=rstd[:], in_=ps[bank][:], func=AF.Sqrt, scale=1.0 / DM, bias=C.eps_col[:, 0:1]),
         reads=[pk], writes=['rstd'])
    k.op('dve', lambda e: e.reciprocal(out=rstd[:], in_=rstd[:]), reads=['rstd'], writes=['rstd'])
    for kc in range(8):
        k.op('dve', lambda e: e.scalar_tensor_tensor(out=dst_fn(kc), in0=xT[:, kc, t0:t0 + 512], scalar=gcol[:, kc:kc + 1],
                                                     in1=rstd[:], op0=ALU.mult, op1=ALU.mult),
             reads=list(xkeys) + ['rstd'], writes=[dst_key])


def xkeys_of(t0, n=512):
    return [('xT', i) for i in range(t0 // 512, (t0 + n - 1) // 512 + 1)]


def ffn(k, nc, C, xT, gcol, w13r, w2r):
    with ExitStack() as es:
        xnT = sbuf(nc, es, "f_xnT", [128, 8, 1024], BF16)
        gT = sbuf(nc, es, "f_gT", [128, NFC, 1024], BF16)
        w13s = sbuf(nc, es, "f_w13s", [128, 2, 2, 8, 128], BF16)
        w2s = sbuf(nc, es, "f_w2s", [128, 2, NFC, 128], BF16)
        sa = sbuf(nc, es, "f_sa", [128, 2, 512], F32)
        ps = C.ps
        wc13 = 0
        wc2 = 0
        for half in range(2):
            for tb in range(2):
                t0 = half * 1024 + tb * 512
                rms_block(k, C, xT, t0, gcol, lambda kc: xnT[:, kc, tb * 512:(tb + 1) * 512], ('xnT', tb), xkeys_of(t0))
            for fc in range(NFC):
                wb = wc13 % 2
                wc13 += 1
                k.dma('pool', w13s[:, wb], w13r[fc], writes=[('w13s', wb)])
                bs = (fc % 2) * 4
                for kc in range(8):
                    for ab in range(2):
                        for tb in range(2):
                            bank = bs + ab * 2 + tb
                            k.op('pe', lambda e: e.matmul(ps[bank][:], lhsT=w13s[:, wb, ab, kc, :],
                                                          rhs=xnT[:, kc, tb * 512:(tb + 1) * 512],
                                                          start=(kc == 0), stop=(kc == 7)),
                                 reads=[('w13s', wb), ('xnT', tb)], writes=['ps%d' % bank], signal=(kc == 7))
                for tb in range(2):
                    ba, bb = bs + tb, bs + 2 + tb
                    k.op('act', lambda e: e.activation(out=sa[:, tb, :], in_=ps[ba][:], func=AF.Silu),
                         reads=['ps%d' % ba], writes=[('sa', tb)])
                    k.op('dve', lambda e: e.tensor_tensor(out=gT[:, fc, tb * 512:(tb + 1) * 512], in0=sa[:, tb, :],
                                                          in1=ps[bb][:], op=ALU.mult),
                         reads=[('sa', tb), 'ps%d' % bb], writes=[('gT', fc, tb)])
            for dc in range(8):
                wb = wc2 % 2
                wc2 += 1
                k.dma('pool', w2s[:, wb], w2r[dc], writes=[('w2s', wb)])
                bs = (dc % 4) * 2
                for fc in range(NFC):
                    for tb in range(2):
                        bank = bs + tb
                        k.op('pe', lambda e: e.matmul(ps[bank][:], lhsT=w2s[:, wb, fc, :],
                                                      rhs=gT[:, fc, tb * 512:(tb + 1) * 512],
                                                      start=(fc == 0), stop=(fc == NFC - 1)),
                             reads=[('w2s', wb), ('gT', fc, tb)], writes=['ps%d' % bank], signal=(fc == NFC - 1))
                for tb in range(2):
                    bank = bs + tb
                    t0 = half * 1024 + tb * 512
                    xk = ('xT', t0 // 512)
                    k.op('dve', lambda e: e.scalar_tensor_tensor(out=xT[:, dc, t0:t0 + 512], in0=ps[bank][:], scalar=0.5,
                                                                 in1=xT[:, dc, t0:t0 + 512], op0=ALU.mult, op1=ALU.add),
                         reads=['ps%d' % bank, xk], writes=[xk])
        k.barrier()


def qk_norm_rope(k, C, T, src, srckey, nh, gain_bc, scale, ti, out, outkey, tag):
    w = nh * 64
    sqv, ssq, t1, t2, kn = T.sqv, T.ssq, T.t1, T.t2, T.kn
    kkn = tag + ('sqv' if T.kn is T.sqv else 'kn')
    yield k.op('act', lambda e: e.activation(out=sqv[:, 0:w], in_=src, func=AF.Square), reads=[srckey], writes=[tag + 'sqv'])
    yield k.op('dve', lambda e: e.tensor_reduce(out=ssq[:, 0:nh], in_=sqv[:, 0:w].rearrange("p (h d) -> p h d", d=64),
                                          axis=AX.X, op=ALU.add), reads=[tag + 'sqv'], writes=[tag + 'ssq'])
    yield k.op('act', lambda e: e.activation(out=ssq[:, 0:nh], in_=ssq[:, 0:nh], func=AF.Sqrt, scale=1.0 / 64, bias=C.eps_col[:, 0:1]),
         reads=[tag + 'ssq'], writes=[tag + 'ssq'])
    yield k.op('dve', lambda e: e.reciprocal(out=ssq[:, 0:nh], in_=ssq[:, 0:nh]), reads=[tag + 'ssq'], writes=[tag + 'ssq'])
    knv = kn[:, 0:w].rearrange("p (h d) -> p h d", d=64)
    yield k.op('dve', lambda e: e.tensor_tensor(out=knv, in0=src.rearrange("p (h d) -> p h d", d=64),
                                          in1=ssq[:, 0:nh].unsqueeze(2).to_broadcast([128, nh, 64]), op=ALU.mult),
         reads=[srckey, tag + 'ssq'], writes=[kkn])
    yield k.op('dve', lambda e: e.scalar_tensor_tensor(out=knv, in0=knv, scalar=float(scale),
                                                 in1=gain_bc.unsqueeze(1).to_broadcast([128, nh, 64]),
                                                 op0=ALU.mult, op1=ALU.mult),
         reads=[kkn], writes=[kkn])
    x0 = knv[:, :, 0:32]
    x1 = knv[:, :, 32:64]
    cb = C.cos[:, ti, :].unsqueeze(1).to_broadcast([128, nh, 32])
    sb_ = C.sin[:, ti, :].unsqueeze(1).to_broadcast([128, nh, 32])
    ov = out.rearrange("p (h d) -> p h d", d=64)
    t1v = t1[:, 0:nh * 32].rearrange("p (h d) -> p h d", d=32)
    t2v = t2[:, 0:nh * 32].rearrange("p (h d) -> p h d", d=32)
    yield k.op('pool', lambda e: e.tensor_tensor(out=t1v, in0=x0, in1=cb, op=ALU.mult), reads=[kkn], writes=[tag + 't1'])
    yield k.op('dve', lambda e: e.tensor_tensor(out=t2v, in0=x1, in1=sb_, op=ALU.mult), reads=[kkn], writes=[tag + 't2'])
    yield k.op('dve', lambda e: e.tensor_tensor(out=ov[:, :, 0:32], in0=t1v, in1=t2v, op=ALU.subtract),
         reads=[tag + 't1', tag + 't2'], writes=[outkey])
    yield k.op('pool', lambda e: e.tensor_tensor(out=t1v, in0=x0, in1=sb_, op=ALU.mult), reads=[kkn, outkey], writes=[tag + 't1'])
    yield k.op('dve', lambda e: e.tensor_tensor(out=t2v, in0=x1, in1=cb, op=ALU.mult), reads=[kkn, outkey], writes=[tag + 't2'])
    yield k.op('dve', lambda e: e.tensor_tensor(out=ov[:, :, 32:64], in0=t1v, in1=t2v, op=ALU.add),
         reads=[tag + 't1', tag + 't2'], writes=[outkey])


def mlstm_gates(k, C, G, ps_g, gkey, pb, pbkey):
    g, e1, lf, a, tmp = G.g, G.e1, G.lf, G.a, G.tmp
    yield k.op('dve', lambda e: e.tensor_tensor(out=g[:], in0=ps_g, in1=C.gateb_bc[:], op=ALU.add), reads=[gkey], writes=[G.pfx + '.g'])
    g4 = g[:].rearrange("p (d i h) -> p d i h", d=2, i=2)
    e1v = e1[:].rearrange("p (d h) -> p d h", d=2)
    yield k.op('act', lambda e: e.activation(out=e1v, in_=g4[:, :, 1, :], func=AF.Exp, scale=-1.0), reads=[G.pfx + '.g'], writes=[G.pfx + '.e1'])
    yield k.op('act', lambda e: e.activation(out=e1[:], in_=e1[:], func=AF.Ln, bias=C.one_col[:, 0:1]), reads=[G.pfx + '.e1'], writes=[G.pfx + '.e1'])
    yield k.op('dve', lambda e: e.tensor_scalar(out=lf[:], in0=e1[:], scalar1=-1.0, scalar2=None, op0=ALU.mult), reads=[G.pfx + '.e1'], writes=[G.pfx + '.lf'])
    yield k.op('pe', lambda e: e.matmul(pb[:, 0:4], lhsT=C.tri_f[:], rhs=lf[:, 0:4], start=True, stop=True), reads=[G.pfx + '.lf'], writes=[pbkey], signal=False)
    yield k.op('pe', lambda e: e.matmul(pb[:, 4:8], lhsT=C.tri_b[:], rhs=lf[:, 4:8], start=True, stop=True), reads=[G.pfx + '.lf'], writes=[pbkey], signal=False)
    yield k.op('pe', lambda e: e.matmul(pb[:, 8:16], lhsT=C.ones_f[:], rhs=lf[:], start=True, stop=True), reads=[G.pfx + '.lf'], writes=[pbkey])
    pbs = G.pbs
    pk = G.pfx + '.pbs'
    yield k.op('dve', lambda e: e.tensor_copy(out=pbs[:], in_=pb), reads=[pbkey], writes=[pk])
    av = a[:].rearrange("p (d h) -> p d h", d=2)
    yield k.op('dve', lambda e: e.tensor_tensor(out=av, in0=g4[:, :, 0, :], in1=pbs[:, 0:8].rearrange("p (d h) -> p d h", d=2), op=ALU.subtract),
         reads=[G.pfx + '.g', pk], writes=[G.pfx + '.a'])
    yield k.op('act', lambda e: e.activation(out=G.eb[:], in_=pbs[:, 0:8], func=AF.Exp), reads=[pk], writes=[G.pfx + '.eb'])
    yield k.op('act', lambda e: e.activation(out=G.edec[:], in_=pbs[:, 8:16], func=AF.Exp), reads=[pk], writes=[G.pfx + '.edec'])
    yield k.op('pool', lambda e: e.tensor_tensor(out=tmp[:], in0=a[:], in1=pbs[:, 8:16], op=ALU.add), reads=[G.pfx + '.a', pk], writes=[G.pfx + '.tmp'])
    yield k.op('act', lambda e: e.activation(out=G.wend[:], in_=tmp[:], func=AF.Exp), reads=[G.pfx + '.tmp'], writes=[G.pfx + '.wend'])


def mlstm_local(k, C, G, ps_kv, kvkey, pl, plkeys, mv_aug_tile, mvkey):
    kw = G.kw
    yield k.op('act', lambda e: e.activation(out=mv_aug_tile[:, :, 0:64], in_=ps_kv[:, 256:512].rearrange("p (h d) -> p h d", d=64), func=AF.Copy),
         reads=[kvkey], writes=[mvkey])
    for d in range(2):
        yield k.op('dve', lambda e: e.scalar_tensor_tensor(out=kw[:, d], in0=ps_kv[:, 0:256].rearrange("p (h d) -> p h d", d=64), scalar=0.125,
                                                     in1=G.wend[:, d * 4:(d + 1) * 4].unsqueeze(2).to_broadcast([128, 4, 64]),
                                                     op0=ALU.mult, op1=ALU.mult),
             reads=[kvkey, G.pfx + '.wend'], writes=[(G.pfx + '.kw', d)])
        for h in range(4):
            yield k.op('pe', lambda e: e.matmul(pl[d][0:64, h * 80:h * 80 + 65], lhsT=kw[:, d, h, :], rhs=mv_aug_tile[:, h, 0:65], start=True, stop=True),
                 reads=[(G.pfx + '.kw', d), mvkey], writes=[plkeys[d]], signal=(h == 3))


def alloc_gates(nc, es, pfx):
    G = Ctx()
    G.pfx = pfx
    G.g = sbuf(nc, es, pfx + "g", [128, 16], F32)
    G.e1 = sbuf(nc, es, pfx + "e1", [128, 8], F32)
    G.lf = sbuf(nc, es, pfx + "lf", [128, 8], F32)
    G.a = sbuf(nc, es, pfx + "a", [128, 8], F32)
    G.tmp = sbuf(nc, es, pfx + "tmp", [128, 8], F32)
    G.eb = sbuf(nc, es, pfx + "eb", [128, 8], F32)
    G.edec = sbuf(nc, es, pfx + "edec", [128, 8], F32)
    G.wend = sbuf(nc, es, pfx + "wend", [128, 8], F32)
    G.kw = sbuf(nc, es, pfx + "kw", [128, 2, 4, 64], BF16)
    G.pbs = sbuf(nc, es, pfx + "pbs", [128, 16], F32)
    G.loc = sbuf(nc, es, pfx + "loc", [64, 2, 4, 65], F32)
    return G


def p1_mixer(k, nc, C, xT, W, l, X):
    ps = C.ps
    with ExitStack() as es:
        xnb = sbuf(nc, es, "p1_xnb", [128, 8, 512], BF16)
        wkv = sbuf(nc, es, "p1_wkv", [128, 8, 256], BF16)
        wmk = sbuf(nc, es, "p1_wmk", [128, 8, 528], BF16)
        wcv = sbuf(nc, es, "p1_wcv", [128, 8, 512], BF16)
        kTl = sbuf(nc, es, "p1_kTl", [128, NT], BF16)
        vsb = sbuf(nc, es, "p1_vsb", [128, NTILE, 128], BF16)
        halo = sbuf(nc, es, "p1_halo", [128, 2, 30], F32)
        sg = sbuf(nc, es, "p1_sg", [128, 64], F32)
        kr = sbuf(nc, es, "p1_kr", [128, 128], F32)
        mv_aug = sbuf(nc, es, "p1_mvaug", [128, 2, 4, 72], BF16)
        agg = sbuf(nc, es, "p1_agg", [64, 8, 66], F32)
        tmpS = sbuf(nc, es, "p1_tmpS", [64, 4, 65], F32)
        Ts = []
        for sfx in ("a", "b"):
            T = Ctx()
            T.sqv = sbuf(nc, es, "p1_sqv" + sfx, [128, 128], F32)
            T.ssq = sbuf(nc, es, "p1_ssq" + sfx, [128, 8], F32)
            T.t1 = sbuf(nc, es, "p1_t1" + sfx, [128, 64], F32)
            T.t2 = sbuf(nc, es, "p1_t2" + sfx, [128, 64], F32)
            T.kn = sbuf(nc, es, "p1_kn" + sfx, [128, 128], F32)
            Ts.append(T)
        kr2 = sbuf(nc, es, "p1_kr2", [128, 128], F32)
        Gs = [alloc_gates(nc, es, "p1_Ga"), alloc_gates(nc, es, "p1_Gb")]

        k.dma('pool', wkv[:], W.w_kv[l], writes=['wkv'])
        k.dma('pool', wmk[:, :, 0:512], W.w_m[l][:, :, 256:768], writes=['wmk'])
        k.dma('pool', wmk[:, :, 512:528], W.w_m[l][:, :, 1024:1040], writes=['wmk'])
        k.dma('pool', wcv[:], W.w_conv[l], writes=['wcv'])
        k.op('dve', lambda e: e.memset(mv_aug[:], 1.0), writes=[('mv', 0), ('mv', 1)])
        k.op('dve', lambda e: e.memset(agg[:], 0.0), writes=['agg'])
        k.op('dve', lambda e: e.memset(agg[:, :, 65:66], 1.0), reads=[], writes=['agg'])
        gcol = C.pp[:, l, 8:16]
        for tb in range(4):
            t0 = tb * 512
            rms_block(k, C, xT, t0, gcol, lambda kc: xnb[:, kc, :], 'xnb', xkeys_of(t0))
            if tb in (0, 3) and 'halo' not in SKIP:
                c0 = 0 if tb == 0 else 512 - PAD
                for cc in range(4):
                    for kc in range(8):
                        k.op('pe', lambda e: e.matmul(ps[6][:, cc * 16:cc * 16 + PAD], lhsT=wcv[:, kc, cc * 128:(cc + 1) * 128],
                                                      rhs=xnb[:, kc, c0:c0 + PAD], start=(kc == 0), stop=(kc == 7)),
                             reads=['wcv', 'xnb'], writes=['ps6'], signal=(kc == 7))
                h0 = 0 if tb == 0 else PAD
                k.op('act', lambda e: e.activation(out=sg[:, 0:32], in_=ps[6][:, 32:64], func=AF.Sigmoid), reads=['ps6'], writes=['sg'])
                for cc in range(2):
                    k.op('dve', lambda e: e.tensor_tensor(out=halo[:, cc, h0:h0 + PAD], in0=ps[6][:, cc * 16:cc * 16 + PAD],
                                                          in1=sg[:, cc * 16:cc * 16 + PAD], op=ALU.mult),
                         reads=['ps6', 'sg'], writes=['halo'])
            def tile_gen(tt):
                ti = tb * 4 + tt
                par = ti % 2
                G = Gs[par]
                T = Ts[par]
                krp = kr if par == 0 else kr2
                krk = 'kr%d' % par
                pkv, kkv = (ps[0], 'ps0') if par == 0 else (ps[6], 'ps6')
                pmk, kmk = (ps[2], 'ps2') if par == 0 else (ps[1], 'ps1')
                tsl = slice(tt * 128, (tt + 1) * 128)
                for kc in range(8):
                    k.op('pe', lambda e: e.matmul(pkv[:, 0:256], lhsT=xnb[:, kc, tsl], rhs=wkv[:, kc, :], start=(kc == 0), stop=(kc == 7)),
                               reads=['xnb', 'wkv'], writes=[kkv], signal=(kc == 7))
                yield
                yield k.op('act', lambda e: e.activation(out=vsb[:, ti, :], in_=pkv[:, 128:256], func=AF.Copy), reads=[kkv], writes=['vsb'])
                yield from qk_norm_rope(k, C, T, pkv[:, 0:128], kkv, 2, C.rp[:, l, 64:128], 1.0, ti, krp[:], krk, 'p1' + 'ab'[par])
                yield k.op('pe', lambda e: e.transpose(out=pkv[:, 256:384], in_=krp[:], identity=C.ident_f[:]), reads=[krk], writes=[kkv])
                yield k.op('act', lambda e: e.activation(out=kTl[:, ti * 128:(ti + 1) * 128], in_=pkv[:, 256:384], func=AF.Copy),
                           reads=[kkv], writes=['kTl'])
                for kc in range(8):
                    k.op('pe', lambda e: e.matmul(pmk[:], lhsT=xnb[:, kc, tsl], rhs=wmk[:, kc, 0:512], start=(kc == 0), stop=(kc == 7)),
                         reads=['xnb', 'wmk'], writes=[kmk], signal=(kc == 7))
                yield
                for kc in range(8):
                    k.op('pe', lambda e: e.matmul(pkv[:, 384:400], lhsT=xnb[:, kc, tsl], rhs=wmk[:, kc, 512:528], start=(kc == 0), stop=(kc == 7)),
                         reads=['xnb', 'wmk'], writes=[kkv], signal=(kc == 7))
                yield
                yield from mlstm_gates(k, C, G, pkv[:, 384:400], kkv, pkv[:, 416:432], kkv)
                lb = (4, 5) if par == 0 else (3, 7)
                yield from mlstm_local(k, C, G, pmk, kmk, [ps[lb[0]], ps[lb[1]]], ['ps%d' % lb[0], 'ps%d' % lb[1]], mv_aug[:, par], ('mv', par))
                for d in range(2):
                    yield k.op('act', lambda e: e.activation(out=G.loc[:, d], in_=ps[lb[d]][0:64, 0:320].rearrange("p (h e) -> p h e", e=80)[:, :, 0:65], func=AF.Copy),
                               reads=['ps%d' % lb[d]], writes=[(G.pfx + '.loc', d)])

            def agg_update(tt):
                par = (tb * 4 + tt) % 2
                G = Gs[par]
                edf = G.edec[0:64, 0:4]
                edb = G.edec[0:64, 4:8]
                k.op('dve', lambda e: e.tensor_tensor(out=agg[:, 0:4, 0:65], in0=agg[:, 0:4, 0:65],
                                                      in1=edf.unsqueeze(2).to_broadcast([64, 4, 65]), op=ALU.mult),
                     reads=['agg', G.pfx + '.edec'], writes=['agg'])
                k.op('dve', lambda e: e.tensor_tensor(out=agg[:, 0:4, 0:65], in0=agg[:, 0:4, 0:65], in1=G.loc[:, 0], op=ALU.add),
                     reads=['agg', (G.pfx + '.loc', 0)], writes=['agg'])
                k.op('dve', lambda e: e.tensor_tensor(out=agg[:, 0:4, 65:66], in0=agg[:, 0:4, 65:66], in1=edf.unsqueeze(2), op=ALU.mult),
                     reads=['agg', G.pfx + '.edec'], writes=['agg'])
                k.op('dve', lambda e: e.tensor_tensor(out=tmpS[:], in0=G.loc[:, 1], in1=agg[:, 4:8, 65:66].to_broadcast([64, 4, 65]), op=ALU.mult),
                     reads=['agg', (G.pfx + '.loc', 1)], writes=['tmpS'])
                k.op('dve', lambda e: e.tensor_tensor(out=agg[:, 4:8, 0:65], in0=agg[:, 4:8, 0:65], in1=tmpS[:], op=ALU.add),
                     reads=['agg', 'tmpS'], writes=['agg'])
                k.op('dve', lambda e: e.tensor_tensor(out=agg[:, 4:8, 65:66], in0=agg[:, 4:8, 65:66], in1=edb.unsqueeze(2), op=ALU.mult),
                     reads=['agg', G.pfx + '.edec'], writes=['agg'])

            for t2 in (0, 2):
                interleave([tile_gen(t2), tile_gen(t2 + 1)])
                agg_update(t2)
                agg_update(t2 + 1)
        if 'xw' in SKIP:
            k.barrier()
            return
        k.dma('sp', X.kx, kTl[:], reads=['kTl'], writes=['X.kx'])
        k.dma('sp', X.vx.rearrange("(t p) c -> p t c", p=128), vsb[:], reads=['vsb'], writes=['X.vx'])
        k.dma('sp', X.hx, halo[:], reads=['halo'], writes=['X.hx'])
        k.dma('sp', X.ax, agg[:], reads=['agg'], writes=['X.ax'])
        k.barrier()


def p2_conv(k, nc, C, xT, W, l, XA, yT_cm):
    ps = C.ps
    with ExitStack() as es:
        xnb = sbuf(nc, es, "cv_xnb", [128, 8, 512], BF16)
        wcv = sbuf(nc, es, "cv_wcv", [128, 8, 512], BF16)
        uT = sbuf(nc, es, "cv_uT", [128, 2, NT + 2 * PAD], F32)
        acc = sbuf(nc, es, "cv_acc", [128, 2, NT], F32)
        hall = sbuf(nc, es, "cv_hall", [128, 4, 2, 30], F32)
        sg = sbuf(nc, es, "cv_sg", [128, 2, 512], F32)
        sq = sbuf(nc, es, "cv_sq", [128, 512], F32)
        mean = sbuf(nc, es, "cv_mean", [128, 512], F32)
        msq = sbuf(nc, es, "cv_msq", [128, 512], F32)
        rs = sbuf(nc, es, "cv_rs", [128, 512], F32)
        tt_ = sbuf(nc, es, "cv_t", [128, 512], F32)
        cp = C.pp[:, l, 44:112].rearrange("p (c j) -> p c j", j=34)
        k.dma('pool', wcv[:], W.w_conv[l], writes=['wcv'])
        for i in range(4):
            k.dma('sp', hall[:, i], XA.hx_all[i], writes=['hall'])
        k.op('dve', lambda e: e.memset(uT[:, :, 0:PAD], 0.0), writes=['uT_h'])
        k.op('dve', lambda e: e.memset(uT[:, :, NT + PAD:NT + 2 * PAD], 0.0), writes=['uT_h'])
        for i in range(4):
            k.op('dve', lambda e: e.scalar_tensor_tensor(out=uT[:, :, 0:PAD], in0=hall[:, i, :, PAD:2 * PAD], scalar=C.sel[:, 8 + i:9 + i],
                                                         in1=uT[:, :, 0:PAD], op0=ALU.mult, op1=ALU.add),
                 reads=['hall', 'uT_h'], writes=['uT_h'])
            k.op('dve', lambda e: e.scalar_tensor_tensor(out=uT[:, :, NT + PAD:NT + 2 * PAD], in0=hall[:, i, :, 0:PAD], scalar=C.sel[:, 12 + i:13 + i],
                                                         in1=uT[:, :, NT + PAD:NT + 2 * PAD], op0=ALU.mult, op1=ALU.add),
                 reads=['hall', 'uT_h'], writes=['uT_h'])
        gcol = C.pp[:, l, 8:16]
        for tb in range(4):
            t0 = tb * 512
            rms_block(k, C, xT, t0, gcol, lambda kc: xnb[:, kc, :], 'xnb', xkeys_of(t0))
            for cc in range(4):
                for kc in range(8):
                    k.op('pe', lambda e: e.matmul(ps[cc][:], lhsT=wcv[:, kc, cc * 128:(cc + 1) * 128], rhs=xnb[:, kc, :],
                                                  start=(kc == 0), stop=(kc == 7)),
                         reads=['wcv', 'xnb'], writes=['ps%d' % cc], signal=(kc == 7))
            for cc in range(2):
                k.op('act', lambda e: e.activation(out=sg[:, cc, :], in_=ps[2 + cc][:], func=AF.Sigmoid), reads=['ps%d' % (2 + cc)], writes=[('sg', cc)])
                k.op('dve', lambda e: e.tensor_tensor(out=uT[:, cc, PAD + t0:PAD + t0 + 512], in0=ps[cc][:], in1=sg[:, cc, :], op=ALU.mult),
                     reads=['ps%d' % cc, ('sg', cc)], writes=[('uT', tb)])
        ukeys = [('uT', i) for i in range(4)] + ['uT_h']
        for cc in range(2):
            k.op('dve', lambda e: e.tensor_scalar(out=acc[:, cc, :], in0=uT[:, cc, 0:NT], scalar1=cp[:, cc, 0:1], scalar2=None, op0=ALU.mult),
                 reads=ukeys, writes=[('acc', cc)])
            for j in range(1, CONV_W):
                k.op('dve', lambda e: e.scalar_tensor_tensor(out=acc[:, cc, :], in0=uT[:, cc, j:j + NT], scalar=cp[:, cc, j:j + 1],
                                                             in1=acc[:, cc, :], op0=ALU.mult, op1=ALU.add),
                     reads=ukeys + [('acc', cc)], writes=[('acc', cc)])
            k.op('act', lambda e: e.activation(out=acc[:, cc, :], in_=acc[:, cc, :], func=AF.Identity, bias=cp[:, cc, 31:32]),
                 reads=[('acc', cc)], writes=[('acc', cc)])
        for tb in range(4):
            t0 = tb * 512
            for cc in range(2):
                k.op('pe', lambda e: e.matmul(ps[4][:], lhsT=C.ones_f[:], rhs=acc[:, cc, t0:t0 + 512], start=(cc == 0), stop=(cc == 1)),
                     reads=[('acc', cc)], writes=['ps4'], signal=(cc == 1))
            for cc in range(2):
                k.op('act', lambda e: e.activation(out=sq[:], in_=acc[:, cc, t0:t0 + 512], func=AF.Square), reads=[('acc', cc)], writes=['cvsq'])
                k.op('pe', lambda e: e.matmul(ps[5][:], lhsT=C.ones_f[:], rhs=sq[:], start=(cc == 0), stop=(cc == 1)),
                     reads=['cvsq'], writes=['ps5'])
            k.op('dve', lambda e: e.tensor_scalar(out=mean[:], in0=ps[4][:], scalar1=1.0 / 256, scalar2=None, op0=ALU.mult), reads=['ps4'], writes=['mean'])
            k.op('pool', lambda e: e.tensor_tensor(out=msq[:], in0=mean[:], in1=mean[:], op=ALU.mult), reads=['mean'], writes=['msq'])
            k.op('dve', lambda e: e.scalar_tensor_tensor(out=rs[:], in0=ps[5][:], scalar=1.0 / 256, in1=msq[:], op0=ALU.mult, op1=ALU.subtract),
                 reads=['ps5', 'msq'], writes=['rs'])
            k.op('act', lambda e: e.activation(out=rs[:], in_=rs[:], func=AF.Sqrt, bias=C.eps_col[:, 0:1]), reads=['rs'], writes=['rs'])
            k.op('dve', lambda e: e.reciprocal(out=rs[:], in_=rs[:]), reads=['rs'], writes=['rs'])
            for cc in range(2):
                k.op('dve', lambda e: e.tensor_tensor(out=tt_[:], in0=acc[:, cc, t0:t0 + 512], in1=mean[:], op=ALU.subtract),
                     reads=[('acc', cc), 'mean'], writes=['cvt'])
                k.op('dve', lambda e: e.tensor_tensor(out=tt_[:], in0=tt_[:], in1=rs[:], op=ALU.mult), reads=['cvt', 'rs'], writes=['cvt'])
                k.op('act', lambda e: e.activation(out=yT_cm[:, cc, t0:t0 + 512], in_=tt_[:], func=AF.Silu, scale=cp[:, cc, 32:33], bias=cp[:, cc, 33:34]),
                     reads=['cvt'], writes=[('ycm', cc, tb)])
        k.barrier()


def p2_mlstm(k, nc, C, xT, W, l, XA, yT_cm):
    ps = C.ps
    with ExitStack() as es:
        xnb = sbuf(nc, es, "ml_xnb", [128, 8, 512], BF16)
        wm = sbuf(nc, es, "ml_wm", [128, 8, 1040], BF16)
        mqT = sbuf(nc, es, "ml_mqT", [128, 2, NT], BF16)
        mkT = sbuf(nc, es, "ml_mkT", [128, 2, NT], BF16)
        mv_aug = sbuf(nc, es, "ml_mvaug", [128, NTILE, 4, 72], BF16)
        sgo = sbuf(nc, es, "ml_sgo", [128, NTILE, 256], BF16)
        CTf = sbuf(nc, es, "ml_CTf", [128, NTILE, 4, 72], BF16)
        CTb = sbuf(nc, es, "ml_CTb", [128, NTILE, 4, 72], BF16)
        locb = sbuf(nc, es, "ml_locb", [64, NTILE, 4, 65], F32)
        edb_all = sbuf(nc, es, "ml_edb", [64, NTILE, 4], F32)
        a_all = sbuf(nc, es, "ml_a", [128, NTILE, 8], F32)
        eb_all = sbuf(nc, es, "ml_eb", [128, NTILE, 8], F32)
        dg = sbuf(nc, es, "ml_dg", [128, 4, 128], F32)
        b_all = sbuf(nc, es, "ml_ball", [128, NTILE, 8], F32)
        S = sbuf(nc, es, "ml_S", [64, 8, 65], F32)
        al = sbuf(nc, es, "ml_al", [64, 8], F32)
        DT = sbuf(nc, es, "ml_DT", [128, 4, 128], F32)
        SwT = sbuf(nc, es, "ml_SwT", [128, 4, 128], BF16)
        tI = sbuf(nc, es, "ml_tI", [128, 4, 65], F32)
        hn = sbuf(nc, es, "ml_hn", [128, 4, 65], F32)
        den = sbuf(nc, es, "ml_den", [128, 4], F32)
        hs = sbuf(nc, es, "ml_hs", [128, 256], F32)
        hq = sbuf(nc, es, "ml_hq", [128, 256], F32)
        hss = sbuf(nc, es, "ml_hss", [128, 4], F32)
        Gs = [alloc_gates(nc, es, "ml_Ga"), alloc_gates(nc, es, "ml_Gb")]

        k.dma('pool', wm[:], W.w_m[l], writes=['wm'])
        aggs = locb[:, 0:9].rearrange("p a h e -> p (a h e)")[:, 0:4 * 8 * 66].rearrange("p (i j e) -> p i j e", i=4, j=8)
        for i in range(4):
            k.dma('sp', aggs[:, i], XA.ax_all[i], writes=['aggs'])
        k.op('dve', lambda e: e.memset(mv_aug[:], 1.0), writes=['mv_all'])
        k.op('dve', lambda e: e.memset(S[:], 0.0), writes=['S'])
        for d in range(2):
            order = range(4) if d == 0 else range(3, -1, -1)
            for i in order:
                selc = C.sel[0:64, d * 4 + i:d * 4 + i + 1]
                k.op('dve', lambda e: e.tensor_scalar(out=al[:, 0:4], in0=aggs[:, i, d * 4:(d + 1) * 4, 65], scalar1=-1.0, scalar2=selc,
                                                      op0=ALU.add, op1=ALU.mult), reads=['aggs'], writes=['al'])
                k.op('dve', lambda e: e.tensor_scalar(out=al[:, 0:4], in0=al[:, 0:4], scalar1=1.0, scalar2=None, op0=ALU.add), reads=['al'], writes=['al'])
                k.op('dve', lambda e: e.tensor_tensor(out=S[:, d * 4:(d + 1) * 4, :], in0=S[:, d * 4:(d + 1) * 4, :],
                                                      in1=al[:, 0:4].unsqueeze(2).to_broadcast([64, 4, 65]), op=ALU.mult),
                     reads=['S', 'al'], writes=['S'])
                k.op('dve', lambda e: e.scalar_tensor_tensor(out=S[:, d * 4:(d + 1) * 4, :], in0=aggs[:, i, d * 4:(d + 1) * 4, 0:65], scalar=selc,
                                                             in1=S[:, d * 4:(d + 1) * 4, :], op0=ALU.mult, op1=ALU.add),
                     reads=['S', 'aggs'], writes=['S'])
        k.barrier()
        gcol = C.pp[:, l, 8:16]
        for tb in range(4):
            t0 = tb * 512
            rms_block(k, C, xT, t0, gcol, lambda kc: xnb[:, kc, :], 'xnb', xkeys_of(t0))
            for cc in range(4):
                for kc in range(8):
                    k.op('pe', lambda e: e.matmul(ps[cc][:], lhsT=wm[:, kc, cc * 128:(cc + 1) * 128], rhs=xnb[:, kc, :], start=(kc == 0), stop=(kc == 7)),
                         reads=['wm', 'xnb'], writes=['ps%d' % cc], signal=(kc == 7))
            for cc in range(2):
                k.op('act', lambda e: e.activation(out=mqT[:, cc, t0:t0 + 512], in_=ps[cc][:], func=AF.Copy), reads=['ps%d' % cc], writes=[('mqT', tb)])
                k.op('dve', lambda e: e.tensor_scalar(out=mkT[:, cc, t0:t0 + 512], in0=ps[2 + cc][:], scalar1=0.125, scalar2=None, op0=ALU.mult),
                     reads=['ps%d' % (2 + cc)], writes=[('mkT', tb)])
            def banks(ti):
                return (4, 5, 6, 7) if ti % 2 == 0 else (0, 1, 2, 3)

            def tileA_gen(tt):
                ti = tb * 4 + tt
                G = Gs[ti % 2]
                bkv, bg, blf, blb = banks(ti)
                tsl = slice(tt * 128, (tt + 1) * 128)
                for kc in range(8):
                    k.op('pe', lambda e: e.matmul(ps[bkv][:], lhsT=xnb[:, kc, tsl], rhs=wm[:, kc, 256:768], start=(kc == 0), stop=(kc == 7)),
                               reads=['xnb', 'wm'], writes=['ps%d' % bkv], signal=(kc == 7))
                yield
                for kc in range(8):
                    k.op('pe', lambda e: e.matmul(ps[bg][:, 0:272], lhsT=xnb[:, kc, tsl], rhs=wm[:, kc, 768:1040], start=(kc == 0), stop=(kc == 7)),
                               reads=['xnb', 'wm'], writes=['ps%d' % bg], signal=(kc == 7))
                yield
                yield k.op('act', lambda e: e.activation(out=sgo[:, ti, :], in_=ps[bg][:, 0:256], func=AF.Sigmoid), reads=['ps%d' % bg], writes=[('sgo', ti)])
                yield from mlstm_gates(k, C, G, ps[bg][:, 256:272], 'ps%d' % bg, ps[bg][:, 288:304], 'ps%d' % bg)
                yield k.op('pool', lambda e: e.tensor_copy(out=a_all[:, ti, :], in_=G.a[:]), reads=[G.pfx + '.a'], writes=[('a_all', ti)])
                yield k.op('pool', lambda e: e.tensor_copy(out=eb_all[:, ti, :], in_=G.eb[:]), reads=[G.pfx + '.eb'], writes=[('eb_all', ti)])
                yield k.op('pool', lambda e: e.tensor_copy(out=b_all[:, ti, :], in_=G.pbs[:, 0:8]), reads=[G.pfx + '.pbs'], writes=[('b_all', ti)])
                yield from mlstm_local(k, C, G, ps[bkv], 'ps%d' % bkv, [ps[blf], ps[blb]], ['ps%d' % blf, 'ps%d' % blb], mv_aug[:, ti], ('mv', ti))
                yield k.op('act', lambda e: e.activation(out=locb[:, ti], in_=ps[blb][0:64, 0:320].rearrange("p (h e) -> p h e", e=80)[:, :, 0:65], func=AF.Copy),
                           reads=['ps%d' % blb], writes=[('locb', ti)])
                yield k.op('pool', lambda e: e.tensor_copy(out=edb_all[:, ti, :], in_=G.edec[0:64, 4:8]), reads=[G.pfx + '.edec'], writes=[('edb', ti)])

            def fwd_advance(tt):
                ti = tb * 4 + tt
                G = Gs[ti % 2]
                blf = banks(ti)[2]
                k.op('act', lambda e: e.activation(out=CTf[0:64, ti, 0:4:2, 0:65], in_=S[:, 0:4:2, :], func=AF.Copy), reads=['S'], writes=[('CTf', ti)])
                k.op('act', lambda e: e.activation(out=CTf[64:128, ti, 1:4:2, 0:65], in_=S[:, 1:4:2, :], func=AF.Copy), reads=['S'], writes=[('CTf', ti)])
                k.op('dve', lambda e: e.tensor_tensor(out=S[:, 0:4, :], in0=S[:, 0:4, :],
                                                      in1=G.edec[0:64, 0:4].unsqueeze(2).to_broadcast([64, 4, 65]), op=ALU.mult),
                     reads=['S', G.pfx + '.edec'], writes=['S'])
                k.op('dve', lambda e: e.tensor_tensor(out=S[:, 0:4, :], in0=S[:, 0:4, :],
                                                      in1=ps[blf][0:64, 0:320].rearrange("p (h e) -> p h e", e=80)[:, :, 0:65], op=ALU.add),
                     reads=['S', 'ps%d' % blf], writes=['S'])

            for t2 in (0, 2):
                if PASSA_INTERLEAVE:
                    interleave([tileA_gen(t2), tileA_gen(t2 + 1)])
                    fwd_advance(t2)
                    fwd_advance(t2 + 1)
                else:
                    drain(tileA_gen(t2))
                    fwd_advance(t2)
                    drain(tileA_gen(t2 + 1))
                    fwd_advance(t2 + 1)
        for ti in range(NTILE - 1, -1, -1):
            k.op('act', lambda e: e.activation(out=CTb[0:64, ti, 0:4:2, 0:65], in_=S[:, 4:8:2, :], func=AF.Copy), reads=['S'], writes=[('CTb', ti)])
            k.op('act', lambda e: e.activation(out=CTb[64:128, ti, 1:4:2, 0:65], in_=S[:, 5:8:2, :], func=AF.Copy), reads=['S'], writes=[('CTb', ti)])
            if ti > 0:
                k.op('dve', lambda e: e.tensor_tensor(out=S[:, 4:8, :], in0=S[:, 4:8, :],
                                                      in1=edb_all[:, ti, :].unsqueeze(2).to_broadcast([64, 4, 65]), op=ALU.mult),
                     reads=['S', ('edb', ti)], writes=['S'])
                k.op('dve', lambda e: e.tensor_tensor(out=S[:, 4:8, :], in0=S[:, 4:8, :], in1=locb[:, ti], op=ALU.add),
                     reads=['S', ('locb', ti)], writes=['S'])
        for ti in range(NTILE):
            tb = ti // 4
            csl = slice(ti * 128, (ti + 1) * 128)
            for hg in range(2):
                U = []
                for h in (2 * hg, 2 * hg + 1):
                    hp = (h % 2) * 64
                    hc = h // 2
                    pA = ps[h % 2]
                    k.op('pe', lambda e: e.matmul(pA[:, 0:128], lhsT=mkT[hp:hp + 64, hc, csl], rhs=mqT[hp:hp + 64, hc, csl], start=True, stop=True),
                         reads=[('mkT', tb), ('mqT', tb)], writes=['ps%d' % (h % 2)])
                    for d in range(2):
                        sl = (h % 2) * 2 + d
                        bank = 2 + sl
                        U.append((h, hp, hc, pA, d, d * 4 + h, sl, 'ps%d' % bank, ps[bank][:, 0:128], ps[bank][:, 256:321], ps[bank][:, 384:449]))
                for h, hp, hc, pA, d, j, sl, bkey, pL, pN, pI in U:
                    k.op('pool', lambda e: e.tensor_scalar(out=dg[:, sl, :], in0=C.ident_f[:], scalar1=b_all[:, ti, j:j + 1], scalar2=None, op0=ALU.mult),
                         reads=[('b_all', ti)], writes=[('dg', sl)])
                for h, hp, hc, pA, d, j, sl, bkey, pL, pN, pI in U:
                    k.op('pe', lambda e: e.matmul(pL, lhsT=C.ones_f[:], rhs=dg[:, sl, :], start=True, stop=False),
                         reads=[('dg', sl)], writes=[bkey])
                    k.op('pe', lambda e: e.matmul(pL, lhsT=C.ident_bf[:], rhs=C.maskneg[:, d, :], start=False, stop=True),
                         reads=[], writes=[bkey])
                for h, hp, hc, pA, d, j, sl, bkey, pL, pN, pI in U:
                    k.op('act', lambda e: e.activation(out=DT[:, sl, :], in_=pL, func=AF.Exp, bias=a_all[:, ti, j:j + 1]),
                         reads=[bkey, ('a_all', ti)], writes=[('DT', sl)])
                for h, hp, hc, pA, d, j, sl, bkey, pL, pN, pI in U:
                    k.op('dve', lambda e: e.tensor_tensor(out=SwT[:, sl, :], in0=DT[:, sl, :], in1=pA[:, 0:128], op=ALU.mult),
                         reads=[('DT', sl), 'ps%d' % (h % 2)], writes=[('SwT', sl)])
                for h, hp, hc, pA, d, j, sl, bkey, pL, pN, pI in U:
                    CT = CTf if d == 0 else CTb
                    ckey = ('CTf', ti) if d == 0 else ('CTb', ti)
                    k.op('pe', lambda e: e.matmul(pN, lhsT=SwT[:, sl, :], rhs=mv_aug[:, ti, h, 0:65], start=True, stop=True),
                         reads=[('SwT', sl), ('mv', ti), 'mv_all'], writes=[bkey], signal=False)
                    k.op('pe', lambda e: e.matmul(pI, lhsT=mqT[hp:hp + 64, hc, csl], rhs=CT[hp:hp + 64, ti, h, 0:65], start=True, stop=True),
                         reads=[('mqT', tb), ckey], writes=[bkey])
                for h, hp, hc, pA, d, j, sl, bkey, pL, pN, pI in U:
                    k.op('act', lambda e: e.activation(out=tI[:, sl, :], in_=pI, func=AF.Copy, scale=eb_all[:, ti, j:j + 1]),
                         reads=[bkey, ('eb_all', ti)], writes=[('tI', sl)])
                for h, hp, hc, pA, d, j, sl, bkey, pL, pN, pI in U:
                    k.op('dve', lambda e: e.tensor_tensor(out=hn[:, sl, :], in0=tI[:, sl, :], in1=pN, op=ALU.add),
                         reads=[('tI', sl), bkey], writes=[('hn', sl)])
                hk = [('hn', i) for i in range(4)]
                k.op('dve', lambda e: e.tensor_scalar(out=den[:], in0=hn[:, :, 64], scalar1=-1.0, scalar2=None, op0=ALU.mult), reads=hk, writes=['den'])
                k.op('dve', lambda e: e.tensor_tensor(out=den[:], in0=den[:], in1=hn[:, :, 64], op=ALU.max), reads=hk + ['den'], writes=['den'])
                k.op('dve', lambda e: e.tensor_scalar(out=den[:], in0=den[:], scalar1=1.0, scalar2=None, op0=ALU.max), reads=['den'], writes=['den'])
                k.op('dve', lambda e: e.reciprocal(out=den[:], in_=den[:]), reads=['den'], writes=['den'])
                for h in (2 * hg, 2 * hg + 1):
                    s0 = (h % 2) * 2
                    k.op('dve', lambda e: e.tensor_scalar(out=hs[:, h * 64:(h + 1) * 64], in0=hn[:, s0, 0:64], scalar1=den[:, s0:s0 + 1], scalar2=None, op0=ALU.mult),
                         reads=[('hn', s0), 'den'], writes=['hs'])
                    k.op('dve', lambda e: e.scalar_tensor_tensor(out=hs[:, h * 64:(h + 1) * 64], in0=hn[:, s0 + 1, 0:64], scalar=den[:, s0 + 1:s0 + 2],
                                                                 in1=hs[:, h * 64:(h + 1) * 64], op0=ALU.mult, op1=ALU.add),
                         reads=[('hn', s0 + 1), 'den', 'hs'], writes=['hs'])
            k.op('act', lambda e: e.activation(out=hq[:], in_=hs[:], func=AF.Square), reads=['hs'], writes=['hq'])
            k.op('dve', lambda e: e.tensor_reduce(out=hss[:], in_=hq[:].rearrange("p (h d) -> p h d", d=64), axis=AX.X, op=ALU.add),
                 reads=['hq'], writes=['hss'])
            k.op('act', lambda e: e.activation(out=hss[:], in_=hss[:], func=AF.Sqrt, scale=1.0 / 64, bias=C.eps_col[:, 0:1]), reads=['hss'], writes=['hss'])
            k.op('dve', lambda e: e.reciprocal(out=hss[:], in_=hss[:]), reads=['hss'], writes=['hss'])
            k.op('dve', lambda e: e.tensor_tensor(out=hq[:].rearrange("p (h d) -> p h d", d=64), in0=hs[:].rearrange("p (h d) -> p h d", d=64),
                                                  in1=hss[:].unsqueeze(2).to_broadcast([128, 4, 64]), op=ALU.mult),
                 reads=['hs', 'hss', 'hq'], writes=['hq'])
            k.op('pool', lambda e: e.tensor_tensor(out=hq[:], in0=hq[:], in1=C.rp[:, l, 144:400], op=ALU.mult), reads=['hq'], writes=['hq'])
            k.op('dve', lambda e: e.tensor_tensor(out=hq[:], in0=hq[:], in1=sgo[:, ti, :], op=ALU.mult), reads=['hq', ('sgo', ti)], writes=['hq'])
            for cc in range(2):
                k.op('pe', lambda e: e.transpose(out=ps[6 + cc][:, 0:128], in_=hq[:, cc * 128:(cc + 1) * 128], identity=C.ident_f[:]),
                     reads=['hq'], writes=['ps%d' % (6 + cc)])
                k.op('act', lambda e: e.activation(out=yT_cm[:, 2 + cc, csl], in_=ps[6 + cc][:, 0:128], func=AF.Copy),
                     reads=['ps%d' % (6 + cc)], writes=[('ycm', 2 + cc, tb)])
        k.barrier()


def p2_attn(k, nc, C, xT, W, l, XA, yT_at):
    ps = C.ps
    with ExitStack() as es:
        xnb = sbuf(nc, es, "at_xnb", [128, 8, 512], BF16)
        wq = sbuf(nc, es, "at_wq", [128, 8, 512], BF16)
        kTd = sbuf(nc, es, "at_kTd", [128, 2, SEQ], BF16)
        Vo = sbuf(nc, es, "at_Vo", [128, 64, 2, 128], BF16)
        qTb = sbuf(nc, es, "at_qTb", [128, 4, 512], BF16)
        PT = sbuf(nc, es, "at_PT", [128, 2, 2, 512], BF16)
        T = Ctx()
        T.sqv = sbuf(nc, es, "at_sqv", [128, 512], F32)
        T.ssq = sbuf(nc, es, "at_ssq", [128, 8], F32)
        T.t1 = sbuf(nc, es, "at_t1", [128, 256], F32)
        T.t2 = sbuf(nc, es, "at_t2", [128, 256], F32)
        T.kn = T.sqv
        qr = sbuf(nc, es, "at_qr", [128, 512], F32)
        rden = qr
        numB = C.rstd
        k.dma('pool', wq[:], W.w_q[l], writes=['wq'])
        k.op('dve', lambda e: e.memset(Vo[:, :, :, 64:128], 1.0), writes=['Vo1'])
        for i in range(4):
            for kvh in range(2):
                for hf in range(2):
                    k.dma('sp', kTd[hf * 64:(hf + 1) * 64, kvh, i * NT:(i + 1) * NT], XA.kx_all[i, kvh * 64:(kvh + 1) * 64, :], writes=['kTd'])
            for kvh in range(2):
                k.dma('sp', Vo[:, i * NTILE:(i + 1) * NTILE, kvh, 0:64],
                      XA.vx_all[i].rearrange("(t p) c -> p t c", p=128)[:, :, kvh * 64:(kvh + 1) * 64], writes=['Vo'])
        gcol = C.pp[:, l, 8:16]
        for qb in range(4):
            t0 = qb * 512
            rms_block(k, C, xT, t0, gcol, lambda kc: xnb[:, kc, :], 'xnb', xkeys_of(t0), bank=6)
            for tt in range(4):
                ti = qb * 4 + tt
                tsl = slice(tt * 128, (tt + 1) * 128)
                for kc in range(8):
                    k.op('pe', lambda e: e.matmul(ps[6][:], lhsT=xnb[:, kc, tsl], rhs=wq[:, kc, :], start=(kc == 0), stop=(kc == 7)),
                         reads=['xnb', 'wq'], writes=['ps6'], signal=(kc == 7))
                drain(qk_norm_rope(k, C, T, ps[6][:], 'ps6', 8, C.rp[:, l, 0:64], 0.125, ti, qr[:], 'qr', 'at'))
                for pr in range(4):
                    k.op('pe', lambda e: e.transpose(out=ps[7][:, pr * 128:(pr + 1) * 128], in_=qr[:, pr * 128:(pr + 1) * 128], identity=C.ident_f[:]),
                         reads=['qr'], writes=['ps7'])
                k.op('act', lambda e: e.activation(out=qTb[:, :, tsl], in_=ps[7][:].rearrange("p (a t) -> p a t", t=128), func=AF.Copy),
                     reads=['ps7'], writes=['qTb'])
            for pr in range(4):
                kvh = pr // 2
                ab = 4 + 2 * (pr % 2)

                def scores(kc):
                    sb_ = (kc % 2) * 2
                    ksl = slice(kc * 128, (kc + 1) * 128)
                    for hf in range(2):
                        k.op('pe', lambda e: e.matmul(ps[sb_ + hf][:], lhsT=kTd[hf * 64:(hf + 1) * 64, kvh, ksl], rhs=qTb[hf * 64:(hf + 1) * 64, pr, :],
                                                      start=True, stop=True),
                             reads=['kTd', 'qTb'], writes=['ps%d' % (sb_ + hf)])

                scores(0)
                for kc in range(64):
                    sb_ = (kc % 2) * 2
                    if EXP2:
                        k.op('act', lambda e: e.activation(out=PT[:, kc % 2].rearrange("p h n -> p (h n)"), in_=C.psd[kc % 2][:], func=AF.Exp),
                             reads=['ps%d' % sb_, 'ps%d' % (sb_ + 1)], writes=[('PT', kc % 2, 0), ('PT', kc % 2, 1)])
                    else:
                        for hf in range(2):
                            k.op('act', lambda e: e.activation(out=PT[:, kc % 2, hf, :], in_=ps[sb_ + hf][:], func=AF.Exp),
                                 reads=['ps%d' % (sb_ + hf)], writes=[('PT', kc % 2, hf)])
                    if kc + 1 < 64:
                        scores(kc + 1)
                    lhs = Vo[:, kc, kvh, :]
                    for hf in range(2):
                        k.op('pe', lambda e: e.matmul(ps[ab + hf][:], lhsT=lhs, rhs=PT[:, kc % 2, hf, :], start=(kc == 0), stop=(kc == 63)),
                             reads=['Vo', 'Vo1', ('PT', kc % 2, hf)], writes=['ps%d' % (ab + hf)], signal=True)
                pA_, pB_ = ps[ab], ps[ab + 1]
                kA, kB = 'ps%d' % ab, 'ps%d' % (ab + 1)
                k.op('dve', lambda e: e.reciprocal(out=rden[0:64, :], in_=pA_[64:128, :]), reads=[kA], writes=['qr'])
                k.op('dve', lambda e: e.tensor_tensor(out=yT_at[0:64, pr, t0:t0 + 512], in0=pA_[0:64, :], in1=rden[0:64, :], op=ALU.mult),
                     reads=[kA, 'qr'], writes=[('yat', qb)])
                k.op('dve', lambda e: e.reciprocal(out=rden[64:128, :], in_=pB_[64:128, :]), reads=[kB], writes=['qr'])
                k.op('act', lambda e: e.activation(out=numB[64:128, :], in_=pB_[0:64, :], func=AF.Copy), reads=[kB], writes=['rstd'])
                k.op('dve', lambda e: e.tensor_tensor(out=yT_at[64:128, pr, t0:t0 + 512], in0=numB[64:128, :], in1=rden[64:128, :], op=ALU.mult),
                     reads=['rstd', 'qr'], writes=[('yat', qb)])
        k.barrier()


def p2_wout(k, nc, C, xT, W, l, yT_at, yT_cm):
    ps = C.ps
    with ExitStack() as es:
        wo = sbuf(nc, es, "wo_w", [128, 8, DM], BF16)
        k.dma('pool', wo[:], W.w_out[l], writes=['wo'])
        for tb in range(4):
            t0 = tb * 512
            for dc in range(8):
                bank = dc % 4
                for kc in range(8):
                    rhs = yT_at[:, kc, t0:t0 + 512] if kc < 4 else yT_cm[:, kc - 4, t0:t0 + 512]
                    k.op('pe', lambda e: e.matmul(ps[bank][:], lhsT=wo[:, kc, dc * 128:(dc + 1) * 128], rhs=rhs, start=(kc == 0), stop=(kc == 7)),
                         reads=['wo'], writes=['ps%d' % bank], signal=(kc == 7))
                xk = ('xT', tb)
                k.op('dve', lambda e: e.tensor_tensor(out=xT[:, dc, t0:t0 + 512], in0=ps[bank][:], in1=xT[:, dc, t0:t0 + 512], op=ALU.add),
                     reads=['ps%d' % bank, xk], writes=[xk])
        k.barrier()


def p2_xattn(k, nc, C, xT, W, l, mem_in):
    ps = C.ps
    with ExitStack() as es:
        wq = sbuf(nc, es, "xa_wq", [128, 8, DM], BF16)
        wo = sbuf(nc, es, "xa_wo", [128, 8, DM], BF16)
        kxT = sbuf(nc, es, "xa_kxT", [128, 8, N_MEM], BF16)
        vx = sbuf(nc, es, "xa_vx", [128, 2, DM], BF16)
        k.dma('pool', wq[:], W.xwq[l], writes=['wq'])
        k.dma('pool', wo[:], W.xwo[l], writes=['wo'])
        with ExitStack() as esp:
            memT = sbuf(nc, esp, "xa_memT", [128, 8, N_MEM], F32)
            wkv = sbuf(nc, esp, "xa_wkv", [128, 2, 8, 512], BF16)
            mnT = sbuf(nc, esp, "xa_mnT", [128, 8, N_MEM], BF16)
            msq = sbuf(nc, esp, "xa_msq", [128, 2, N_MEM], BF16)
            kraw = sbuf(nc, esp, "xa_kraw", [128, 8, N_MEM], F32)
            ksq = sbuf(nc, esp, "xa_ksq", [128, 2, N_MEM], BF16)
            krs = sbuf(nc, esp, "xa_krs", [128, 2, N_MEM], F32)
            k.dma('sp', memT[:], mem_in, writes=['memT'])
            gm = C.pp[:, l, 24:32]
            for kc in range(8):
                j = kc % 2
                k.op('act', lambda e: e.activation(out=msq[:, j, :], in_=memT[:, kc, :], func=AF.Square), reads=['memT'], writes=[('msq', j)])
                k.op('pe', lambda e: e.matmul(ps[7][:, 0:N_MEM], lhsT=C.ones_bf[:], rhs=msq[:, j, :], start=(kc == 0), stop=(kc == 7)),
                     reads=[('msq', j)], writes=['ps7'])
            k.op('act', lambda e: e.activation(out=krs[:, 0, :], in_=ps[7][:, 0:N_MEM], func=AF.Sqrt, scale=1.0 / DM, bias=C.eps_col[:, 0:1]), reads=['ps7'], writes=[('krs', 0)])
            k.op('dve', lambda e: e.reciprocal(out=krs[:, 0, :], in_=krs[:, 0, :]), reads=[('krs', 0)], writes=[('krs', 0)])
            for kc in range(8):
                k.op('dve', lambda e: e.scalar_tensor_tensor(out=mnT[:, kc, :], in0=memT[:, kc, :], scalar=gm[:, kc:kc + 1], in1=krs[:, 0, :], op0=ALU.mult, op1=ALU.mult),
                     reads=['memT', ('krs', 0)], writes=['mnT'])
            for g4 in range(4):
                wb = g4 % 2
                k.dma('pool', wkv[:, wb], W.xwkv[l][:, :, g4 * 512:(g4 + 1) * 512], writes=[('xwkv', wb)])
                if g4 < 2:
                    for c4 in range(4):
                        cc = g4 * 4 + c4
                        for kc in range(8):
                            k.op('pe', lambda e: e.matmul(ps[c4][:, 0:N_MEM], lhsT=wkv[:, wb, kc, c4 * 128:(c4 + 1) * 128], rhs=mnT[:, kc, :],
                                                          start=(kc == 0), stop=(kc == 7)),
                                 reads=[('xwkv', wb), 'mnT'], writes=['ps%d' % c4], signal=(kc == 7))
                        k.op('act', lambda e: e.activation(out=kraw[:, cc, :], in_=ps[c4][:, 0:N_MEM], func=AF.Copy), reads=['ps%d' % c4], writes=[('kraw', cc)])
                else:
                    for mt in range(2):
                        for kc in range(8):
                            k.op('pe', lambda e: e.matmul(ps[4 + mt][:], lhsT=mnT[:, kc, mt * 128:(mt + 1) * 128], rhs=wkv[:, wb, kc, :],
                                                          start=(kc == 0), stop=(kc == 7)),
                                 reads=[('xwkv', wb), 'mnT'], writes=['ps%d' % (4 + mt)], signal=(kc == 7))
                        k.op('act', lambda e: e.activation(out=vx[:, mt, (g4 - 2) * 512:(g4 - 1) * 512], in_=ps[4 + mt][:], func=AF.Copy),
                             reads=['ps%d' % (4 + mt)], writes=['vx'])
            gk = C.pp[:, l, 42:44]
            for h in range(4):
                hb = h % 2
                for dcc in range(2):
                    cc = h * 2 + dcc
                    k.op('act', lambda e: e.activation(out=ksq[:, dcc, :], in_=kraw[:, cc, :], func=AF.Square), reads=[('kraw', cc)], writes=[('ksq', dcc)])
                    k.op('pe', lambda e: e.matmul(ps[6 + hb][:, 0:N_MEM], lhsT=C.ones_bf[:], rhs=ksq[:, dcc, :], start=(dcc == 0), stop=(dcc == 1)),
                         reads=[('ksq', dcc)], writes=['ps%d' % (6 + hb)])
                k.op('act', lambda e: e.activation(out=krs[:, hb, :], in_=ps[6 + hb][:, 0:N_MEM], func=AF.Sqrt, scale=1.0 / 256, bias=C.eps_col[:, 0:1]),
                     reads=['ps%d' % (6 + hb)], writes=[('krs', hb)])
                k.op('dve', lambda e: e.reciprocal(out=krs[:, hb, :], in_=krs[:, hb, :]), reads=[('krs', hb)], writes=[('krs', hb)])
                for dcc in range(2):
                    cc = h * 2 + dcc
                    k.op('dve', lambda e: e.scalar_tensor_tensor(out=kxT[:, cc, :], in0=kraw[:, cc, :], scalar=gk[:, dcc:dcc + 1], in1=krs[:, hb, :],
                                                                 op0=ALU.mult, op1=ALU.mult),
                         reads=[('kraw', cc), ('krs', hb)], writes=['kxT'])
            k.barrier()
        xnb = sbuf(nc, es, "xa_xnb", [128, 8, 512], BF16)
        qraw = sbuf(nc, es, "xa_qraw", [128, 8, 512], F32)
        qsq = sbuf(nc, es, "xa_qsq", [128, 8, 512], BF16)
        qrs = sbuf(nc, es, "xa_qrs", [128, 4, 512], F32)
        qT = sbuf(nc, es, "xa_qT", [128, 8, 512], BF16)
        PT = sbuf(nc, es, "xa_PT", [128, 4, 2, 512], BF16)
        rden = sbuf(nc, es, "xa_rden", [128, 4, 512], F32)
        oT = sbuf(nc, es, "xa_oT", [128, 8, 512], BF16)
        gx = C.pp[:, l, 16:24]
        gq = C.pp[:, l, 40:42]
        rr = [0]

        def nb():
            b = rr[0] % 7
            rr[0] += 1
            return b

        for tb in range(4):
            t0 = tb * 512
            rms_block(k, C, xT, t0, gx, lambda kc: xnb[:, kc, :], 'xnb', xkeys_of(t0), bank=7)
            for cc in range(8):
                b = nb()
                for kc in range(8):
                    k.op('pe', lambda e: e.matmul(ps[b][:], lhsT=wq[:, kc, cc * 128:(cc + 1) * 128], rhs=xnb[:, kc, :], start=(kc == 0), stop=(kc == 7)),
                         reads=['wq', 'xnb'], writes=['ps%d' % b], signal=(kc == 7))
                k.op('act', lambda e: e.activation(out=qraw[:, cc, :], in_=ps[b][:], func=AF.Copy), reads=['ps%d' % b], writes=[('qraw', cc)])
                k.op('pool', lambda e: e.tensor_tensor(out=qsq[:, cc, :], in0=qraw[:, cc, :], in1=qraw[:, cc, :], op=ALU.mult),
                     reads=[('qraw', cc)], writes=[('qsq', cc)])
            for h in range(4):
                b = nb()
                for dcc in range(2):
                    k.op('pe', lambda e: e.matmul(ps[b][:], lhsT=C.ones_bf[:], rhs=qsq[:, h * 2 + dcc, :], start=(dcc == 0), stop=(dcc == 1)),
                         reads=[('qsq', h * 2 + dcc)], writes=['ps%d' % b], signal=(dcc == 1))
                k.op('act', lambda e: e.activation(out=qrs[:, h, :], in_=ps[b][:], func=AF.Sqrt, scale=1.0 / 256, bias=C.eps_col[:, 0:1]),
                     reads=['ps%d' % b], writes=[('qrs', h)])
                k.op('dve', lambda e: e.reciprocal(out=qrs[:, h, :], in_=qrs[:, h, :]), reads=[('qrs', h)], writes=[('qrs', h)])
                k.op('pool', lambda e: e.tensor_scalar(out=qrs[:, h, :], in0=qrs[:, h, :], scalar1=1.0 / 16, scalar2=None, op0=ALU.mult),
                     reads=[('qrs', h)], writes=[('qrs', h)])
                for dcc in range(2):
                    cc = h * 2 + dcc
                    k.op('dve', lambda e: e.scalar_tensor_tensor(out=qT[:, cc, :], in0=qraw[:, cc, :], scalar=gq[:, dcc:dcc + 1], in1=qrs[:, h, :],
                                                                 op0=ALU.mult, op1=ALU.mult),
                         reads=[('qraw', cc), ('qrs', h)], writes=[('qT', cc)])
            for h in range(4):
                for mt in range(2):
                    b = nb()
                    for dcc in range(2):
                        cc = h * 2 + dcc
                        k.op('pe', lambda e: e.matmul(ps[b][:], lhsT=kxT[:, cc, mt * 128:(mt + 1) * 128], rhs=qT[:, cc, :], start=(dcc == 0), stop=(dcc == 1)),
                             reads=['kxT', ('qT', cc)], writes=['ps%d' % b], signal=(dcc == 1))
                    k.op('act', lambda e: e.activation(out=PT[:, h, mt, :], in_=ps[b][:], func=AF.Exp), reads=['ps%d' % b], writes=[('PT', h, mt)])
            for h in range(4):
                b = nb()
                for mt in range(2):
                    k.op('pe', lambda e: e.matmul(ps[b][:], lhsT=C.ones_bf[:], rhs=PT[:, h, mt, :], start=(mt == 0), stop=(mt == 1)),
                         reads=[('PT', h, mt)], writes=['ps%d' % b], signal=(mt == 1))
                k.op('dve', lambda e: e.reciprocal(out=rden[:, h, :], in_=ps[b][:]), reads=['ps%d' % b], writes=[('rden', h)])
            for cc in range(8):
                h = cc // 2
                b = nb()
                for mt in range(2):
                    k.op('pe', lambda e: e.matmul(ps[b][:], lhsT=vx[:, mt, cc * 128:(cc + 1) * 128], rhs=PT[:, h, mt, :], start=(mt == 0), stop=(mt == 1)),
                         reads=['vx', ('PT', h, mt)], writes=['ps%d' % b], signal=(mt == 1))
                k.op('dve', lambda e: e.tensor_tensor(out=oT[:, cc, :], in0=ps[b][:], in1=rden[:, h, :], op=ALU.mult),
                     reads=['ps%d' % b, ('rden', h)], writes=[('oT', cc)])
            for dc in range(8):
                b = nb()
                for kc in range(8):
                    k.op('pe', lambda e: e.matmul(ps[b][:], lhsT=wo[:, kc, dc * 128:(dc + 1) * 128], rhs=oT[:, kc, :], start=(kc == 0), stop=(kc == 7)),
                         reads=['wo', ('oT', kc)], writes=['ps%d' % b], signal=(kc == 7))
                xk = ('xT', tb)
                k.op('dve', lambda e: e.tensor_tensor(out=xT[:, dc, t0:t0 + 512], in0=ps[b][:], in1=xT[:, dc, t0:t0 + 512], op=ALU.add),
                     reads=['ps%d' % b, xk], writes=[xk])
        k.barrier()


WSPEC = [
    ("f1_w13", [NFC, 128, 2, 8, 128]), ("f1_w2", [8, 128, NFC, 128]),
    ("f2_w13", [NFC, 128, 2, 8, 128]), ("f2_w2", [8, 128, NFC, 128]),
    ("w_q", [128, 8, 512]), ("w_kv", [128, 8, 256]), ("w_conv", [128, 8, 512]), ("w_m", [128, 8, 1040]),
    ("w_out", [128, 8, DM]), ("xwq", [128, 8, DM]), ("xwkv", [128, 8, 2048]), ("xwo", [128, 8, DM]),
]
NPP = 112
NRP = 400


def build(prog, debug=()):
    nc = bass.Bass("TRN2", target_bir_lowering=False)

    def din(name, shape, dt=F32):
        return nc.dram_tensor(name, list(shape), dt, kind="ExternalInput").ap()

    def dout(name, shape, dt=F32):
        return nc.dram_tensor(name, list(shape), dt, kind="ExternalOutput").ap()

    class WL:
        def __init__(self, name, shp):
            self.name, self.shp, self.aps = name, shp, {}

        def __getitem__(self, l):
            if l not in self.aps:
                self.aps[l] = din("%s_%d" % (self.name, l), self.shp)
            return self.aps[l]

    W = Ctx()
    for name, shp in WSPEC:
        setattr(W, name, WL(name, shp))
    FUSED = (prog == 'F')
    if FUSED:
        x_in4 = din("xT_in", [4, 128, 8, NT])
        x_out4 = dout("xT_out", [4, 128, 8, NT])
        cos_in4 = din("cos", [4, 128, NTILE, 32])
        sin_in4 = din("sin", [4, 128, NTILE, 32])
        sel_in4 = din("sel", [4, 128, 16])
        xq = nc.dram_tensor("xq_scr", [4, 128, 8, NT], F32).ap()
        x_in = x_out = cos_in = sin_in = sel_in = None
    else:
        x_in = din("xT_in", [128, 8, NT])
        x_out = dout("xT_out", [128, 8, NT])
        cos_in = din("cos", [128, NTILE, 32])
        sin_in = din("sin", [128, NTILE, 32])
        sel_in = din("sel", [128, 16])
    mem_in = din("memT", [128, 8, N_MEM])
    pp_in = din("pp", [128, 2, NPP])
    rp_in = din("rp", [128, 2, NRP])
    cst_in = din("cst", [128, 5, 128])
    XO = Ctx()
    XA = Ctx()
    if prog in ('A', 'B'):
        XO.kx = dout("kx", [128, NT], BF16)
        XO.vx = dout("vx", [NT, 128], BF16)
        XO.hx = dout("hx", [128, 2, 30])
        XO.ax = dout("ax", [64, 8, 66])
    if FUSED:
        XA.kx_all = nc.dram_tensor("kx_scr", [4, 128, NT], BF16).ap()
        XA.vx_all = nc.dram_tensor("vx_scr", [4, NT, 128], BF16).ap()
        XA.hx_all = nc.dram_tensor("hx_scr", [4, 128, 2, 30], F32).ap()
        XA.ax_all = nc.dram_tensor("ax_scr", [4, 64, 8, 66], F32).ap()
    if prog in ('B', 'C'):
        XA.kx_all = din("kx_all", [4, 128, NT], BF16)
        XA.vx_all = din("vx_all", [4, NT, 128], BF16)
        XA.hx_all = din("hx_all", [4, 128, 2, 30])
        XA.ax_all = din("ax_all", [4, 64, 8, 66])

    with ExitStack() as es:
        k = K(nc, es)
        C = Ctx()
        C.psd = [es.enter_context(nc.psum_tensor("psd%d" % i, [128, 1024], F32)) for i in range(4)]
        C.ps = [C.psd[i // 2][:, (i % 2) * 512:(i % 2 + 1) * 512] for i in range(8)]
        xT = sbuf(nc, es, "xT", [128, 8, NT], F32)
        C.pp = sbuf(nc, es, "pp_s", [128, 2, NPP], F32)
        C.rp = sbuf(nc, es, "rp_s", [128, 2, NRP], F32)
        C.cos = sbuf(nc, es, "cos_s", [128, NTILE, 32], F32)
        C.sin = sbuf(nc, es, "sin_s", [128, NTILE, 32], F32)
        C.sel = sbuf(nc, es, "sel_s", [128, 16], F32)
        cst = sbuf(nc, es, "cst_s", [128, 3, 128], F32)
        C.ident_f = cst[:, 0, :]
        C.tri_f = cst[:, 1, :]
        C.tri_b = cst[:, 2, :]
        C.ones_f = sbuf(nc, es, "ones_f", [128, 128], F32)
        C.ones_bf = sbuf(nc, es, "ones_bf", [128, 128], BF16)
        C.ident_bf = sbuf(nc, es, "ident_bf", [128, 128], BF16)
        C.maskneg = sbuf(nc, es, "maskneg", [128, 2, 128], BF16)
        C.eps_col = sbuf(nc, es, "eps_col", [128, 1], F32)
        C.one_col = sbuf(nc, es, "one_col", [128, 1], F32)
        C.sq = sbuf(nc, es, "c_sq", [128, 2, 512], BF16)
        C.rstd = sbuf(nc, es, "c_rstd", [128, 512], F32)
        C.gateb_bc = None

        def load_x(src):
            for i in range(4):
                k.dma('sp', xT[:, :, i * 512:(i + 1) * 512], src[:, :, i * 512:(i + 1) * 512], writes=[('xT', i)])

        def store_x(dst):
            for i in range(4):
                k.dma('sp', dst[:, :, i * 512:(i + 1) * 512], xT[:, :, i * 512:(i + 1) * 512], reads=[('xT', i)])

        def load_quarter_consts(q):
            k.dma('sp', C.cos[:], cos_in4[q], writes=['c'])
            k.dma('sp', C.sin[:], sin_in4[q], writes=['c'])
            k.dma('sp', C.sel[:], sel_in4[q], writes=['c'])
            k.barrier()

        if not FUSED:
            load_x(x_in)
            k.dma('sp', C.cos[:], cos_in, writes=['c'])
            k.dma('sp', C.sin[:], sin_in, writes=['c'])
            k.dma('sp', C.sel[:], sel_in, writes=['c'])
        k.dma('sp', C.pp[:], pp_in, writes=['c'])
        k.dma('sp', C.rp[:], rp_in, writes=['c'])
        k.dma('sp', cst[:], cst_in[:, 0:3, :], writes=['c'])
        k.op('dve', lambda e: e.memset(C.ones_f[:], 1.0), writes=['c1'])
        k.op('dve', lambda e: e.memset(C.ones_bf[:], 1.0), writes=['c2'])
        k.op('dve', lambda e: e.memset(C.eps_col[:], EPS), writes=['c3'])
        k.op('dve', lambda e: e.memset(C.one_col[:], 1.0), writes=['c4'])
        k.op('dve', lambda e: e.tensor_copy(out=C.ident_bf[:], in_=cst[:, 0, :]), reads=['c'], writes=['c5'])
        with ExitStack() as es0:
            mtmp = sbuf(nc, es0, "mtmp", [128, 2, 128], F32)
            k.dma('sp', mtmp[:], cst_in[:, 3:5, :], writes=['mtmp'])
            k.op('dve', lambda e: e.tensor_copy(out=C.maskneg[:], in_=mtmp[:]), reads=['mtmp'], writes=['c6'])
            k.barrier()

        def P1(l):
            C.gateb_bc = C.rp[:, l, 128:144]
            if 'ffn' not in SKIP:
                ffn(k, nc, C, xT, C.pp[:, l, 0:8], W.f1_w13[l], W.f1_w2[l])
            if 'x1' in debug:
                k.dump("x1_%d" % l, xT[:], [128, 8, NT], F32, [('xT', i) for i in range(4)])
            if 'ffn' in SKIP:
                pass
            if 'mixer' not in SKIP:
                p1_mixer(k, nc, C, xT, W, l, XO)

        def P2(l):
            C.gateb_bc = C.rp[:, l, 128:144]
            with ExitStack() as es2:
                yT_cm = sbuf(nc, es2, "yT_cm", [128, 4, NT], BF16)
                if 'conv' not in SKIP:
                    p2_conv(k, nc, C, xT, W, l, XA, yT_cm)
                if 'mlstm' not in SKIP:
                    p2_mlstm(k, nc, C, xT, W, l, XA, yT_cm)
                with ExitStack() as es3:
                    yT_at = sbuf(nc, es3, "yT_at", [128, 4, NT], BF16)
                    if 'attn' not in SKIP:
                        p2_attn(k, nc, C, xT, W, l, XA, yT_at)
                    if 'y' in debug:
                        k.dump("yat_%d" % l, yT_at[:], [128, 4, NT], BF16, [])
                        k.dump("ycm_%d" % l, yT_cm[:], [128, 4, NT], BF16, [])
                    if 'wout' not in SKIP:
                        p2_wout(k, nc, C, xT, W, l, yT_at, yT_cm)
            if 'x2' in debug:
                k.dump("x2_%d" % l, xT[:], [128, 8, NT], F32, [('xT', i) for i in range(4)])
            if 'xattn' not in SKIP:
                p2_xattn(k, nc, C, xT, W, l, mem_in)
            if 'x3' in debug:
                k.dump("x3_%d" % l, xT[:], [128, 8, NT], F32, [('xT', i) for i in range(4)])
            if 'ffn2' not in SKIP:
                ffn(k, nc, C, xT, C.pp[:, l, 32:40], W.f2_w13[l], W.f2_w2[l])

        if prog == 'A':
            P1(0)
        elif prog == 'B':
            P2(0)
            P1(1)
        elif prog == 'C':
            P2(1)
        else:
            for l in range(2):
                for q in range(4):
                    load_quarter_consts(q)
                    load_x(x_in4[q] if l == 0 else xq[q])
                    XO.kx, XO.vx, XO.hx, XO.ax = XA.kx_all[q], XA.vx_all[q], XA.hx_all[q], XA.ax_all[q]
                    P1(l)
                    store_x(xq[q])
                    k.barrier()
                for q in range(4):
                    load_quarter_consts(q)
                    load_x(xq[q])
                    P2(l)
                    store_x(xq[q] if l == 0 else x_out4[q])
                    k.barrier()
        if not FUSED:
            store_x(x_out)
        k.barrier()
        stats = (k.nops, k.nwaits, list(k.dumps))
    used = []
    for name, _ in WSPEC:
        used += ['%s_%d' % (name, l) for l in getattr(W, name).aps]
    return nc, used, stats


def _perm64():
    return np.concatenate([np.arange(0, 64, 2), np.arange(1, 64, 2)])


def prep_weights(inp):
    f = lambda a: np.ascontiguousarray(a, dtype=np.float32)
    L = 2
    out = {}

    def kmaj(w):
        return f(w.reshape(L, 8, 128, w.shape[-1]).transpose(0, 2, 1, 3))

    for pfx, a, b in (("f1", "ffn1_w13", "ffn1_w2"), ("f2", "ffn2_w13", "ffn2_w2")):
        w13 = inp[a]
        out[pfx + "_w13"] = f(w13.reshape(L, 8, 128, 2, NFC, 128).transpose(0, 4, 2, 3, 1, 5))
        w2 = inp[b]
        out[pfx + "_w2"] = f(w2.reshape(L, NFC, 128, 8, 128).transpose(0, 3, 2, 1, 4))
    w_in = inp["w_in"]
    p64 = _perm64()
    qcols = np.concatenate([h * 64 + p64 for h in range(8)])
    kcols = 512 + np.concatenate([h * 64 + p64 for h in range(2)])
    out["w_q"] = kmaj(w_in[:, :, qcols])
    out["w_kv"] = kmaj(np.concatenate([w_in[:, :, kcols], w_in[:, :, 640:768]], axis=-1))
    out["w_conv"] = kmaj(w_in[:, :, 768:1280])
    out["w_m"] = kmaj(w_in[:, :, 1280:2320])
    out["w_out"] = kmaj(inp["w_out"])
    out["xwq"] = kmaj(inp["xattn_wq"])
    out["xwkv"] = kmaj(inp["xattn_wkv"])
    out["xwo"] = kmaj(inp["xattn_wo"])
    pp = np.zeros((128, L, NPP), np.float32)
    rp = np.zeros((128, L, NRP), np.float32)
    for l in range(L):
        col = lambda v: v.reshape(-1, 128).T
        pp[:, l, 0:8] = col(inp["ffn1_norm"][l])
        pp[:, l, 8:16] = col(inp["mix_norm"][l])
        pp[:, l, 16:24] = col(inp["xattn_norm"][l])
        pp[:, l, 24:32] = col(inp["mem_norm"][l])
        pp[:, l, 32:40] = col(inp["ffn2_norm"][l])
        pp[:, l, 40:42] = col(inp["xattn_q_norm"][l])
        pp[:, l, 42:44] = col(inp["xattn_k_norm"][l])
        cp = np.zeros((128, 2, 34), np.float32)
        cp[:, :, 0:31] = inp["conv_dw_w"][l].T.reshape(2, 128, 31).transpose(1, 0, 2)
        cp[:, :, 31] = col(inp["conv_dw_b"][l])
        cp[:, :, 32] = col(inp["conv_ln_g"][l])
        cp[:, :, 33] = col(inp["conv_ln_b"][l])
        pp[:, l, 44:112] = cp.reshape(128, 68)
        rp[:, l, 0:64] = inp["attn_q_norm"][l][p64][None, :]
        rp[:, l, 64:128] = inp["attn_k_norm"][l][p64][None, :]
        rp[:, l, 128:144] = inp["mlstm_gate_b"][l].reshape(16)[None, :]
        rp[:, l, 144:400] = inp["mlstm_out_norm"][l].reshape(256)[None, :]
    out["pp"] = pp
    out["rp"] = rp
    cst = np.zeros((128, 5, 128), np.float32)
    s = np.arange(128)[:, None]
    t = np.arange(128)[None, :]
    cst[:, 0] = (s == t)
    cst[:, 1] = (s <= t)
    cst[:, 2] = (s >= t)
    cst[:, 3] = np.where(s <= t, 0.0, NEG)
    cst[:, 4] = np.where(s >= t, 0.0, NEG)
    out["cst"] = cst
    return out


def prep_core(inp, c):
    b, p = c // 4, c % 4
    d = {}
    xs = inp["x"][b, p * NT:(p + 1) * NT]
    d["xT_in"] = np.ascontiguousarray(xs.reshape(NT, 8, 128).transpose(2, 1, 0), dtype=np.float32)
    d["memT"] = np.ascontiguousarray(inp["mem"][b].reshape(N_MEM, 8, 128).transpose(2, 1, 0), dtype=np.float32)
    pos = p * NT + np.arange(NT)
    row = (pos // GRID_W).astype(np.float32)
    colp = (pos % GRID_W).astype(np.float32)
    inv_freq = (np.float32(10000.0) ** (-np.arange(16, dtype=np.float32) / np.float32(16))).astype(np.float32)
    ang = np.concatenate([row[:, None] * inv_freq[None, :], colp[:, None] * inv_freq[None, :]], axis=-1).astype(np.float32)
    d["cos"] = np.ascontiguousarray(np.cos(ang).astype(np.float32).reshape(NTILE, 128, 32).transpose(1, 0, 2))
    d["sin"] = np.ascontiguousarray(np.sin(ang).astype(np.float32).reshape(NTILE, 128, 32).transpose(1, 0, 2))
    sel = np.zeros((128, 16), np.float32)
    for i in range(4):
        sel[:, i] = 1.0 if i < p else 0.0
        sel[:, 4 + i] = 1.0 if i > p else 0.0
        sel[:, 8 + i] = 1.0 if i == p - 1 else 0.0
        sel[:, 12 + i] = 1.0 if i == p + 1 else 0.0
    d["sel"] = sel
    return d


def run_prog(prog, inp_w, cores, extra, debug=()):
    nc, used, stats = build(prog, debug)
    in_maps = []
    for c in range(NCORES):
        m = {kk: inp_w[kk] for kk in ("pp", "rp", "cst")}
        for nm in used:
            base, l = nm.rsplit("_", 1)
            m[nm] = inp_w[base][int(l)]
        m.update(cores[c])
        m.update(extra[c])
        in_maps.append(m)
    res = run_bass_kernel_spmd(nc, in_maps, core_ids=list(range(NCORES)))
    return res.results


def gather_exchange(results):
    ex = []
    for c in range(NCORES):
        g0 = (c // 4) * 4
        ex.append({
            "kx_all": np.stack([results[g0 + i]["kx"] for i in range(4)]),
            "vx_all": np.stack([results[g0 + i]["vx"] for i in range(4)]),
            "hx_all": np.stack([results[g0 + i]["hx"] for i in range(4)]),
            "ax_all": np.stack([results[g0 + i]["ax"] for i in range(4)]),
        })
    return ex


def prep_core_fused(inp, b):
    qs = [prep_core(inp, b * 4 + p) for p in range(4)]
    d = {"memT": qs[0]["memT"]}
    for kk in ("xT_in", "cos", "sin", "sel"):
        d[kk] = np.stack([q[kk] for q in qs])
    return d


def kernel_unfused(**inputs):
    inp = {kk: np.asarray(v) for kk, v in inputs.items()}
    w = prep_weights(inp)
    cores = [prep_core(inp, c) for c in range(NCORES)]
    rA = run_prog('A', w, cores, [{} for _ in range(NCORES)])
    ex = gather_exchange(rA)
    for c in range(NCORES):
        ex[c]["xT_in"] = rA[c]["xT_out"]
    rB = run_prog('B', w, cores, ex)
    ex = gather_exchange(rB)
    for c in range(NCORES):
        ex[c]["xT_in"] = rB[c]["xT_out"]
    rC = run_prog('C', w, cores, ex)
    out = np.zeros((2, SEQ, DM), np.float32)
    for c in range(NCORES):
        b, p = c // 4, c % 4
        out[b, p * NT:(p + 1) * NT] = rC[c]["xT_out"].transpose(2, 1, 0).reshape(NT, DM)
    return out


def kernel_fused(**inputs):
    inp = {kk: np.asarray(v) for kk, v in inputs.items()}
    w = prep_weights(inp)
    pb = [prep_core_fused(inp, b) for b in range(2)]
    cores = [pb[c // 4] for c in range(NCORES)]
    r = run_prog('F', w, cores, [{} for _ in range(NCORES)])
    out = np.zeros((2, SEQ, DM), np.float32)
    for b in range(2):
        xo = r[b * 4]["xT_out"]
        for p in range(4):
            out[b, p * NT:(p + 1) * NT] = xo[p].transpose(2, 1, 0).reshape(NT, DM)
    return out


FUSED_DEFAULT = True


def kernel(**inputs):
    return kernel_fused(**inputs) if FUSED_DEFAULT else kernel_unfused(**inputs)
```

```python
import numpy as np
import ml_dtypes
from contextlib import ExitStack
import concourse.bass as bass
import concourse.mybir as mybir
from concourse.bass_utils import run_bass_kernel_spmd

F32 = mybir.dt.float32
BF16 = mybir.dt.bfloat16
AF = mybir.ActivationFunctionType
ALU = mybir.AluOpType
AX = mybir.AxisListType

NCORES = 8
NT = 2048
NTILE = 16
DM = 1024
DFF = 2816
NFC = 22
EPS = 1e-6
SEQ = 8192
GRID_W = 64
N_MEM = 256
CONV_W = 31
PAD = 15
NEG = -30000.0
SKIP = set()
PASSA_INTERLEAVE = True
CONV_PE = True
EXP2 = True


class Eng:
    def __init__(self, name, h, sem):
        self.name = name
        self.h = h
        self.sem = sem
        self.n = 0
        self.pending = False
        self.clock = {}


class K:
    NDS = 24

    def __init__(self, nc, es):
        self.nc = nc
        self.eng = {}
        for name, h in (("pe", nc.tensor), ("act", nc.scalar), ("dve", nc.vector),
                        ("pool", nc.gpsimd), ("sp", nc.sync)):
            sem = es.enter_context(nc.semaphore("s_" + name))
            self.eng[name] = Eng(name, h, sem)
        self.dsem = [es.enter_context(nc.semaphore("d%d" % i)) for i in range(self.NDS)]
        self.dcount = [0] * self.NDS
        self.dnext = 0
        self.dnext_p = 0
        self.last_w = {}
        self.readers = {}
        self.nwaits = 0
        self.nops = 0
        self.dumps = []

    def _merge(self, eng, clk):
        c = eng.clock
        for kk, v in clk.items():
            if c.get(kk, 0) < v:
                c[kk] = v

    def _wait(self, eng, t):
        if t[0] == 'E':
            _, en, n, clk = t
            if en == eng.name and en == 'pe':
                return
            if eng.clock.get(en, 0) >= n:
                return
            src = self.eng[en]
            assert src.n >= n, "waiting on unsignalled %s op (n=%d have %d)" % (en, n, src.n)
            eng.h.wait_ge(src.sem, n)
            self.nwaits += 1
            eng.clock[en] = n
            self._merge(eng, clk)
        else:
            _, si, val, clk = t
            key = ('d', si)
            if eng.clock.get(key, 0) >= val:
                return
            eng.h.wait_ge(self.dsem[si], val)
            self.nwaits += 1
            eng.clock[key] = val
            self._merge(eng, clk)

    def _deps(self, eng, reads, writes):
        for r in reads:
            t = self.last_w.get(r)
            if t is not None:
                self._wait(eng, t)
        for w in writes:
            t = self.last_w.get(w)
            if t is not None:
                self._wait(eng, t)
            rd = self.readers.get(w)
            if rd:
                for t in rd.values():
                    self._wait(eng, t)

    def _record(self, t, rkey, reads, writes):
        for r in reads:
            self.readers.setdefault(r, {})[rkey] = t
        for w in writes:
            self.last_w[w] = t
            self.readers[w] = {}

    @staticmethod
    def _is_ps(key):
        return isinstance(key, str) and key.startswith('ps') and key[2:].isdigit()

    def op(self, en, fn, reads=(), writes=(), signal=True):
        psr = [r for r in reads if self._is_ps(r)]
        if psr:
            reads = [r for r in reads if not self._is_ps(r)]
            writes = list(writes) + [r for r in psr if r not in writes]
        eng = self.eng[en]
        self._deps(eng, reads, writes)
        ins = fn(eng.h)
        self.nops += 1
        if signal:
            eng.n += 1
            ins.then_inc(eng.sem, 1)
            eng.pending = False
            t = ('E', en, eng.n, dict(eng.clock))
        else:
            eng.pending = True
            t = ('E', en, eng.n + 1, dict(eng.clock))
        self._record(t, en, reads, writes)
        return t

    def dma(self, qn, out, in_, reads=(), writes=()):
        eng = self.eng[qn]
        self._deps(eng, reads, writes)
        if qn == 'pool':
            si = self.dnext_p % 8
            self.dnext_p += 1
        else:
            si = 8 + self.dnext % (self.NDS - 8)
            self.dnext += 1
        cnt = self.dcount[si]
        if cnt > 0:
            self._wait(eng, ('D', si, 16 * cnt, {}))
        eng.h.dma_start(out=out, in_=in_).then_inc(self.dsem[si], 16)
        self.nops += 1
        self.dcount[si] = cnt + 1
        t = ('D', si, 16 * (cnt + 1), dict(eng.clock))
        self._record(t, ('d', si), reads, writes)
        return t

    def barrier(self):
        ts = []
        for e in self.eng.values():
            assert not e.pending, "pending unsignalled ops on " + e.name
            if e.n > 0:
                ts.append(('E', e.name, e.n, {}))
        for si in range(self.NDS):
            if self.dcount[si] > 0:
                ts.append(('D', si, 16 * self.dcount[si], {}))
        for e in self.eng.values():
            for t in ts:
                self._wait(e, t)
        self.last_w = {}
        self.readers = {}

    def dump(self, name, ap, shape, dtype, reads):
        d = self.nc.dram_tensor("dbg_" + name, list(shape), dtype, kind="ExternalOutput").ap()
        self.dma('sp', d, ap, reads=reads)
        self.dumps.append("dbg_" + name)


class Ctx:
    pass


def drain(g):
    for _ in g:
        pass


def interleave(gens):
    gens = list(gens)
    while gens:
        for g in list(gens):
            try:
                next(g)
            except StopIteration:
                gens.remove(g)


_UNIQ = [0]


def sbuf(nc, es, name, shape, dt):
    _UNIQ[0] += 1
    return es.enter_context(nc.sbuf_tensor("%s_%d" % (name, _UNIQ[0]), list(shape), dt))


def rms_block(k, C, xT, t0, gcol, dst_fn, dst_key, xkeys, bank=7):
    ps, sq, rstd = C.ps, C.sq, C.rstd
    pk = 'ps%d' % bank
    for kc in range(8):
        j = kc % 2
        k.op('act', lambda e: e.activation(out=sq[:, j, :], in_=xT[:, kc, t0:t0 + 512], func=AF.Square),
             reads=xkeys, writes=[('sq', j)])
        k.op('pe', lambda e: e.matmul(ps[bank][:], lhsT=C.ones_bf[:], rhs=sq[:, j, :], start=(kc == 0), stop=(kc == 7)),
             reads=[('sq', j)], writes=[pk])
    k.op('act', lambda e: e.activation(out=rstd[:], in_=ps[bank][:], func=AF.Sqrt, scale=1.0 / DM, bias=C.eps_col[:, 0:1]),
         reads=[pk], writes=['rstd'])
    k.op('dve', lambda e: e.reciprocal(out=rstd[:], in_=rstd[:]), reads=['rstd'], writes=['rstd'])
    for kc in range(8):
        k.op('dve', lambda e: e.scalar_tensor_tensor(out=dst_fn(kc), in0=xT[:, kc, t0:t0 + 512], scalar=gcol[:, kc:kc + 1],
                                                     in1=rstd[:], op0=ALU.mult, op1=ALU.mult),
             reads=list(xkeys) + ['rstd'], writes=[dst_key])


def xkeys_of(t0, n=512):
    return [('xT', i) for i in range(t0 // 512, (t0 + n - 1) // 512 + 1)]


def ffn(k, nc, C, xT, gcol, w13r, w2r):
    with ExitStack() as es:
        xnT = sbuf(nc, es, "f_xnT", [128, 8, 1024], BF16)
        gT = sbuf(nc, es, "f_gT", [128, NFC, 1024], BF16)
        w13s = sbuf(nc, es, "f_w13s", [128, 2, 2, 8, 128], BF16)
        w2s = sbuf(nc, es, "f_w2s", [128, 2, NFC, 128], BF16)
        sa = sbuf(nc, es, "f_sa", [128, 2, 512], F32)
        ps = C.ps
        wc13 = 0
        wc2 = 0
        for half in range(2):
            for tb in range(2):
                t0 = half * 1024 + tb * 512
                rms_block(k, C, xT, t0, gcol, lambda kc: xnT[:, kc, tb * 512:(tb + 1) * 512], ('xnT', tb), xkeys_of(t0))
            for fc in range(NFC):
                wb = wc13 % 2
                wc13 += 1
                k.dma('pool', w13s[:, wb], w13r[fc], writes=[('w13s', wb)])
                bs = (fc % 2) * 4
                for kc in range(8):
                    for ab in range(2):
                        for tb in range(2):
                            bank = bs + ab * 2 + tb
                            k.op('pe', lambda e: e.matmul(ps[bank][:], lhsT=w13s[:, wb, ab, kc, :],
                                                          rhs=xnT[:, kc, tb * 512:(tb + 1) * 512],
                                                          start=(kc == 0), stop=(kc == 7)),
                                 reads=[('w13s', wb), ('xnT', tb)], writes=['ps%d' % bank], signal=(kc == 7))
                for tb in range(2):
                    ba, bb = bs + tb, bs + 2 + tb
                    k.op('act', lambda e: e.activation(out=sa[:, tb, :], in_=ps[ba][:], func=AF.Silu),
                         reads=['ps%d' % ba], writes=[('sa', tb)])
                    k.op('dve', lambda e: e.tensor_tensor(out=gT[:, fc, tb * 512:(tb + 1) * 512], in0=sa[:, tb, :],
                                                          in1=ps[bb][:], op=ALU.mult),
                         reads=[('sa', tb), 'ps%d' % bb], writes=[('gT', fc, tb)])
            for dc in range(8):
                wb = wc2 % 2
                wc2 += 1
                k.dma('pool', w2s[:, wb], w2r[dc], writes=[('w2s', wb)])
                bs = (dc % 4) * 2
                for fc in range(NFC):
                    for tb in range(2):
                        bank = bs + tb
                        k.op('pe', lambda e: e.matmul(ps[bank][:], lhsT=w2s[:, wb, fc, :],
                                                      rhs=gT[:, fc, tb * 512:(tb + 1) * 512],
                                                      start=(fc == 0), stop=(fc == NFC - 1)),
                             reads=[('w2s', wb), ('gT', fc, tb)], writes=['ps%d' % bank], signal=(fc == NFC - 1))
                for tb in range(2):
                    bank = bs + tb
                    t0 = half * 1024 + tb * 512
                    xk = ('xT', t0 // 512)
                    k.op('dve', lambda e: e.scalar_tensor_tensor(out=xT[:, dc, t0:t0 + 512], in0=ps[bank][:], scalar=0.5,
                                                                 in1=xT[:, dc, t0:t0 + 512], op0=ALU.mult, op1=ALU.add),
                         reads=['ps%d' % bank, xk], writes=[xk])
        k.barrier()


def qk_norm_rope(k, C, T, src, srckey, nh, gain_bc, scale, ti, out, outkey, tag):
    w = nh * 64
    sqv, ssq, t1, t2, kn = T.sqv, T.ssq, T.t1, T.t2, T.kn
    kkn = tag + ('sqv' if T.kn is T.sqv else 'kn')
    yield k.op('act', lambda e: e.activation(out=sqv[:, 0:w], in_=src, func=AF.Square), reads=[srckey], writes=[tag + 'sqv'])
    yield k.op('dve', lambda e: e.tensor_reduce(out=ssq[:, 0:nh], in_=sqv[:, 0:w].rearrange("p (h d) -> p h d", d=64),
                                          axis=AX.X, op=ALU.add), reads=[tag + 'sqv'], writes=[tag + 'ssq'])
    yield k.op('act', lambda e: e.activation(out=ssq[:, 0:nh], in_=ssq[:, 0:nh], func=AF.Sqrt, scale=1.0 / 64, bias=C.eps_col[:, 0:1]),
         reads=[tag + 'ssq'], writes=[tag + 'ssq'])
    yield k.op('dve', lambda e: e.reciprocal(out=ssq[:, 0:nh], in_=ssq[:, 0:nh]), reads=[tag + 'ssq'], writes=[tag + 'ssq'])
    knv = kn[:, 0:w].rearrange("p (h d) -> p h d", d=64)
    yield k.op('dve', lambda e: e.tensor_tensor(out=knv, in0=src.rearrange("p (h d) -> p h d", d=64),
                                          in1=ssq[:, 0:nh].unsqueeze(2).to_broadcast([128, nh, 64]), op=ALU.mult),
         reads=[srckey, tag + 'ssq'], writes=[kkn])
    yield k.op('dve', lambda e: e.scalar_tensor_tensor(out=knv, in0=knv, scalar=float(scale),
                                                 in1=gain_bc.unsqueeze(1).to_broadcast([128, nh, 64]),
                                                 op0=ALU.mult, op1=ALU.mult),
         reads=[kkn], writes=[kkn])
    x0 = knv[:, :, 0:32]
    x1 = knv[:, :, 32:64]
    cb = C.cos[:, ti, :].unsqueeze(1).to_broadcast([128, nh, 32])
    sb_ = C.sin[:, ti, :].unsqueeze(1).to_broadcast([128, nh, 32])
    ov = out.rearrange("p (h d) -> p h d", d=64)
    t1v = t1[:, 0:nh * 32].rearrange("p (h d) -> p h d", d=32)
    t2v = t2[:, 0:nh * 32].rearrange("p (h d) -> p h d", d=32)
    yield k.op('pool', lambda e: e.tensor_tensor(out=t1v, in0=x0, in1=cb, op=ALU.mult), reads=[kkn], writes=[tag + 't1'])
    yield k.op('dve', lambda e: e.tensor_tensor(out=t2v, in0=x1, in1=sb_, op=ALU.mult), reads=[kkn], writes=[tag + 't2'])
    yield k.op('dve', lambda e: e.tensor_tensor(out=ov[:, :, 0:32], in0=t1v, in1=t2v, op=ALU.subtract),
         reads=[tag + 't1', tag + 't2'], writes=[outkey])
    yield k.op('pool', lambda e: e.tensor_tensor(out=t1v, in0=x0, in1=sb_, op=ALU.mult), reads=[kkn, outkey], writes=[tag + 't1'])
    yield k.op('dve', lambda e: e.tensor_tensor(out=t2v, in0=x1, in1=cb, op=ALU.mult), reads=[kkn, outkey], writes=[tag + 't2'])
    yield k.op('dve', lambda e: e.tensor_tensor(out=ov[:, :, 32:64], in0=t1v, in1=t2v, op=ALU.add),
         reads=[tag + 't1', tag + 't2'], writes=[outkey])


def mlstm_gates(k, C, G, ps_g, gkey, pb, pbkey):
    g, e1, lf, a, tmp = G.g, G.e1, G.lf, G.a, G.tmp
    yield k.op('dve', lambda e: e.tensor_tensor(out=g[:], in0=ps_g, in1=C.gateb_bc[:], op=ALU.add), reads=[gkey], writes=[G.pfx + '.g'])
    g4 = g[:].rearrange("p (d i h) -> p d i h", d=2, i=2)
    e1v = e1[:].rearrange("p (d h) -> p d h", d=2)
    yield k.op('act', lambda e: e.activation(out=e1v, in_=g4[:, :, 1, :], func=AF.Exp, scale=-1.0), reads=[G.pfx + '.g'], writes=[G.pfx + '.e1'])
    yield k.op('act', lambda e: e.activation(out=e1[:], in_=e1[:], func=AF.Ln, bias=C.one_col[:, 0:1]), reads=[G.pfx + '.e1'], writes=[G.pfx + '.e1'])
    yield k.op('dve', lambda e: e.tensor_scalar(out=lf[:], in0=e1[:], scalar1=-1.0, scalar2=None, op0=ALU.mult), reads=[G.pfx + '.e1'], writes=[G.pfx + '.lf'])
    yield k.op('pe', lambda e: e.matmul(pb[:, 0:4], lhsT=C.tri_f[:], rhs=lf[:, 0:4], start=True, stop=True), reads=[G.pfx + '.lf'], writes=[pbkey], signal=False)
    yield k.op('pe', lambda e: e.matmul(pb[:, 4:8], lhsT=C.tri_b[:], rhs=lf[:, 4:8], start=True, stop=True), reads=[G.pfx + '.lf'], writes=[pbkey], signal=False)
    yield k.op('pe', lambda e: e.matmul(pb[:, 8:16], lhsT=C.ones_f[:], rhs=lf[:], start=True, stop=True), reads=[G.pfx + '.lf'], writes=[pbkey])
    pbs = G.pbs
    pk = G.pfx + '.pbs'
    yield k.op('dve', lambda e: e.tensor_copy(out=pbs[:], in_=pb), reads=[pbkey], writes=[pk])
    av = a[:].rearrange("p (d h) -> p d h", d=2)
    yield k.op('dve', lambda e: e.tensor_tensor(out=av, in0=g4[:, :, 0, :], in1=pbs[:, 0:8].rearrange("p (d h) -> p d h", d=2), op=ALU.subtract),
         reads=[G.pfx + '.g', pk], writes=[G.pfx + '.a'])
    yield k.op('act', lambda e: e.activation(out=G.eb[:], in_=pbs[:, 0:8], func=AF.Exp), reads=[pk], writes=[G.pfx + '.eb'])
    yield k.op('act', lambda e: e.activation(out=G.edec[:], in_=pbs[:, 8:16], func=AF.Exp), reads=[pk], writes=[G.pfx + '.edec'])
    yield k.op('pool', lambda e: e.tensor_tensor(out=tmp[:], in0=a[:], in1=pbs[:, 8:16], op=ALU.add), reads=[G.pfx + '.a', pk], writes=[G.pfx + '.tmp'])
    yield k.op('act', lambda e: e.activation(out=G.wend[:], in_=tmp[:], func=AF.Exp), reads=[G.pfx + '.tmp'], writes=[G.pfx + '.wend'])


def mlstm_local(k, C, G, ps_kv, kvkey, pl, plkeys, mv_aug_tile, mvkey):
    kw = G.kw
    yield k.op('act', lambda e: e.activation(out=mv_aug_tile[:, :, 0:64], in_=ps_kv[:, 256:512].rearrange("p (h d) -> p h d", d=64), func=AF.Copy),
         reads=[kvkey], writes=[mvkey])
    for d in range(2):
        yield k.op('dve', lambda e: e.scalar_tensor_tensor(out=kw[:, d], in0=ps_kv[:, 0:256].rearrange("p (h d) -> p h d", d=64), scalar=0.125,
                                                     in1=G.wend[:, d * 4:(d + 1) * 4].unsqueeze(2).to_broadcast([128, 4, 64]),
                                                     op0=ALU.mult, op1=ALU.mult),
             reads=[kvkey, G.pfx + '.wend'], writes=[(G.pfx + '.kw', d)])
        for h in range(4):
            yield k.op('pe', lambda e: e.matmul(pl[d][0:64, h * 80:h * 80 + 65], lhsT=kw[:, d, h, :], rhs=mv_aug_tile[:, h, 0:65], start=True, stop=True),
                 reads=[(G.pfx + '.kw', d), mvkey], writes=[plkeys[d]], signal=(h == 3))


def alloc_gates(nc, es, pfx):
    G = Ctx()
    G.pfx = pfx
    G.g = sbuf(nc, es, pfx + "g", [128, 16], F32)
    G.e1 = sbuf(nc, es, pfx + "e1", [128, 8], F32)
    G.lf = sbuf(nc, es, pfx + "lf", [128, 8], F32)
    G.a = sbuf(nc, es, pfx + "a", [128, 8], F32)
    G.tmp = sbuf(nc, es, pfx + "tmp", [128, 8], F32)
    G.eb = sbuf(nc, es, pfx + "eb", [128, 8], F32)
    G.edec = sbuf(nc, es, pfx + "edec", [128, 8], F32)
    G.wend = sbuf(nc, es, pfx + "wend", [128, 8], F32)
    G.kw = sbuf(nc, es, pfx + "kw", [128, 2, 4, 64], BF16)
    G.pbs = sbuf(nc, es, pfx + "pbs", [128, 16], F32)
    G.loc = sbuf(nc, es, pfx + "loc", [64, 2, 4, 65], F32)
    return G


def p1_mixer(k, nc, C, xT, W, l, X):
    ps = C.ps
    with ExitStack() as es:
        xnb = sbuf(nc, es, "p1_xnb", [128, 8, 512], BF16)
        wkv = sbuf(nc, es, "p1_wkv", [128, 8, 256], BF16)
        wmk = sbuf(nc, es, "p1_wmk", [128, 8, 528], BF16)
        wcv = sbuf(nc, es, "p1_wcv", [128, 8, 512], BF16)
        kTl = sbuf(nc, es, "p1_kTl", [128, NT], BF16)
        vsb = sbuf(nc, es, "p1_vsb", [128, NTILE, 128], BF16)
        halo = sbuf(nc, es, "p1_halo", [128, 2, 30], F32)
        sg = sbuf(nc, es, "p1_sg", [128, 64], F32)
        kr = sbuf(nc, es, "p1_kr", [128, 128], F32)
        mv_aug = sbuf(nc, es, "p1_mvaug", [128, 2, 4, 72], BF16)
        agg = sbuf(nc, es, "p1_agg", [64, 8, 66], F32)
        tmpS = sbuf(nc, es, "p1_tmpS", [64, 4, 65], F32)
        Ts = []
        for sfx in ("a", "b"):
            T = Ctx()
            T.sqv = sbuf(nc, es, "p1_sqv" + sfx, [128, 128], F32)
            T.ssq = sbuf(nc, es, "p1_ssq" + sfx, [128, 8], F32)
            T.t1 = sbuf(nc, es, "p1_t1" + sfx, [128, 64], F32)
            T.t2 = sbuf(nc, es, "p1_t2" + sfx, [128, 64], F32)
            T.kn = sbuf(nc, es, "p1_kn" + sfx, [128, 128], F32)
            Ts.append(T)
        kr2 = sbuf(nc, es, "p1_kr2", [128, 128], F32)
        Gs = [alloc_gates(nc, es, "p1_Ga"), alloc_gates(nc, es, "p1_Gb")]

        k.dma('pool', wkv[:], W.w_kv[l], writes=['wkv'])
        k.dma('pool', wmk[:, :, 0:512], W.w_m[l][:, :, 256:768], writes=['wmk'])
        k.dma('pool', wmk[:, :, 512:528], W.w_m[l][:, :, 1024:1040], writes=['wmk'])
        k.dma('pool', wcv[:], W.w_conv[l], writes=['wcv'])
        k.op('dve', lambda e: e.memset(mv_aug[:], 1.0), writes=[('mv', 0), ('mv', 1)])
        k.op('dve', lambda e: e.memset(agg[:], 0.0), writes=['agg'])
        k.op('dve', lambda e: e.memset(agg[:, :, 65:66], 1.0), reads=[], writes=['agg'])
        gcol = C.pp[:, l, 8:16]
        for tb in range(4):
            t0 = tb * 512
            rms_block(k, C, xT, t0, gcol, lambda kc: xnb[:, kc, :], 'xnb', xkeys_of(t0))
            if tb in (0, 3) and 'halo' not in SKIP:
                c0 = 0 if tb == 0 else 512 - PAD
                for cc in range(4):
                    for kc in range(8):
                        k.op('pe', lambda e: e.matmul(ps[6][:, cc * 16:cc * 16 + PAD], lhsT=wcv[:, kc, cc * 128:(cc + 1) * 128],
                                                      rhs=xnb[:, kc, c0:c0 + PAD], start=(kc == 0), stop=(kc == 7)),
                             reads=['wcv', 'xnb'], writes=['ps6'], signal=(kc == 7))
                h0 = 0 if tb == 0 else PAD
                k.op('act', lambda e: e.activation(out=sg[:, 0:32], in_=ps[6][:, 32:64], func=AF.Sigmoid), reads=['ps6'], writes=['sg'])
                for cc in range(2):
                    k.op('dve', lambda e: e.tensor_tensor(out=halo[:, cc, h0:h0 + PAD], in0=ps[6][:, cc * 16:cc * 16 + PAD],
                                                          in1=sg[:, cc * 16:cc * 16 + PAD], op=ALU.mult),
                         reads=['ps6', 'sg'], writes=['halo'])
            def tile_gen(tt):
                ti = tb * 4 + tt
                par = ti % 2
                G = Gs[par]
                T = Ts[par]
                krp = kr if par == 0 else kr2
                krk = 'kr%d' % par
                pkv, kkv = (ps[0], 'ps0') if par == 0 else (ps[6], 'ps6')
                pmk, kmk = (ps[2], 'ps2') if par == 0 else (ps[1], 'ps1')
                tsl = slice(tt * 128, (tt + 1) * 128)
                for kc in range(8):
                    k.op('pe', lambda e: e.matmul(pkv[:, 0:256], lhsT=xnb[:, kc, tsl], rhs=wkv[:, kc, :], start=(kc == 0), stop=(kc == 7)),
                               reads=['xnb', 'wkv'], writes=[kkv], signal=(kc == 7))
                yield
                yield k.op('act', lambda e: e.activation(out=vsb[:, ti, :], in_=pkv[:, 128:256], func=AF.Copy), reads=[kkv], writes=['vsb'])
                yield from qk_norm_rope(k, C, T, pkv[:, 0:128], kkv, 2, C.rp[:, l, 64:128], 1.0, ti, krp[:], krk, 'p1' + 'ab'[par])
                yield k.op('pe', lambda e: e.transpose(out=pkv[:, 256:384], in_=krp[:], identity=C.ident_f[:]), reads=[krk], writes=[kkv])
                yield k.op('act', lambda e: e.activation(out=kTl[:, ti * 128:(ti + 1) * 128], in_=pkv[:, 256:384], func=AF.Copy),
                           reads=[kkv], writes=['kTl'])
                for kc in range(8):
                    k.op('pe', lambda e: e.matmul(pmk[:], lhsT=xnb[:, kc, tsl], rhs=wmk[:, kc, 0:512], start=(kc == 0), stop=(kc == 7)),
                         reads=['xnb', 'wmk'], writes=[kmk], signal=(kc == 7))
                yield
                for kc in range(8):
                    k.op('pe', lambda e: e.matmul(pkv[:, 384:400], lhsT=xnb[:, kc, tsl], rhs=wmk[:, kc, 512:528], start=(kc == 0), stop=(kc == 7)),
                         reads=['xnb', 'wmk'], writes=[kkv], signal=(kc == 7))
                yield
                yield from mlstm_gates(k, C, G, pkv[:, 384:400], kkv, pkv[:, 416:432], kkv)
                lb = (4, 5) if par == 0 else (3, 7)
                yield from mlstm_local(k, C, G, pmk, kmk, [ps[lb[0]], ps[lb[1]]], ['ps%d' % lb[0], 'ps%d' % lb[1]], mv_aug[:, par], ('mv', par))
                for d in range(2):
                    yield k.op('act', lambda e: e.activation(out=G.loc[:, d], in_=ps[lb[d]][0:64, 0:320].rearrange("p (h e) -> p h e", e=80)[:, :, 0:65], func=AF.Copy),
                               reads=['ps%d' % lb[d]], writes=[(G.pfx + '.loc', d)])

            def agg_update(tt):
                par = (tb * 4 + tt) % 2
                G = Gs[par]
                edf = G.edec[0:64, 0:4]
                edb = G.edec[0:64, 4:8]
                k.op('dve', lambda e: e.tensor_tensor(out=agg[:, 0:4, 0:65], in0=agg[:, 0:4, 0:65],
                                                      in1=edf.unsqueeze(2).to_broadcast([64, 4, 65]), op=ALU.mult),
                     reads=['agg', G.pfx + '.edec'], writes=['agg'])
                k.op('dve', lambda e: e.tensor_tensor(out=agg[:, 0:4, 0:65], in0=agg[:, 0:4, 0:65], in1=G.loc[:, 0], op=ALU.add),
                     reads=['agg', (G.pfx + '.loc', 0)], writes=['agg'])
                k.op('dve', lambda e: e.tensor_tensor(out=agg[:, 0:4, 65:66], in0=agg[:, 0:4, 65:66], in1=edf.unsqueeze(2), op=ALU.mult),
                     reads=['agg', G.pfx + '.edec'], writes=['agg'])
                k.op('dve', lambda e: e.tensor_tensor(out=tmpS[:], in0=G.loc[:, 1], in1=agg[:, 4:8, 65:66].to_broadcast([64, 4, 65]), op=ALU.mult),
                     reads=['agg', (G.pfx + '.loc', 1)], writes=['tmpS'])
                k.op('dve', lambda e: e.tensor_tensor(out=agg[:, 4:8, 0:65], in0=agg[:, 4:8, 0:65], in1=tmpS[:], op=ALU.add),
                     reads=['agg', 'tmpS'], writes=['agg'])
                k.op('dve', lambda e: e.tensor_tensor(out=agg[:, 4:8, 65:66], in0=agg[:, 4:8, 65:66], in1=edb.unsqueeze(2), op=ALU.mult),
                     reads=['agg', G.pfx + '.edec'], writes=['agg'])

            for t2 in (0, 2):
                interleave([tile_gen(t2), tile_gen(t2 + 1)])
                agg_update(t2)
                agg_update(t2 + 1)
        if 'xw' in SKIP:
            k.barrier()
            return
        k.dma('sp', X.kx, kTl[:], reads=['kTl'], writes=['X.kx'])
        k.dma('sp', X.vx.rearrange("(t p) c -> p t c", p=128), vsb[:], reads=['vsb'], writes=['X.vx'])
        k.dma('sp', X.hx, halo[:], reads=['halo'], writes=['X.hx'])
        k.dma('sp', X.ax, agg[:], reads=['agg'], writes=['X.ax'])
        k.barrier()


def p2_conv(k, nc, C, xT, W, l, XA, yT_cm):
    ps = C.ps
    with ExitStack() as es:
        xnb = sbuf(nc, es, "cv_xnb", [128, 8, 512], BF16)
        wcv = sbuf(nc, es, "cv_wcv", [128, 8, 512], BF16)
        uT = sbuf(nc, es, "cv_uT", [128, 2, NT + 2 * PAD], BF16 if CONV_PE else F32)
        acc = sbuf(nc, es, "cv_acc", [128, 2, NT], F32)
        hall = sbuf(nc, es, "cv_hall", [128, 4, 2, 30], F32)
        hacc = sbuf(nc, es, "cv_hacc", [128, 2, 30], F32)
        sg = sbuf(nc, es, "cv_sg", [128, 2, 512], F32)
        sq = sbuf(nc, es, "cv_sq", [128, 512], F32)
        mean = sbuf(nc, es, "cv_mean", [128, 512], F32)
        msq = sbuf(nc, es, "cv_msq", [128, 512], F32)
        rs = sbuf(nc, es, "cv_rs", [128, 512], F32)
        tt_ = sbuf(nc, es, "cv_t", [128, 512], F32)
        cp = C.pp[:, l, 44:112].rearrange("p (c j) -> p c j", j=34)
        k.dma('pool', wcv[:], W.w_conv[l], writes=['wcv'])
        for i in range(4):
            k.dma('sp', hall[:, i], XA.hx_all[i], writes=['hall'])
        if CONV_PE:
            dgw = sbuf(nc, es, "cv_dgw", [128, 2, CONV_W, 128], BF16)
            for cc in range(2):
                for j in range(CONV_W):
                    k.op('pool', lambda e: e.tensor_scalar(out=dgw[:, cc, j, :], in0=C.ident_f[:], scalar1=cp[:, cc, j:j + 1], scalar2=None, op0=ALU.mult),
                         reads=[], writes=[('dgw', cc)])
        k.op('dve', lambda e: e.memset(hacc[:], 0.0), writes=['hacc'])
        for i in range(4):
            k.op('dve', lambda e: e.scalar_tensor_tensor(out=hacc[:, :, 0:PAD], in0=hall[:, i, :, PAD:2 * PAD], scalar=C.sel[:, 8 + i:9 + i],
                                                         in1=hacc[:, :, 0:PAD], op0=ALU.mult, op1=ALU.add),
                 reads=['hall', 'hacc'], writes=['hacc'])
            k.op('dve', lambda e: e.scalar_tensor_tensor(out=hacc[:, :, PAD:2 * PAD], in0=hall[:, i, :, 0:PAD], scalar=C.sel[:, 12 + i:13 + i],
                                                         in1=hacc[:, :, PAD:2 * PAD], op0=ALU.mult, op1=ALU.add),
                 reads=['hall', 'hacc'], writes=['hacc'])
        k.op('dve', lambda e: e.tensor_copy(out=uT[:, :, 0:PAD], in_=hacc[:, :, 0:PAD]), reads=['hacc'], writes=['uT_h'])
        k.op('dve', lambda e: e.tensor_copy(out=uT[:, :, NT + PAD:NT + 2 * PAD], in_=hacc[:, :, PAD:2 * PAD]), reads=['hacc'], writes=['uT_h'])
        gcol = C.pp[:, l, 8:16]
        for tb in range(4):
            t0 = tb * 512
            rms_block(k, C, xT, t0, gcol, lambda kc: xnb[:, kc, :], 'xnb', xkeys_of(t0))
            for cc in range(4):
                for kc in range(8):
                    k.op('pe', lambda e: e.matmul(ps[cc][:], lhsT=wcv[:, kc, cc * 128:(cc + 1) * 128], rhs=xnb[:, kc, :],
                                                  start=(kc == 0), stop=(kc == 7)),
                         reads=['wcv', 'xnb'], writes=['ps%d' % cc], signal=(kc == 7))
            for cc in range(2):
                k.op('act', lambda e: e.activation(out=sg[:, cc, :], in_=ps[2 + cc][:], func=AF.Sigmoid), reads=['ps%d' % (2 + cc)], writes=[('sg', cc)])
                k.op('dve', lambda e: e.tensor_tensor(out=uT[:, cc, PAD + t0:PAD + t0 + 512], in0=ps[cc][:], in1=sg[:, cc, :], op=ALU.mult),
                     reads=['ps%d' % cc, ('sg', cc)], writes=[('uT', tb)])
        ukeys = [('uT', i) for i in range(4)] + ['uT_h']
        if CONV_PE:
            rr = 0
            for tb in range(4):
                t0 = tb * 512
                for cc in range(2):
                    b = 4 + (rr % 4)
                    rr += 1
                    for j in range(CONV_W):
                        k.op('pe', lambda e: e.matmul(ps[b][:], lhsT=dgw[:, cc, j, :], rhs=uT[:, cc, t0 + j:t0 + j + 512], start=(j == 0), stop=(j == CONV_W - 1)),
                             reads=ukeys + [('dgw', cc)], writes=['ps%d' % b], signal=(j == CONV_W - 1))
                    k.op('act', lambda e: e.activation(out=acc[:, cc, t0:t0 + 512], in_=ps[b][:], func=AF.Identity, bias=cp[:, cc, 31:32]),
                         reads=['ps%d' % b], writes=[('acc', cc)])
        else:
            for cc in range(2):
                k.op('dve', lambda e: e.tensor_scalar(out=acc[:, cc, :], in0=uT[:, cc, 0:NT], scalar1=cp[:, cc, 0:1], scalar2=None, op0=ALU.mult),
                     reads=ukeys, writes=[('acc', cc)])
                for j in range(1, CONV_W):
                    k.op('dve', lambda e: e.scalar_tensor_tensor(out=acc[:, cc, :], in0=uT[:, cc, j:j + NT], scalar=cp[:, cc, j:j + 1],
                                                                 in1=acc[:, cc, :], op0=ALU.mult, op1=ALU.add),
                         reads=ukeys + [('acc', cc)], writes=[('acc', cc)])
                k.op('act', lambda e: e.activation(out=acc[:, cc, :], in_=acc[:, cc, :], func=AF.Identity, bias=cp[:, cc, 31:32]),
                     reads=[('acc', cc)], writes=[('acc', cc)])
        for tb in range(4):
            t0 = tb * 512
            for cc in range(2):
                k.op('pe', lambda e: e.matmul(ps[0][:], lhsT=C.ones_f[:], rhs=acc[:, cc, t0:t0 + 512], start=(cc == 0), stop=(cc == 1)),
                     reads=[('acc', cc)], writes=['ps0'], signal=(cc == 1))
            for cc in range(2):
                k.op('act', lambda e: e.activation(out=sq[:], in_=acc[:, cc, t0:t0 + 512], func=AF.Square), reads=[('acc', cc)], writes=['cvsq'])
                k.op('pe', lambda e: e.matmul(ps[1][:], lhsT=C.ones_f[:], rhs=sq[:], start=(cc == 0), stop=(cc == 1)),
                     reads=['cvsq'], writes=['ps1'])
            k.op('dve', lambda e: e.tensor_scalar(out=mean[:], in0=ps[0][:], scalar1=1.0 / 256, scalar2=None, op0=ALU.mult), reads=['ps0'], writes=['mean'])
            k.op('pool', lambda e: e.tensor_tensor(out=msq[:], in0=mean[:], in1=mean[:], op=ALU.mult), reads=['mean'], writes=['msq'])
            k.op('dve', lambda e: e.scalar_tensor_tensor(out=rs[:], in0=ps[1][:], scalar=1.0 / 256, in1=msq[:], op0=ALU.mult, op1=ALU.subtract),
                 reads=['ps1', 'msq'], writes=['rs'])
            k.op('act', lambda e: e.activation(out=rs[:], in_=rs[:], func=AF.Sqrt, bias=C.eps_col[:, 0:1]), reads=['rs'], writes=['rs'])
            k.op('dve', lambda e: e.reciprocal(out=rs[:], in_=rs[:]), reads=['rs'], writes=['rs'])
            for cc in range(2):
                k.op('dve', lambda e: e.tensor_tensor(out=tt_[:], in0=acc[:, cc, t0:t0 + 512], in1=mean[:], op=ALU.subtract),
                     reads=[('acc', cc), 'mean'], writes=['cvt'])
                k.op('dve', lambda e: e.tensor_tensor(out=tt_[:], in0=tt_[:], in1=rs[:], op=ALU.mult), reads=['cvt', 'rs'], writes=['cvt'])
                k.op('act', lambda e: e.activation(out=yT_cm[:, cc, t0:t0 + 512], in_=tt_[:], func=AF.Silu, scale=cp[:, cc, 32:33], bias=cp[:, cc, 33:34]),
                     reads=['cvt'], writes=[('ycm', cc, tb)])
        k.barrier()


def p2_mlstm(k, nc, C, xT, W, l, XA, yT_cm):
    ps = C.ps
    with ExitStack() as es:
        xnb = sbuf(nc, es, "ml_xnb", [128, 8, 512], BF16)
        wm = sbuf(nc, es, "ml_wm", [128, 8, 1040], BF16)
        mqT = sbuf(nc, es, "ml_mqT", [128, 2, NT], BF16)
        mkT = sbuf(nc, es, "ml_mkT", [128, 2, NT], BF16)
        mv_aug = sbuf(nc, es, "ml_mvaug", [128, NTILE, 4, 72], BF16)
        sgo = sbuf(nc, es, "ml_sgo", [128, NTILE, 256], BF16)
        CTf = sbuf(nc, es, "ml_CTf", [128, NTILE, 4, 72], BF16)
        CTb = sbuf(nc, es, "ml_CTb", [128, NTILE, 4, 72], BF16)
        locb = sbuf(nc, es, "ml_locb", [64, NTILE, 4, 65], F32)
        edb_all = sbuf(nc, es, "ml_edb", [64, NTILE, 4], F32)
        a_all = sbuf(nc, es, "ml_a", [128, NTILE, 8], F32)
        eb_all = sbuf(nc, es, "ml_eb", [128, NTILE, 8], F32)
        dg = sbuf(nc, es, "ml_dg", [128, 4, 128], F32)
        b_all = sbuf(nc, es, "ml_ball", [128, NTILE, 8], F32)
        S = sbuf(nc, es, "ml_S", [64, 8, 65], F32)
        al = sbuf(nc, es, "ml_al", [64, 8], F32)
        DT = sbuf(nc, es, "ml_DT", [128, 4, 128], F32)
        SwT = sbuf(nc, es, "ml_SwT", [128, 4, 128], BF16)
        tI = sbuf(nc, es, "ml_tI", [128, 4, 65], F32)
        hn = sbuf(nc, es, "ml_hn", [128, 4, 65], F32)
        den = sbuf(nc, es, "ml_den", [128, 4], F32)
        hs = sbuf(nc, es, "ml_hs", [128, 256], F32)
        hq = sbuf(nc, es, "ml_hq", [128, 256], F32)
        hss = sbuf(nc, es, "ml_hss", [128, 4], F32)
        Gs = [alloc_gates(nc, es, "ml_Ga"), alloc_gates(nc, es, "ml_Gb")]

        k.dma('pool', wm[:], W.w_m[l], writes=['wm'])
        aggs = locb[:, 0:9].rearrange("p a h e -> p (a h e)")[:, 0:4 * 8 * 66].rearrange("p (i j e) -> p i j e", i=4, j=8)
        for i in range(4):
            k.dma('sp', aggs[:, i], XA.ax_all[i], writes=['aggs'])
        k.op('dve', lambda e: e.memset(mv_aug[:], 1.0), writes=['mv_all'])
        k.op('dve', lambda e: e.memset(S[:], 0.0), writes=['S'])
        for d in range(2):
            order = range(4) if d == 0 else range(3, -1, -1)
            for i in order:
                selc = C.sel[0:64, d * 4 + i:d * 4 + i + 1]
                k.op('dve', lambda e: e.tensor_scalar(out=al[:, 0:4], in0=aggs[:, i, d * 4:(d + 1) * 4, 65], scalar1=-1.0, scalar2=selc,
                                                      op0=ALU.add, op1=ALU.mult), reads=['aggs'], writes=['al'])
                k.op('dve', lambda e: e.tensor_scalar(out=al[:, 0:4], in0=al[:, 0:4], scalar1=1.0, scalar2=None, op0=ALU.add), reads=['al'], writes=['al'])
                k.op('dve', lambda e: e.tensor_tensor(out=S[:, d * 4:(d + 1) * 4, :], in0=S[:, d * 4:(d + 1) * 4, :],
                                                      in1=al[:, 0:4].unsqueeze(2).to_broadcast([64, 4, 65]), op=ALU.mult),
                     reads=['S', 'al'], writes=['S'])
                k.op('dve', lambda e: e.scalar_tensor_tensor(out=S[:, d * 4:(d + 1) * 4, :], in0=aggs[:, i, d * 4:(d + 1) * 4, 0:65], scalar=selc,
                                                             in1=S[:, d * 4:(d + 1) * 4, :], op0=ALU.mult, op1=ALU.add),
                     reads=['S', 'aggs'], writes=['S'])
        k.barrier()
        gcol = C.pp[:, l, 8:16]
        for tb in range(4):
            t0 = tb * 512
            rms_block(k, C, xT, t0, gcol, lambda kc: xnb[:, kc, :], 'xnb', xkeys_of(t0))
            for cc in range(4):
                for kc in range(8):
                    k.op('pe', lambda e: e.matmul(ps[cc][:], lhsT=wm[:, kc, cc * 128:(cc + 1) * 128], rhs=xnb[:, kc, :], start=(kc == 0), stop=(kc == 7)),
                         reads=['wm', 'xnb'], writes=['ps%d' % cc], signal=(kc == 7))
            for cc in range(2):
                k.op('act', lambda e: e.activation(out=mqT[:, cc, t0:t0 + 512], in_=ps[cc][:], func=AF.Copy), reads=['ps%d' % cc], writes=[('mqT', tb)])
                k.op('dve', lambda e: e.tensor_scalar(out=mkT[:, cc, t0:t0 + 512], in0=ps[2 + cc][:], scalar1=0.125, scalar2=None, op0=ALU.mult),
                     reads=['ps%d' % (2 + cc)], writes=[('mkT', tb)])
            def banks(ti):
                return (4, 5, 6, 7) if ti % 2 == 0 else (0, 1, 2, 3)

            def tileA_gen(tt):
                ti = tb * 4 + tt
                G = Gs[ti % 2]
                bkv, bg, blf, blb = banks(ti)
                tsl = slice(tt * 128, (tt + 1) * 128)
                for kc in range(8):
                    k.op('pe', lambda e: e.matmul(ps[bkv][:], lhsT=xnb[:, kc, tsl], rhs=wm[:, kc, 256:768], start=(kc == 0), stop=(kc == 7)),
                               reads=['xnb', 'wm'], writes=['ps%d' % bkv], signal=(kc == 7))
                yield
                for kc in range(8):
                    k.op('pe', lambda e: e.matmul(ps[bg][:, 0:272], lhsT=xnb[:, kc, tsl], rhs=wm[:, kc, 768:1040], start=(kc == 0), stop=(kc == 7)),
                               reads=['xnb', 'wm'], writes=['ps%d' % bg], signal=(kc == 7))
                yield
                yield k.op('act', lambda e: e.activation(out=sgo[:, ti, :], in_=ps[bg][:, 0:256], func=AF.Sigmoid), reads=['ps%d' % bg], writes=[('sgo', ti)])
                yield from mlstm_gates(k, C, G, ps[bg][:, 256:272], 'ps%d' % bg, ps[bg][:, 288:304], 'ps%d' % bg)
                yield k.op('pool', lambda e: e.tensor_copy(out=a_all[:, ti, :], in_=G.a[:]), reads=[G.pfx + '.a'], writes=[('a_all', ti)])
                yield k.op('pool', lambda e: e.tensor_copy(out=eb_all[:, ti, :], in_=G.eb[:]), reads=[G.pfx + '.eb'], writes=[('eb_all', ti)])
                yield k.op('pool', lambda e: e.tensor_copy(out=b_all[:, ti, :], in_=G.pbs[:, 0:8]), reads=[G.pfx + '.pbs'], writes=[('b_all', ti)])
                yield from mlstm_local(k, C, G, ps[bkv], 'ps%d' % bkv, [ps[blf], ps[blb]], ['ps%d' % blf, 'ps%d' % blb], mv_aug[:, ti], ('mv', ti))
                yield k.op('act', lambda e: e.activation(out=locb[:, ti], in_=ps[blb][0:64, 0:320].rearrange("p (h e) -> p h e", e=80)[:, :, 0:65], func=AF.Copy),
                           reads=['ps%d' % blb], writes=[('locb', ti)])
                yield k.op('pool', lambda e: e.tensor_copy(out=edb_all[:, ti, :], in_=G.edec[0:64, 4:8]), reads=[G.pfx + '.edec'], writes=[('edb', ti)])

            def fwd_advance(tt):
                ti = tb * 4 + tt
                G = Gs[ti % 2]
                blf = banks(ti)[2]
                k.op('act', lambda e: e.activation(out=CTf[0:64, ti, 0:4:2, 0:65], in_=S[:, 0:4:2, :], func=AF.Copy), reads=['S'], writes=[('CTf', ti)])
                k.op('act', lambda e: e.activation(out=CTf[64:128, ti, 1:4:2, 0:65], in_=S[:, 1:4:2, :], func=AF.Copy), reads=['S'], writes=[('CTf', ti)])
                k.op('dve', lambda e: e.tensor_tensor(out=S[:, 0:4, :], in0=S[:, 0:4, :],
                                                      in1=G.edec[0:64, 0:4].unsqueeze(2).to_broadcast([64, 4, 65]), op=ALU.mult),
                     reads=['S', G.pfx + '.edec'], writes=['S'])
                k.op('dve', lambda e: e.tensor_tensor(out=S[:, 0:4, :], in0=S[:, 0:4, :],
                                                      in1=ps[blf][0:64, 0:320].rearrange("p (h e) -> p h e", e=80)[:, :, 0:65], op=ALU.add),
                     reads=['S', 'ps%d' % blf], writes=['S'])

            for t2 in (0, 2):
                if PASSA_INTERLEAVE:
                    interleave([tileA_gen(t2), tileA_gen(t2 + 1)])
                    fwd_advance(t2)
                    fwd_advance(t2 + 1)
                else:
                    drain(tileA_gen(t2))
                    fwd_advance(t2)
                    drain(tileA_gen(t2 + 1))
                    fwd_advance(t2 + 1)
        for ti in range(NTILE - 1, -1, -1):
            k.op('act', lambda e: e.activation(out=CTb[0:64, ti, 0:4:2, 0:65], in_=S[:, 4:8:2, :], func=AF.Copy), reads=['S'], writes=[('CTb', ti)])
            k.op('act', lambda e: e.activation(out=CTb[64:128, ti, 1:4:2, 0:65], in_=S[:, 5:8:2, :], func=AF.Copy), reads=['S'], writes=[('CTb', ti)])
            if ti > 0:
                k.op('dve', lambda e: e.tensor_tensor(out=S[:, 4:8, :], in0=S[:, 4:8, :],
                                                      in1=edb_all[:, ti, :].unsqueeze(2).to_broadcast([64, 4, 65]), op=ALU.mult),
                     reads=['S', ('edb', ti)], writes=['S'])
                k.op('dve', lambda e: e.tensor_tensor(out=S[:, 4:8, :], in0=S[:, 4:8, :], in1=locb[:, ti], op=ALU.add),
                     reads=['S', ('locb', ti)], writes=['S'])
        for ti in range(NTILE):
            tb = ti // 4
            csl = slice(ti * 128, (ti + 1) * 128)
            for hg in range(2):
                U = []
                for h in (2 * hg, 2 * hg + 1):
                    hp = (h % 2) * 64
                    hc = h // 2
                    pA = ps[h % 2]
                    k.op('pe', lambda e: e.matmul(pA[:, 0:128], lhsT=mkT[hp:hp + 64, hc, csl], rhs=mqT[hp:hp + 64, hc, csl], start=True, stop=True),
                         reads=[('mkT', tb), ('mqT', tb)], writes=['ps%d' % (h % 2)])
                    for d in range(2):
                        sl = (h % 2) * 2 + d
                        bank = 2 + sl
                        U.append((h, hp, hc, pA, d, d * 4 + h, sl, 'ps%d' % bank, ps[bank][:, 0:128], ps[bank][:, 256:321], ps[bank][:, 384:449]))
                for h, hp, hc, pA, d, j, sl, bkey, pL, pN, pI in U:
                    k.op('pool', lambda e: e.tensor_scalar(out=dg[:, sl, :], in0=C.ident_f[:], scalar1=b_all[:, ti, j:j + 1], scalar2=None, op0=ALU.mult),
                         reads=[('b_all', ti)], writes=[('dg', sl)])
                for h, hp, hc, pA, d, j, sl, bkey, pL, pN, pI in U:
                    k.op('pe', lambda e: e.matmul(pL, lhsT=C.ones_f[:], rhs=dg[:, sl, :], start=True, stop=False),
                         reads=[('dg', sl)], writes=[bkey])
                    k.op('pe', lambda e: e.matmul(pL, lhsT=C.ident_bf[:], rhs=C.maskneg[:, d, :], start=False, stop=True),
                         reads=[], writes=[bkey])
                for h, hp, hc, pA, d, j, sl, bkey, pL, pN, pI in U:
                    k.op('act', lambda e: e.activation(out=DT[:, sl, :], in_=pL, func=AF.Exp, bias=a_all[:, ti, j:j + 1]),
                         reads=[bkey, ('a_all', ti)], writes=[('DT', sl)])
                for h, hp, hc, pA, d, j, sl, bkey, pL, pN, pI in U:
                    k.op('dve', lambda e: e.tensor_tensor(out=SwT[:, sl, :], in0=DT[:, sl, :], in1=pA[:, 0:128], op=ALU.mult),
                         reads=[('DT', sl), 'ps%d' % (h % 2)], writes=[('SwT', sl)])
                for h, hp, hc, pA, d, j, sl, bkey, pL, pN, pI in U:
                    CT = CTf if d == 0 else CTb
                    ckey = ('CTf', ti) if d == 0 else ('CTb', ti)
                    k.op('pe', lambda e: e.matmul(pN, lhsT=SwT[:, sl, :], rhs=mv_aug[:, ti, h, 0:65], start=True, stop=True),
                         reads=[('SwT', sl), ('mv', ti), 'mv_all'], writes=[bkey], signal=False)
                    k.op('pe', lambda e: e.matmul(pI, lhsT=mqT[hp:hp + 64, hc, csl], rhs=CT[hp:hp + 64, ti, h, 0:65], start=True, stop=True),
                         reads=[('mqT', tb), ckey], writes=[bkey])
                for h, hp, hc, pA, d, j, sl, bkey, pL, pN, pI in U:
                    k.op('act', lambda e: e.activation(out=tI[:, sl, :], in_=pI, func=AF.Copy, scale=eb_all[:, ti, j:j + 1]),
                         reads=[bkey, ('eb_all', ti)], writes=[('tI', sl)])
                for h, hp, hc, pA, d, j, sl, bkey, pL, pN, pI in U:
                    k.op('dve', lambda e: e.tensor_tensor(out=hn[:, sl, :], in0=tI[:, sl, :], in1=pN, op=ALU.add),
                         reads=[('tI', sl), bkey], writes=[('hn', sl)])
                hk = [('hn', i) for i in range(4)]
                k.op('dve', lambda e: e.tensor_scalar(out=den[:], in0=hn[:, :, 64], scalar1=-1.0, scalar2=None, op0=ALU.mult), reads=hk, writes=['den'])
                k.op('dve', lambda e: e.tensor_tensor(out=den[:], in0=den[:], in1=hn[:, :, 64], op=ALU.max), reads=hk + ['den'], writes=['den'])
                k.op('dve', lambda e: e.tensor_scalar(out=den[:], in0=den[:], scalar1=1.0, scalar2=None, op0=ALU.max), reads=['den'], writes=['den'])
                k.op('dve', lambda e: e.reciprocal(out=den[:], in_=den[:]), reads=['den'], writes=['den'])
                for h in (2 * hg, 2 * hg + 1):
                    s0 = (h % 2) * 2
                    k.op('dve', lambda e: e.tensor_scalar(out=hs[:, h * 64:(h + 1) * 64], in0=hn[:, s0, 0:64], scalar1=den[:, s0:s0 + 1], scalar2=None, op0=ALU.mult),
                         reads=[('hn', s0), 'den'], writes=['hs'])
                    k.op('dve', lambda e: e.scalar_tensor_tensor(out=hs[:, h * 64:(h + 1) * 64], in0=hn[:, s0 + 1, 0:64], scalar=den[:, s0 + 1:s0 + 2],
                                                                 in1=hs[:, h * 64:(h + 1) * 64], op0=ALU.mult, op1=ALU.add),
                         reads=[('hn', s0 + 1), 'den', 'hs'], writes=['hs'])
            k.op('act', lambda e: e.activation(out=hq[:], in_=hs[:], func=AF.Square), reads=['hs'], writes=['hq'])
            k.op('dve', lambda e: e.tensor_reduce(out=hss[:], in_=hq[:].rearrange("p (h d) -> p h d", d=64), axis=AX.X, op=ALU.add),
                 reads=['hq'], writes=['hss'])
            k.op('act', lambda e: e.activation(out=hss[:], in_=hss[:], func=AF.Sqrt, scale=1.0 / 64, bias=C.eps_col[:, 0:1]), reads=['hss'], writes=['hss'])
            k.op('dve', lambda e: e.reciprocal(out=hss[:], in_=hss[:]), reads=['hss'], writes=['hss'])
            k.op('dve', lambda e: e.tensor_tensor(out=hq[:].rearrange("p (h d) -> p h d", d=64), in0=hs[:].rearrange("p (h d) -> p h d", d=64),
                                                  in1=hss[:].unsqueeze(2).to_broadcast([128, 4, 64]), op=ALU.mult),
                 reads=['hs', 'hss', 'hq'], writes=['hq'])
            k.op('pool', lambda e: e.tensor_tensor(out=hq[:], in0=hq[:], in1=C.rp[:, l, 144:400], op=ALU.mult), reads=['hq'], writes=['hq'])
            k.op('dve', lambda e: e.tensor_tensor(out=hq[:], in0=hq[:], in1=sgo[:, ti, :], op=ALU.mult), reads=['hq', ('sgo', ti)], writes=['hq'])
            for cc in range(2):
                k.op('pe', lambda e: e.transpose(out=ps[6 + cc][:, 0:128], in_=hq[:, cc * 128:(cc + 1) * 128], identity=C.ident_f[:]),
                     reads=['hq'], writes=['ps%d' % (6 + cc)])
                k.op('act', lambda e: e.activation(out=yT_cm[:, 2 + cc, csl], in_=ps[6 + cc][:, 0:128], func=AF.Copy),
                     reads=['ps%d' % (6 + cc)], writes=[('ycm', 2 + cc, tb)])
        k.barrier()


def p2_attn(k, nc, C, xT, W, l, XA, yT_at):
    ps = C.ps
    with ExitStack() as es:
        xnb = sbuf(nc, es, "at_xnb", [128, 8, 512], BF16)
        wq = sbuf(nc, es, "at_wq", [128, 8, 512], BF16)
        kTd = sbuf(nc, es, "at_kTd", [128, 2, SEQ], BF16)
        Vo = sbuf(nc, es, "at_Vo", [128, 64, 2, 128], BF16)
        qTb = sbuf(nc, es, "at_qTb", [128, 4, 512], BF16)
        PT = sbuf(nc, es, "at_PT", [128, 2, 2, 512], BF16)
        T = Ctx()
        T.sqv = sbuf(nc, es, "at_sqv", [128, 512], F32)
        T.ssq = sbuf(nc, es, "at_ssq", [128, 8], F32)
        T.t1 = sbuf(nc, es, "at_t1", [128, 256], F32)
        T.t2 = sbuf(nc, es, "at_t2", [128, 256], F32)
        T.kn = T.sqv
        qr = sbuf(nc, es, "at_qr", [128, 512], F32)
        rden = qr
        numB = C.rstd
        k.dma('pool', wq[:], W.w_q[l], writes=['wq'])
        k.op('dve', lambda e: e.memset(Vo[:, :, :, 64:128], 1.0), writes=['Vo1'])
        for i in range(4):
            for kvh in range(2):
                for hf in range(2):
                    k.dma('sp', kTd[hf * 64:(hf + 1) * 64, kvh, i * NT:(i + 1) * NT], XA.kx_all[i, kvh * 64:(kvh + 1) * 64, :], writes=['kTd'])
            for kvh in range(2):
                k.dma('sp', Vo[:, i * NTILE:(i + 1) * NTILE, kvh, 0:64],
                      XA.vx_all[i].rearrange("(t p) c -> p t c", p=128)[:, :, kvh * 64:(kvh + 1) * 64], writes=['Vo'])
        gcol = C.pp[:, l, 8:16]
        for qb in range(4):
            t0 = qb * 512
            rms_block(k, C, xT, t0, gcol, lambda kc: xnb[:, kc, :], 'xnb', xkeys_of(t0), bank=6)
            for tt in range(4):
                ti = qb * 4 + tt
                tsl = slice(tt * 128, (tt + 1) * 128)
                for kc in range(8):
                    k.op('pe', lambda e: e.matmul(ps[6][:], lhsT=xnb[:, kc, tsl], rhs=wq[:, kc, :], start=(kc == 0), stop=(kc == 7)),
                         reads=['xnb', 'wq'], writes=['ps6'], signal=(kc == 7))
                drain(qk_norm_rope(k, C, T, ps[6][:], 'ps6', 8, C.rp[:, l, 0:64], 0.125, ti, qr[:], 'qr', 'at'))
                for pr in range(4):
                    k.op('pe', lambda e: e.transpose(out=ps[7][:, pr * 128:(pr + 1) * 128], in_=qr[:, pr * 128:(pr + 1) * 128], identity=C.ident_f[:]),
                         reads=['qr'], writes=['ps7'])
                k.op('act', lambda e: e.activation(out=qTb[:, :, tsl], in_=ps[7][:].rearrange("p (a t) -> p a t", t=128), func=AF.Copy),
                     reads=['ps7'], writes=['qTb'])
            for pr in range(4):
                kvh = pr // 2
                ab = 4 + 2 * (pr % 2)

                def scores(kc):
                    sb_ = (kc % 2) * 2
                    ksl = slice(kc * 128, (kc + 1) * 128)
                    for hf in range(2):
                        k.op('pe', lambda e: e.matmul(ps[sb_ + hf][:], lhsT=kTd[hf * 64:(hf + 1) * 64, kvh, ksl], rhs=qTb[hf * 64:(hf + 1) * 64, pr, :],
                                                      start=True, stop=True),
                             reads=['kTd', 'qTb'], writes=['ps%d' % (sb_ + hf)])

                scores(0)
                for kc in range(64):
                    sb_ = (kc % 2) * 2
                    if EXP2:
                        k.op('act', lambda e: e.activation(out=PT[:, kc % 2].rearrange("p h n -> p (h n)"), in_=C.psd[kc % 2][:], func=AF.Exp),
                             reads=['ps%d' % sb_, 'ps%d' % (sb_ + 1)], writes=[('PT', kc % 2, 0), ('PT', kc % 2, 1)])
                    else:
                        for hf in range(2):
                            k.op('act', lambda e: e.activation(out=PT[:, kc % 2, hf, :], in_=ps[sb_ + hf][:], func=AF.Exp),
                                 reads=['ps%d' % (sb_ + hf)], writes=[('PT', kc % 2, hf)])
                    if kc + 1 < 64:
                        scores(kc + 1)
                    lhs = Vo[:, kc, kvh, :]
                    for hf in range(2):
                        k.op('pe', lambda e: e.matmul(ps[ab + hf][:], lhsT=lhs, rhs=PT[:, kc % 2, hf, :], start=(kc == 0), stop=(kc == 63)),
                             reads=['Vo', 'Vo1', ('PT', kc % 2, hf)], writes=['ps%d' % (ab + hf)], signal=True)
                pA_, pB_ = ps[ab], ps[ab + 1]
                kA, kB = 'ps%d' % ab, 'ps%d' % (ab + 1)
                k.op('dve', lambda e: e.reciprocal(out=rden[0:64, :], in_=pA_[64:128, :]), reads=[kA], writes=['qr'])
                k.op('dve', lambda e: e.tensor_tensor(out=yT_at[0:64, pr, t0:t0 + 512], in0=pA_[0:64, :], in1=rden[0:64, :], op=ALU.mult),
                     reads=[kA, 'qr'], writes=[('yat', qb)])
                k.op('dve', lambda e: e.reciprocal(out=rden[64:128, :], in_=pB_[64:128, :]), reads=[kB], writes=['qr'])
                k.op('act', lambda e: e.activation(out=numB[64:128, :], in_=pB_[0:64, :], func=AF.Copy), reads=[kB], writes=['rstd'])
                k.op('dve', lambda e: e.tensor_tensor(out=yT_at[64:128, pr, t0:t0 + 512], in0=numB[64:128, :], in1=rden[64:128, :], op=ALU.mult),
                     reads=['rstd', 'qr'], writes=[('yat', qb)])
        k.barrier()


def p2_wout(k, nc, C, xT, W, l, yT_at, yT_cm):
    ps = C.ps
    with ExitStack() as es:
        wo = sbuf(nc, es, "wo_w", [128, 8, DM], BF16)
        k.dma('pool', wo[:], W.w_out[l], writes=['wo'])
        for tb in range(4):
            t0 = tb * 512
            for dc in range(8):
                bank = dc % 4
                for kc in range(8):
                    rhs = yT_at[:, kc, t0:t0 + 512] if kc < 4 else yT_cm[:, kc - 4, t0:t0 + 512]
                    k.op('pe', lambda e: e.matmul(ps[bank][:], lhsT=wo[:, kc, dc * 128:(dc + 1) * 128], rhs=rhs, start=(kc == 0), stop=(kc == 7)),
                         reads=['wo'], writes=['ps%d' % bank], signal=(kc == 7))
                xk = ('xT', tb)
                k.op('dve', lambda e: e.tensor_tensor(out=xT[:, dc, t0:t0 + 512], in0=ps[bank][:], in1=xT[:, dc, t0:t0 + 512], op=ALU.add),
                     reads=['ps%d' % bank, xk], writes=[xk])
        k.barrier()


def p2_xattn(k, nc, C, xT, W, l, mem_in):
    ps = C.ps
    with ExitStack() as es:
        wq = sbuf(nc, es, "xa_wq", [128, 8, DM], BF16)
        wo = sbuf(nc, es, "xa_wo", [128, 8, DM], BF16)
        kxT = sbuf(nc, es, "xa_kxT", [128, 8, N_MEM], BF16)
        vx = sbuf(nc, es, "xa_vx", [128, 2, DM], BF16)
        k.dma('pool', wq[:], W.xwq[l], writes=['wq'])
        k.dma('pool', wo[:], W.xwo[l], writes=['wo'])
        with ExitStack() as esp:
            memT = sbuf(nc, esp, "xa_memT", [128, 8, N_MEM], F32)
            wkv = sbuf(nc, esp, "xa_wkv", [128, 2, 8, 512], BF16)
            mnT = sbuf(nc, esp, "xa_mnT", [128, 8, N_MEM], BF16)
            msq = sbuf(nc, esp, "xa_msq", [128, 2, N_MEM], BF16)
            kraw = sbuf(nc, esp, "xa_kraw", [128, 8, N_MEM], F32)
            ksq = sbuf(nc, esp, "xa_ksq", [128, 2, N_MEM], BF16)
            krs = sbuf(nc, esp, "xa_krs", [128, 2, N_MEM], F32)
            k.dma('sp', memT[:], mem_in, writes=['memT'])
            gm = C.pp[:, l, 24:32]
            for kc in range(8):
                j = kc % 2
                k.op('act', lambda e: e.activation(out=msq[:, j, :], in_=memT[:, kc, :], func=AF.Square), reads=['memT'], writes=[('msq', j)])
                k.op('pe', lambda e: e.matmul(ps[7][:, 0:N_MEM], lhsT=C.ones_bf[:], rhs=msq[:, j, :], start=(kc == 0), stop=(kc == 7)),
                     reads=[('msq', j)], writes=['ps7'])
            k.op('act', lambda e: e.activation(out=krs[:, 0, :], in_=ps[7][:, 0:N_MEM], func=AF.Sqrt, scale=1.0 / DM, bias=C.eps_col[:, 0:1]), reads=['ps7'], writes=[('krs', 0)])
            k.op('dve', lambda e: e.reciprocal(out=krs[:, 0, :], in_=krs[:, 0, :]), reads=[('krs', 0)], writes=[('krs', 0)])
            for kc in range(8):
                k.op('dve', lambda e: e.scalar_tensor_tensor(out=mnT[:, kc, :], in0=memT[:, kc, :], scalar=gm[:, kc:kc + 1], in1=krs[:, 0, :], op0=ALU.mult, op1=ALU.mult),
                     reads=['memT', ('krs', 0)], writes=['mnT'])
            for g4 in range(4):
                wb = g4 % 2
                k.dma('pool', wkv[:, wb], W.xwkv[l][:, :, g4 * 512:(g4 + 1) * 512], writes=[('xwkv', wb)])
                if g4 < 2:
                    for c4 in range(4):
                        cc = g4 * 4 + c4
                        for kc in range(8):
                            k.op('pe', lambda e: e.matmul(ps[c4][:, 0:N_MEM], lhsT=wkv[:, wb, kc, c4 * 128:(c4 + 1) * 128], rhs=mnT[:, kc, :],
                                                          start=(kc == 0), stop=(kc == 7)),
                                 reads=[('xwkv', wb), 'mnT'], writes=['ps%d' % c4], signal=(kc == 7))
                        k.op('act', lambda e: e.activation(out=kraw[:, cc, :], in_=ps[c4][:, 0:N_MEM], func=AF.Copy), reads=['ps%d' % c4], writes=[('kraw', cc)])
                else:
                    for mt in range(2):
                        for kc in range(8):
                            k.op('pe', lambda e: e.matmul(ps[4 + mt][:], lhsT=mnT[:, kc, mt * 128:(mt + 1) * 128], rhs=wkv[:, wb, kc, :],
                                                          start=(kc == 0), stop=(kc == 7)),
                                 reads=[('xwkv', wb), 'mnT'], writes=['ps%d' % (4 + mt)], signal=(kc == 7))
                        k.op('act', lambda e: e.activation(out=vx[:, mt, (g4 - 2) * 512:(g4 - 1) * 512], in_=ps[4 + mt][:], func=AF.Copy),
                             reads=['ps%d' % (4 + mt)], writes=['vx'])
            gk = C.pp[:, l, 42:44]
            for h in range(4):
                hb = h % 2
                for dcc in range(2):
                    cc = h * 2 + dcc
                    k.op('act', lambda e: e.activation(out=ksq[:, dcc, :], in_=kraw[:, cc, :], func=AF.Square), reads=[('kraw', cc)], writes=[('ksq', dcc)])
                    k.op('pe', lambda e: e.matmul(ps[6 + hb][:, 0:N_MEM], lhsT=C.ones_bf[:], rhs=ksq[:, dcc, :], start=(dcc == 0), stop=(dcc == 1)),
                         reads=[('ksq', dcc)], writes=['ps%d' % (6 + hb)])
                k.op('act', lambda e: e.activation(out=krs[:, hb, :], in_=ps[6 + hb][:, 0:N_MEM], func=AF.Sqrt, scale=1.0 / 256, bias=C.eps_col[:, 0:1]),
                     reads=['ps%d' % (6 + hb)], writes=[('krs', hb)])
                k.op('dve', lambda e: e.reciprocal(out=krs[:, hb, :], in_=krs[:, hb, :]), reads=[('krs', hb)], writes=[('krs', hb)])
                for dcc in range(2):
                    cc = h * 2 + dcc
                    k.op('dve', lambda e: e.scalar_tensor_tensor(out=kxT[:, cc, :], in0=kraw[:, cc, :], scalar=gk[:, dcc:dcc + 1], in1=krs[:, hb, :],
                                                                 op0=ALU.mult, op1=ALU.mult),
                         reads=[('kraw', cc), ('krs', hb)], writes=['kxT'])
            k.barrier()
        xnb = sbuf(nc, es, "xa_xnb", [128, 8, 512], BF16)
        qraw = sbuf(nc, es, "xa_qraw", [128, 8, 512], F32)
        qsq = sbuf(nc, es, "xa_qsq", [128, 8, 512], BF16)
        qrs = sbuf(nc, es, "xa_qrs", [128, 4, 512], F32)
        qT = sbuf(nc, es, "xa_qT", [128, 8, 512], BF16)
        PT = sbuf(nc, es, "xa_PT", [128, 4, 2, 512], BF16)
        rden = sbuf(nc, es, "xa_rden", [128, 4, 512], F32)
        oT = sbuf(nc, es, "xa_oT", [128, 8, 512], BF16)
        gx = C.pp[:, l, 16:24]
        gq = C.pp[:, l, 40:42]
        rr = [0]

        def nb():
            b = rr[0] % 7
            rr[0] += 1
            return b

        for tb in range(4):
            t0 = tb * 512
            rms_block(k, C, xT, t0, gx, lambda kc: xnb[:, kc, :], 'xnb', xkeys_of(t0), bank=7)
            for cc in range(8):
                b = nb()
                for kc in range(8):
                    k.op('pe', lambda e: e.matmul(ps[b][:], lhsT=wq[:, kc, cc * 128:(cc + 1) * 128], rhs=xnb[:, kc, :], start=(kc == 0), stop=(kc == 7)),
                         reads=['wq', 'xnb'], writes=['ps%d' % b], signal=(kc == 7))
                k.op('act', lambda e: e.activation(out=qraw[:, cc, :], in_=ps[b][:], func=AF.Copy), reads=['ps%d' % b], writes=[('qraw', cc)])
                k.op('pool', lambda e: e.tensor_tensor(out=qsq[:, cc, :], in0=qraw[:, cc, :], in1=qraw[:, cc, :], op=ALU.mult),
                     reads=[('qraw', cc)], writes=[('qsq', cc)])
            for h in range(4):
                b = nb()
                for dcc in range(2):
                    k.op('pe', lambda e: e.matmul(ps[b][:], lhsT=C.ones_bf[:], rhs=qsq[:, h * 2 + dcc, :], start=(dcc == 0), stop=(dcc == 1)),
                         reads=[('qsq', h * 2 + dcc)], writes=['ps%d' % b], signal=(dcc == 1))
                k.op('act', lambda e: e.activation(out=qrs[:, h, :], in_=ps[b][:], func=AF.Sqrt, scale=1.0 / 256, bias=C.eps_col[:, 0:1]),
                     reads=['ps%d' % b], writes=[('qrs', h)])
                k.op('dve', lambda e: e.reciprocal(out=qrs[:, h, :], in_=qrs[:, h, :]), reads=[('qrs', h)], writes=[('qrs', h)])
                k.op('pool', lambda e: e.tensor_scalar(out=qrs[:, h, :], in0=qrs[:, h, :], scalar1=1.0 / 16, scalar2=None, op0=ALU.mult),
                     reads=[('qrs', h)], writes=[('qrs', h)])
                for dcc in range(2):
                    cc = h * 2 + dcc
                    k.op('dve', lambda e: e.scalar_tensor_tensor(out=qT[:, cc, :], in0=qraw[:, cc, :], scalar=gq[:, dcc:dcc + 1], in1=qrs[:, h, :],
                                                                 op0=ALU.mult, op1=ALU.mult),
                         reads=[('qraw', cc), ('qrs', h)], writes=[('qT', cc)])
            for h in range(4):
                for mt in range(2):
                    b = nb()
                    for dcc in range(2):
                        cc = h * 2 + dcc
                        k.op('pe', lambda e: e.matmul(ps[b][:], lhsT=kxT[:, cc, mt * 128:(mt + 1) * 128], rhs=qT[:, cc, :], start=(dcc == 0), stop=(dcc == 1)),
                             reads=['kxT', ('qT', cc)], writes=['ps%d' % b], signal=(dcc == 1))
                    k.op('act', lambda e: e.activation(out=PT[:, h, mt, :], in_=ps[b][:], func=AF.Exp), reads=['ps%d' % b], writes=[('PT', h, mt)])
            for h in range(4):
                b = nb()
                for mt in range(2):
                    k.op('pe', lambda e: e.matmul(ps[b][:], lhsT=C.ones_bf[:], rhs=PT[:, h, mt, :], start=(mt == 0), stop=(mt == 1)),
                         reads=[('PT', h, mt)], writes=['ps%d' % b], signal=(mt == 1))
                k.op('dve', lambda e: e.reciprocal(out=rden[:, h, :], in_=ps[b][:]), reads=['ps%d' % b], writes=[('rden', h)])
            for cc in range(8):
                h = cc // 2
                b = nb()
                for mt in range(2):
                    k.op('pe', lambda e: e.matmul(ps[b][:], lhsT=vx[:, mt, cc * 128:(cc + 1) * 128], rhs=PT[:, h, mt, :], start=(mt == 0), stop=(mt == 1)),
                         reads=['vx', ('PT', h, mt)], writes=['ps%d' % b], signal=(mt == 1))
                k.op('dve', lambda e: e.tensor_tensor(out=oT[:, cc, :], in0=ps[b][:], in1=rden[:, h, :], op=ALU.mult),
                     reads=['ps%d' % b, ('rden', h)], writes=[('oT', cc)])
            for dc in range(8):
                b = nb()
                for kc in range(8):
                    k.op('pe', lambda e: e.matmul(ps[b][:], lhsT=wo[:, kc, dc * 128:(dc + 1) * 128], rhs=oT[:, kc, :], start=(kc == 0), stop=(kc == 7)),
                         reads=['wo', ('oT', kc)], writes=['ps%d' % b], signal=(kc == 7))
                xk = ('xT', tb)
                k.op('dve', lambda e: e.tensor_tensor(out=xT[:, dc, t0:t0 + 512], in0=ps[b][:], in1=xT[:, dc, t0:t0 + 512], op=ALU.add),
                     reads=['ps%d' % b, xk], writes=[xk])
        k.barrier()


WSPEC = [
    ("f1_w13", [NFC, 128, 2, 8, 128]), ("f1_w2", [8, 128, NFC, 128]),
    ("f2_w13", [NFC, 128, 2, 8, 128]), ("f2_w2", [8, 128, NFC, 128]),
    ("w_q", [128, 8, 512]), ("w_kv", [128, 8, 256]), ("w_conv", [128, 8, 512]), ("w_m", [128, 8, 1040]),
    ("w_out", [128, 8, DM]), ("xwq", [128, 8, DM]), ("xwkv", [128, 8, 2048]), ("xwo", [128, 8, DM]),
]
NPP = 112
NRP = 400


def build(prog, debug=()):
    nc = bass.Bass("TRN2", target_bir_lowering=False)

    def din(name, shape, dt=F32):
        return nc.dram_tensor(name, list(shape), dt, kind="ExternalInput").ap()

    def dout(name, shape, dt=F32):
        return nc.dram_tensor(name, list(shape), dt, kind="ExternalOutput").ap()

    class WL:
        def __init__(self, name, shp):
            self.name, self.shp, self.aps = name, shp, {}

        def __getitem__(self, l):
            if l not in self.aps:
                self.aps[l] = din("%s_%d" % (self.name, l), self.shp)
            return self.aps[l]

    W = Ctx()
    for name, shp in WSPEC:
        setattr(W, name, WL(name, shp))
    FUSED = (prog == 'F')
    if FUSED:
        x_in4 = din("xT_in", [4, 128, 8, NT])
        x_out4 = dout("xT_out", [4, 128, 8, NT])
        cos_in4 = din("cos", [4, 128, NTILE, 32])
        sin_in4 = din("sin", [4, 128, NTILE, 32])
        sel_in4 = din("sel", [4, 128, 16])
        xq = nc.dram_tensor("xq_scr", [4, 128, 8, NT], F32).ap()
        x_in = x_out = cos_in = sin_in = sel_in = None
    else:
        x_in = din("xT_in", [128, 8, NT])
        x_out = dout("xT_out", [128, 8, NT])
        cos_in = din("cos", [128, NTILE, 32])
        sin_in = din("sin", [128, NTILE, 32])
        sel_in = din("sel", [128, 16])
    mem_in = din("memT", [128, 8, N_MEM])
    pp_in = din("pp", [128, 2, NPP])
    rp_in = din("rp", [128, 2, NRP])
    cst_in = din("cst", [128, 5, 128])
    XO = Ctx()
    XA = Ctx()
    if prog in ('A', 'B'):
        XO.kx = dout("kx", [128, NT], BF16)
        XO.vx = dout("vx", [NT, 128], BF16)
        XO.hx = dout("hx", [128, 2, 30])
        XO.ax = dout("ax", [64, 8, 66])
    if FUSED:
        XA.kx_all = nc.dram_tensor("kx_scr", [4, 128, NT], BF16).ap()
        XA.vx_all = nc.dram_tensor("vx_scr", [4, NT, 128], BF16).ap()
        XA.hx_all = nc.dram_tensor("hx_scr", [4, 128, 2, 30], F32).ap()
        XA.ax_all = nc.dram_tensor("ax_scr", [4, 64, 8, 66], F32).ap()
    if prog in ('B', 'C'):
        XA.kx_all = din("kx_all", [4, 128, NT], BF16)
        XA.vx_all = din("vx_all", [4, NT, 128], BF16)
        XA.hx_all = din("hx_all", [4, 128, 2, 30])
        XA.ax_all = din("ax_all", [4, 64, 8, 66])

    with ExitStack() as es:
        k = K(nc, es)
        C = Ctx()
        C.psd = [es.enter_context(nc.psum_tensor("psd%d" % i, [128, 1024], F32)) for i in range(4)]
        C.ps = [C.psd[i // 2][:, (i % 2) * 512:(i % 2 + 1) * 512] for i in range(8)]
        xT = sbuf(nc, es, "xT", [128, 8, NT], F32)
        C.pp = sbuf(nc, es, "pp_s", [128, 2, NPP], F32)
        C.rp = sbuf(nc, es, "rp_s", [128, 2, NRP], F32)
        C.cos = sbuf(nc, es, "cos_s", [128, NTILE, 32], F32)
        C.sin = sbuf(nc, es, "sin_s", [128, NTILE, 32], F32)
        C.sel = sbuf(nc, es, "sel_s", [128, 16], F32)
        cst = sbuf(nc, es, "cst_s", [128, 3, 128], F32)
        C.ident_f = cst[:, 0, :]
        C.tri_f = cst[:, 1, :]
        C.tri_b = cst[:, 2, :]
        C.ones_f = sbuf(nc, es, "ones_f", [128, 128], F32)
        C.ones_bf = sbuf(nc, es, "ones_bf", [128, 128], BF16)
        C.ident_bf = sbuf(nc, es, "ident_bf", [128, 128], BF16)
        C.maskneg = sbuf(nc, es, "maskneg", [128, 2, 128], BF16)
        C.eps_col = sbuf(nc, es, "eps_col", [128, 1], F32)
        C.one_col = sbuf(nc, es, "one_col", [128, 1], F32)
        C.sq = sbuf(nc, es, "c_sq", [128, 2, 512], BF16)
        C.rstd = sbuf(nc, es, "c_rstd", [128, 512], F32)
        C.gateb_bc = None

        def load_x(src):
            for i in range(4):
                k.dma('sp', xT[:, :, i * 512:(i + 1) * 512], src[:, :, i * 512:(i + 1) * 512], writes=[('xT', i)])

        def store_x(dst):
            for i in range(4):
                k.dma('sp', dst[:, :, i * 512:(i + 1) * 512], xT[:, :, i * 512:(i + 1) * 512], reads=[('xT', i)])

        def load_quarter_consts(q):
            k.dma('sp', C.cos[:], cos_in4[q], writes=['c'])
            k.dma('sp', C.sin[:], sin_in4[q], writes=['c'])
            k.dma('sp', C.sel[:], sel_in4[q], writes=['c'])
            k.barrier()

        if not FUSED:
            load_x(x_in)
            k.dma('sp', C.cos[:], cos_in, writes=['c'])
            k.dma('sp', C.sin[:], sin_in, writes=['c'])
            k.dma('sp', C.sel[:], sel_in, writes=['c'])
        k.dma('sp', C.pp[:], pp_in, writes=['c'])
        k.dma('sp', C.rp[:], rp_in, writes=['c'])
        k.dma('sp', cst[:], cst_in[:, 0:3, :], writes=['c'])
        k.op('dve', lambda e: e.memset(C.ones_f[:], 1.0), writes=['c1'])
        k.op('dve', lambda e: e.memset(C.ones_bf[:], 1.0), writes=['c2'])
        k.op('dve', lambda e: e.memset(C.eps_col[:], EPS), writes=['c3'])
        k.op('dve', lambda e: e.memset(C.one_col[:], 1.0), writes=['c4'])
        k.op('dve', lambda e: e.tensor_copy(out=C.ident_bf[:], in_=cst[:, 0, :]), reads=['c'], writes=['c5'])
        with ExitStack() as es0:
            mtmp = sbuf(nc, es0, "mtmp", [128, 2, 128], F32)
            k.dma('sp', mtmp[:], cst_in[:, 3:5, :], writes=['mtmp'])
            k.op('dve', lambda e: e.tensor_copy(out=C.maskneg[:], in_=mtmp[:]), reads=['mtmp'], writes=['c6'])
            k.barrier()

        def P1(l):
            C.gateb_bc = C.rp[:, l, 128:144]
            if 'ffn' not in SKIP:
                ffn(k, nc, C, xT, C.pp[:, l, 0:8], W.f1_w13[l], W.f1_w2[l])
            if 'x1' in debug:
                k.dump("x1_%d" % l, xT[:], [128, 8, NT], F32, [('xT', i) for i in range(4)])
            if 'ffn' in SKIP:
                pass
            if 'mixer' not in SKIP:
                p1_mixer(k, nc, C, xT, W, l, XO)

        def P2(l):
            C.gateb_bc = C.rp[:, l, 128:144]
            with ExitStack() as es2:
                yT_cm = sbuf(nc, es2, "yT_cm", [128, 4, NT], BF16)
                if 'conv' not in SKIP:
                    p2_conv(k, nc, C, xT, W, l, XA, yT_cm)
                if 'mlstm' not in SKIP:
                    p2_mlstm(k, nc, C, xT, W, l, XA, yT_cm)
                with ExitStack() as es3:
                    yT_at = sbuf(nc, es3, "yT_at", [128, 4, NT], BF16)
                    if 'attn' not in SKIP:
                        p2_attn(k, nc, C, xT, W, l, XA, yT_at)
                    if 'y' in debug:
                        k.dump("yat_%d" % l, yT_at[:], [128, 4, NT], BF16, [])
                        k.dump("ycm_%d" % l, yT_cm[:], [128, 4, NT], BF16, [])
                    if 'wout' not in SKIP:
                        p2_wout(k, nc, C, xT, W, l, yT_at, yT_cm)
            if 'x2' in debug:
                k.dump("x2_%d" % l, xT[:], [128, 8, NT], F32, [('xT', i) for i in range(4)])
            if 'xattn' not in SKIP:
                p2_xattn(k, nc, C, xT, W, l, mem_in)
            if 'x3' in debug:
                k.dump("x3_%d" % l, xT[:], [128, 8, NT], F32, [('xT', i) for i in range(4)])
            if 'ffn2' not in SKIP:
                ffn(k, nc, C, xT, C.pp[:, l, 32:40], W.f2_w13[l], W.f2_w2[l])

        if prog == 'A':
            P1(0)
        elif prog == 'B':
            P2(0)
            P1(1)
        elif prog == 'C':
            P2(1)
        else:
            for l in range(2):
                for q in range(4):
                    load_quarter_consts(q)
                    load_x(x_in4[q] if l == 0 else xq[q])
                    XO.kx, XO.vx, XO.hx, XO.ax = XA.kx_all[q], XA.vx_all[q], XA.hx_all[q], XA.ax_all[q]
                    P1(l)
                    store_x(xq[q])
                    k.barrier()
                for q in range(4):
                    load_quarter_consts(q)
                    load_x(xq[q])
                    P2(l)
                    store_x(xq[q] if l == 0 else x_out4[q])
                    k.barrier()
        if not FUSED:
            store_x(x_out)
        k.barrier()
        stats = (k.nops, k.nwaits, list(k.dumps))
    used = []
    for name, _ in WSPEC:
        used += ['%s_%d' % (name, l) for l in getattr(W, name).aps]
    return nc, used, stats


def _perm64():
    return np.concatenate([np.arange(0, 64, 2), np.arange(1, 64, 2)])


def prep_weights(inp):
    f = lambda a: np.ascontiguousarray(a, dtype=np.float32)
    L = 2
    out = {}

    def kmaj(w):
        return f(w.reshape(L, 8, 128, w.shape[-1]).transpose(0, 2, 1, 3))

    for pfx, a, b in (("f1", "ffn1_w13", "ffn1_w2"), ("f2", "ffn2_w13", "ffn2_w2")):
        w13 = inp[a]
        out[pfx + "_w13"] = f(w13.reshape(L, 8, 128, 2, NFC, 128).transpose(0, 4, 2, 3, 1, 5))
        w2 = inp[b]
        out[pfx + "_w2"] = f(w2.reshape(L, NFC, 128, 8, 128).transpose(0, 3, 2, 1, 4))
    w_in = inp["w_in"]
    p64 = _perm64()
    qcols = np.concatenate([h * 64 + p64 for h in range(8)])
    kcols = 512 + np.concatenate([h * 64 + p64 for h in range(2)])
    out["w_q"] = kmaj(w_in[:, :, qcols])
    out["w_kv"] = kmaj(np.concatenate([w_in[:, :, kcols], w_in[:, :, 640:768]], axis=-1))
    out["w_conv"] = kmaj(w_in[:, :, 768:1280])
    out["w_m"] = kmaj(w_in[:, :, 1280:2320])
    out["w_out"] = kmaj(inp["w_out"])
    out["xwq"] = kmaj(inp["xattn_wq"])
    out["xwkv"] = kmaj(inp["xattn_wkv"])
    out["xwo"] = kmaj(inp["xattn_wo"])
    pp = np.zeros((128, L, NPP), np.float32)
    rp = np.zeros((128, L, NRP), np.float32)
    for l in range(L):
        col = lambda v: v.reshape(-1, 128).T
        pp[:, l, 0:8] = col(inp["ffn1_norm"][l])
        pp[:, l, 8:16] = col(inp["mix_norm"][l])
        pp[:, l, 16:24] = col(inp["xattn_norm"][l])
        pp[:, l, 24:32] = col(inp["mem_norm"][l])
        pp[:, l, 32:40] = col(inp["ffn2_norm"][l])
        pp[:, l, 40:42] = col(inp["xattn_q_norm"][l])
        pp[:, l, 42:44] = col(inp["xattn_k_norm"][l])
        cp = np.zeros((128, 2, 34), np.float32)
        cp[:, :, 0:31] = inp["conv_dw_w"][l].T.reshape(2, 128, 31).transpose(1, 0, 2)
        cp[:, :, 31] = col(inp["conv_dw_b"][l])
        cp[:, :, 32] = col(inp["conv_ln_g"][l])
        cp[:, :, 33] = col(inp["conv_ln_b"][l])
        pp[:, l, 44:112] = cp.reshape(128, 68)
        rp[:, l, 0:64] = inp["attn_q_norm"][l][p64][None, :]
        rp[:, l, 64:128] = inp["attn_k_norm"][l][p64][None, :]
        rp[:, l, 128:144] = inp["mlstm_gate_b"][l].reshape(16)[None, :]
        rp[:, l, 144:400] = inp["mlstm_out_norm"][l].reshape(256)[None, :]
    out["pp"] = pp
    out["rp"] = rp
    cst = np.zeros((128, 5, 128), np.float32)
    s = np.arange(128)[:, None]
    t = np.arange(128)[None, :]
    cst[:, 0] = (s == t)
    cst[:, 1] = (s <= t)
    cst[:, 2] = (s >= t)
    cst[:, 3] = np.where(s <= t, 0.0, NEG)
    cst[:, 4] = np.where(s >= t, 0.0, NEG)
    out["cst"] = cst
    return out


def prep_core(inp, c):
    b, p = c // 4, c % 4
    d = {}
    xs = inp["x"][b, p * NT:(p + 1) * NT]
    d["xT_in"] = np.ascontiguousarray(xs.reshape(NT, 8, 128).transpose(2, 1, 0), dtype=np.float32)
    d["memT"] = np.ascontiguousarray(inp["mem"][b].reshape(N_MEM, 8, 128).transpose(2, 1, 0), dtype=np.float32)
    pos = p * NT + np.arange(NT)
    row = (pos // GRID_W).astype(np.float32)
    colp = (pos % GRID_W).astype(np.float32)
    inv_freq = (np.float32(10000.0) ** (-np.arange(16, dtype=np.float32) / np.float32(16))).astype(np.float32)
    ang = np.concatenate([row[:, None] * inv_freq[None, :], colp[:, None] * inv_freq[None, :]], axis=-1).astype(np.float32)
    d["cos"] = np.ascontiguousarray(np.cos(ang).astype(np.float32).reshape(NTILE, 128, 32).transpose(1, 0, 2))
    d["sin"] = np.ascontiguousarray(np.sin(ang).astype(np.float32).reshape(NTILE, 128, 32).transpose(1, 0, 2))
    sel = np.zeros((128, 16), np.float32)
    for i in range(4):
        sel[:, i] = 1.0 if i < p else 0.0
        sel[:, 4 + i] = 1.0 if i > p else 0.0
        sel[:, 8 + i] = 1.0 if i == p - 1 else 0.0
        sel[:, 12 + i] = 1.0 if i == p + 1 else 0.0
    d["sel"] = sel
    return d


def run_prog(prog, inp_w, cores, extra, debug=()):
    nc, used, stats = build(prog, debug)
    in_maps = []
    for c in range(NCORES):
        m = {kk: inp_w[kk] for kk in ("pp", "rp", "cst")}
        for nm in used:
            base, l = nm.rsplit("_", 1)
            m[nm] = inp_w[base][int(l)]
        m.update(cores[c])
        m.update(extra[c])
        in_maps.append(m)
    res = run_bass_kernel_spmd(nc, in_maps, core_ids=list(range(NCORES)))
    return res.results


def gather_exchange(results):
    ex = []
    for c in range(NCORES):
        g0 = (c // 4) * 4
        ex.append({
            "kx_all": np.stack([results[g0 + i]["kx"] for i in range(4)]),
            "vx_all": np.stack([results[g0 + i]["vx"] for i in range(4)]),
            "hx_all": np.stack([results[g0 + i]["hx"] for i in range(4)]),
            "ax_all": np.stack([results[g0 + i]["ax"] for i in range(4)]),
        })
    return ex


def prep_core_fused(inp, b):
    qs = [prep_core(inp, b * 4 + p) for p in range(4)]
    d = {"memT": qs[0]["memT"]}
    for kk in ("xT_in", "cos", "sin", "sel"):
        d[kk] = np.stack([q[kk] for q in qs])
    return d


def kernel_unfused(**inputs):
    inp = {kk: np.asarray(v) for kk, v in inputs.items()}
    w = prep_weights(inp)
    cores = [prep_core(inp, c) for c in range(NCORES)]
    rA = run_prog('A', w, cores, [{} for _ in range(NCORES)])
    ex = gather_exchange(rA)
    for c in range(NCORES):
        ex[c]["xT_in"] = rA[c]["xT_out"]
    rB = run_prog('B', w, cores, ex)
    ex = gather_exchange(rB)
    for c in range(NCORES):
        ex[c]["xT_in"] = rB[c]["xT_out"]
    rC = run_prog('C', w, cores, ex)
    out = np.zeros((2, SEQ, DM), np.float32)
    for c in range(NCORES):
        b, p = c // 4, c % 4
        out[b, p * NT:(p + 1) * NT] = rC[c]["xT_out"].transpose(2, 1, 0).reshape(NT, DM)
    return out


def kernel_fused(**inputs):
    inp = {kk: np.asarray(v) for kk, v in inputs.items()}
    w = prep_weights(inp)
    pb = [prep_core_fused(inp, b) for b in range(2)]
    cores = [pb[c // 4] for c in range(NCORES)]
    r = run_prog('F', w, cores, [{} for _ in range(NCORES)])
    out = np.zeros((2, SEQ, DM), np.float32)
    for b in range(2):
        xo = r[b * 4]["xT_out"]
        for p in range(4):
            out[b, p * NT:(p + 1) * NT] = xo[p].transpose(2, 1, 0).reshape(NT, DM)
    return out


FUSED_DEFAULT = True


def kernel(**inputs):
    return kernel_fused(**inputs) if FUSED_DEFAULT else kernel_unfused(**inputs)
```

```python
import numpy as np
import ml_dtypes
from contextlib import ExitStack
import concourse.bass as bass
import concourse.mybir as mybir
from concourse.bass_utils import run_bass_kernel_spmd

F32 = mybir.dt.float32
BF16 = mybir.dt.bfloat16
AF = mybir.ActivationFunctionType
ALU = mybir.AluOpType
AX = mybir.AxisListType

NCORES = 8
NT = 2048
NTILE = 16
DM = 1024
DFF = 2816
NFC = 22
EPS = 1e-6
SEQ = 8192
GRID_W = 64
N_MEM = 256
CONV_W = 31
PAD = 15
NEG = -30000.0
SKIP = set()
PASSA_INTERLEAVE = True
CONV_PE = False
EXP2 = True


class Eng:
    def __init__(self, name, h, sem):
        self.name = name
        self.h = h
        self.sem = sem
        self.n = 0
        self.pending = False
        self.clock = {}


class K:
    NDS = 24

    def __init__(self, nc, es):
        self.nc = nc
        self.eng = {}
        for name, h in (("pe", nc.tensor), ("act", nc.scalar), ("dve", nc.vector),
                        ("pool", nc.gpsimd), ("sp", nc.sync)):
            sem = es.enter_context(nc.semaphore("s_" + name))
            self.eng[name] = Eng(name, h, sem)
        self.dsem = [es.enter_context(nc.semaphore("d%d" % i)) for i in range(self.NDS)]
        self.dcount = [0] * self.NDS
        self.dnext = 0
        self.dnext_p = 0
        self.last_w = {}
        self.readers = {}
        self.nwaits = 0
        self.nops = 0
        self.dumps = []

    def _merge(self, eng, clk):
        c = eng.clock
        for kk, v in clk.items():
            if c.get(kk, 0) < v:
                c[kk] = v

    def _wait(self, eng, t):
        if t[0] == 'E':
            _, en, n, clk = t
            if en == eng.name and en == 'pe':
                return
            if eng.clock.get(en, 0) >= n:
                return
            src = self.eng[en]
            assert src.n >= n, "waiting on unsignalled %s op (n=%d have %d)" % (en, n, src.n)
            eng.h.wait_ge(src.sem, n)
            self.nwaits += 1
            eng.clock[en] = n
            self._merge(eng, clk)
        else:
            _, si, val, clk = t
            key = ('d', si)
            if eng.clock.get(key, 0) >= val:
                return
            eng.h.wait_ge(self.dsem[si], val)
            self.nwaits += 1
            eng.clock[key] = val
            self._merge(eng, clk)

    def _deps(self, eng, reads, writes):
        for r in reads:
            t = self.last_w.get(r)
            if t is not None:
                self._wait(eng, t)
        for w in writes:
            t = self.last_w.get(w)
            if t is not None:
                self._wait(eng, t)
            rd = self.readers.get(w)
            if rd:
                for t in rd.values():
                    self._wait(eng, t)

    def _record(self, t, rkey, reads, writes):
        for r in reads:
            self.readers.setdefault(r, {})[rkey] = t
        for w in writes:
            self.last_w[w] = t
            self.readers[w] = {}

    @staticmethod
    def _is_ps(key):
        return isinstance(key, str) and key.startswith('ps') and key[2:].isdigit()

    def op(self, en, fn, reads=(), writes=(), signal=True):
        psr = [r for r in reads if self._is_ps(r)]
        if psr:
            reads = [r for r in reads if not self._is_ps(r)]
            writes = list(writes) + [r for r in psr if r not in writes]
        eng = self.eng[en]
        self._deps(eng, reads, writes)
        ins = fn(eng.h)
        self.nops += 1
        if signal:
            eng.n += 1
            ins.then_inc(eng.sem, 1)
            eng.pending = False
            t = ('E', en, eng.n, dict(eng.clock))
        else:
            eng.pending = True
            t = ('E', en, eng.n + 1, dict(eng.clock))
        self._record(t, en, reads, writes)
        return t

    def dma(self, qn, out, in_, reads=(), writes=()):
        eng = self.eng[qn]
        self._deps(eng, reads, writes)
        if qn == 'pool':
            si = self.dnext_p % 8
            self.dnext_p += 1
        else:
            si = 8 + self.dnext % (self.NDS - 8)
            self.dnext += 1
        cnt = self.dcount[si]
        if cnt > 0:
            self._wait(eng, ('D', si, 16 * cnt, {}))
        eng.h.dma_start(out=out, in_=in_).then_inc(self.dsem[si], 16)
        self.nops += 1
        self.dcount[si] = cnt + 1
        t = ('D', si, 16 * (cnt + 1), dict(eng.clock))
        self._record(t, ('d', si), reads, writes)
        return t

    def barrier(self):
        ts = []
        for e in self.eng.values():
            assert not e.pending, "pending unsignalled ops on " + e.name
            if e.n > 0:
                ts.append(('E', e.name, e.n, {}))
        for si in range(self.NDS):
            if self.dcount[si] > 0:
                ts.append(('D', si, 16 * self.dcount[si], {}))
        for e in self.eng.values():
            for t in ts:
                self._wait(e, t)
        self.last_w = {}
        self.readers = {}

    def dump(self, name, ap, shape, dtype, reads):
        d = self.nc.dram_tensor("dbg_" + name, list(shape), dtype, kind="ExternalOutput").ap()
        self.dma('sp', d, ap, reads=reads)
        self.dumps.append("dbg_" + name)


class Ctx:
    pass


def drain(g):
    for _ in g:
        pass


def interleave(gens):
    gens = list(gens)
    while gens:
        for g in list(gens):
            try:
                next(g)
            except StopIteration:
                gens.remove(g)


_UNIQ = [0]


def sbuf(nc, es, name, shape, dt):
    _UNIQ[0] += 1
    return es.enter_context(nc.sbuf_tensor("%s_%d" % (name, _UNIQ[0]), list(shape), dt))


def rms_block(k, C, xT, t0, gcol, dst_fn, dst_key, xkeys, bank=7):
    ps, sq, rstd = C.ps, C.sq, C.rstd
    pk = 'ps%d' % bank
    for kc in range(8):
        j = kc % 2
        k.op('act', lambda e: e.activation(out=sq[:, j, :], in_=xT[:, kc, t0:t0 + 512], func=AF.Square),
             reads=xkeys, writes=[('sq', j)])
        k.op('pe', lambda e: e.matmul(ps[bank][:], lhsT=C.ones_bf[:], rhs=sq[:, j, :], start=(kc == 0), stop=(kc == 7)),
             reads=[('sq', j)], writes=[pk])
    k.op('act', lambda e: e.activation(out=rstd[:], in_=ps[bank][:], func=AF.Sqrt, scale=1.0 / DM, bias=C.eps_col[:, 0:1]),
         reads=[pk], writes=['rstd'])
    k.op('dve', lambda e: e.reciprocal(out=rstd[:], in_=rstd[:]), reads=['rstd'], writes=['rstd'])
    for kc in range(8):
        k.op('dve', lambda e: e.scalar_tensor_tensor(out=dst_fn(kc), in0=xT[:, kc, t0:t0 + 512], scalar=gcol[:, kc:kc + 1],
                                                     in1=rstd[:], op0=ALU.mult, op1=ALU.mult),
             reads=list(xkeys) + ['rstd'], writes=[dst_key])


def xkeys_of(t0, n=512):
    return [('xT', i) for i in range(t0 // 512, (t0 + n - 1) // 512 + 1)]


def ffn(k, nc, C, xT, gcol, w13r, w2r):
    with ExitStack() as es:
        xnT = sbuf(nc, es, "f_xnT", [128, 8, 1024], BF16)
        gT = sbuf(nc, es, "f_gT", [128, NFC, 1024], BF16)
        w13s = sbuf(nc, es, "f_w13s", [128, 2, 2, 8, 128], BF16)
        w2s = sbuf(nc, es, "f_w2s", [128, 2, NFC, 128], BF16)
        sa = sbuf(nc, es, "f_sa", [128, 2, 512], F32)
        ps = C.ps
        wc13 = 0
        wc2 = 0
        for half in range(2):
            for tb in range(2):
                t0 = half * 1024 + tb * 512
                rms_block(k, C, xT, t0, gcol, lambda kc: xnT[:, kc, tb * 512:(tb + 1) * 512], ('xnT', tb), xkeys_of(t0))
            for fc in range(NFC):
                wb = wc13 % 2
                wc13 += 1
                k.dma('pool', w13s[:, wb], w13r[fc], writes=[('w13s', wb)])
                bs = (fc % 2) * 4
                for kc in range(8):
                    for ab in range(2):
                        for tb in range(2):
                            bank = bs + ab * 2 + tb
                            k.op('pe', lambda e: e.matmul(ps[bank][:], lhsT=w13s[:, wb, ab, kc, :],
                                                          rhs=xnT[:, kc, tb * 512:(tb + 1) * 512],
                                                          start=(kc == 0), stop=(kc == 7)),
                                 reads=[('w13s', wb), ('xnT', tb)], writes=['ps%d' % bank], signal=(kc == 7))
                for tb in range(2):
                    ba, bb = bs + tb, bs + 2 + tb
                    k.op('act', lambda e: e.activation(out=sa[:, tb, :], in_=ps[ba][:], func=AF.Silu),
                         reads=['ps%d' % ba], writes=[('sa', tb)])
                    k.op('dve', lambda e: e.tensor_tensor(out=gT[:, fc, tb * 512:(tb + 1) * 512], in0=sa[:, tb, :],
                                                          in1=ps[bb][:], op=ALU.mult),
                         reads=[('sa', tb), 'ps%d' % bb], writes=[('gT', fc, tb)])
            for dc in range(8):
                wb = wc2 % 2
                wc2 += 1
                k.dma('pool', w2s[:, wb], w2r[dc], writes=[('w2s', wb)])
                bs = (dc % 4) * 2
                for fc in range(NFC):
                    for tb in range(2):
                        bank = bs + tb
                        k.op('pe', lambda e: e.matmul(ps[bank][:], lhsT=w2s[:, wb, fc, :],
                                                      rhs=gT[:, fc, tb * 512:(tb + 1) * 512],
                                                      start=(fc == 0), stop=(fc == NFC - 1)),
                             reads=[('w2s', wb), ('gT', fc, tb)], writes=['ps%d' % bank], signal=(fc == NFC - 1))
                for tb in range(2):
                    bank = bs + tb
                    t0 = half * 1024 + tb * 512
                    xk = ('xT', t0 // 512)
                    k.op('dve', lambda e: e.scalar_tensor_tensor(out=xT[:, dc, t0:t0 + 512], in0=ps[bank][:], scalar=0.5,
                                                                 in1=xT[:, dc, t0:t0 + 512], op0=ALU.mult, op1=ALU.add),
                         reads=['ps%d' % bank, xk], writes=[xk])
        k.barrier()


def qk_norm_rope(k, C, T, src, srckey, nh, gain_bc, scale, ti, out, outkey, tag):
    w = nh * 64
    sqv, ssq, t1, t2, kn = T.sqv, T.ssq, T.t1, T.t2, T.kn
    kkn = tag + ('sqv' if T.kn is T.sqv else 'kn')
    yield k.op('act', lambda e: e.activation(out=sqv[:, 0:w], in_=src, func=AF.Square), reads=[srckey], writes=[tag + 'sqv'])
    yield k.op('dve', lambda e: e.tensor_reduce(out=ssq[:, 0:nh], in_=sqv[:, 0:w].rearrange("p (h d) -> p h d", d=64),
                                          axis=AX.X, op=ALU.add), reads=[tag + 'sqv'], writes=[tag + 'ssq'])
    yield k.op('act', lambda e: e.activation(out=ssq[:, 0:nh], in_=ssq[:, 0:nh], func=AF.Sqrt, scale=1.0 / 64, bias=C.eps_col[:, 0:1]),
         reads=[tag + 'ssq'], writes=[tag + 'ssq'])
    yield k.op('dve', lambda e: e.reciprocal(out=ssq[:, 0:nh], in_=ssq[:, 0:nh]), reads=[tag + 'ssq'], writes=[tag + 'ssq'])
    knv = kn[:, 0:w].rearrange("p (h d) -> p h d", d=64)
    yield k.op('dve', lambda e: e.tensor_tensor(out=knv, in0=src.rearrange("p (h d) -> p h d", d=64),
                                          in1=ssq[:, 0:nh].unsqueeze(2).to_broadcast([128, nh, 64]), op=ALU.mult),
         reads=[srckey, tag + 'ssq'], writes=[kkn])
    yield k.op('dve', lambda e: e.scalar_tensor_tensor(out=knv, in0=knv, scalar=float(scale),
                                                 in1=gain_bc.unsqueeze(1).to_broadcast([128, nh, 64]),
                                                 op0=ALU.mult, op1=ALU.mult),
         reads=[kkn], writes=[kkn])
    x0 = knv[:, :, 0:32]
    x1 = knv[:, :, 32:64]
    cb = C.cos[:, ti, :].unsqueeze(1).to_broadcast([128, nh, 32])
    sb_ = C.sin[:, ti, :].unsqueeze(1).to_broadcast([128, nh, 32])
    ov = out.rearrange("p (h d) -> p h d", d=64)
    t1v = t1[:, 0:nh * 32].rearrange("p (h d) -> p h d", d=32)
    t2v = t2[:, 0:nh * 32].rearrange("p (h d) -> p h d", d=32)
    yield k.op('pool', lambda e: e.tensor_tensor(out=t1v, in0=x0, in1=cb, op=ALU.mult), reads=[kkn], writes=[tag + 't1'])
    yield k.op('dve', lambda e: e.tensor_tensor(out=t2v, in0=x1, in1=sb_, op=ALU.mult), reads=[kkn], writes=[tag + 't2'])
    yield k.op('dve', lambda e: e.tensor_tensor(out=ov[:, :, 0:32], in0=t1v, in1=t2v, op=ALU.subtract),
         reads=[tag + 't1', tag + 't2'], writes=[outkey])
    yield k.op('pool', lambda e: e.tensor_tensor(out=t1v, in0=x0, in1=sb_, op=ALU.mult), reads=[kkn, outkey], writes=[tag + 't1'])
    yield k.op('dve', lambda e: e.tensor_tensor(out=t2v, in0=x1, in1=cb, op=ALU.mult), reads=[kkn, outkey], writes=[tag + 't2'])
    yield k.op('dve', lambda e: e.tensor_tensor(out=ov[:, :, 32:64], in0=t1v, in1=t2v, op=ALU.add),
         reads=[tag + 't1', tag + 't2'], writes=[outkey])


def mlstm_gates(k, C, G, ps_g, gkey, pb, pbkey):
    g, e1, lf, a, tmp = G.g, G.e1, G.lf, G.a, G.tmp
    yield k.op('dve', lambda e: e.tensor_tensor(out=g[:], in0=ps_g, in1=C.gateb_bc[:], op=ALU.add), reads=[gkey], writes=[G.pfx + '.g'])
    g4 = g[:].rearrange("p (d i h) -> p d i h", d=2, i=2)
    e1v = e1[:].rearrange("p (d h) -> p d h", d=2)
    yield k.op('act', lambda e: e.activation(out=e1v, in_=g4[:, :, 1, :], func=AF.Exp, scale=-1.0), reads=[G.pfx + '.g'], writes=[G.pfx + '.e1'])
    yield k.op('act', lambda e: e.activation(out=e1[:], in_=e1[:], func=AF.Ln, bias=C.one_col[:, 0:1]), reads=[G.pfx + '.e1'], writes=[G.pfx + '.e1'])
    yield k.op('dve', lambda e: e.tensor_scalar(out=lf[:], in0=e1[:], scalar1=-1.0, scalar2=None, op0=ALU.mult), reads=[G.pfx + '.e1'], writes=[G.pfx + '.lf'])
    yield k.op('pe', lambda e: e.matmul(pb[:, 0:4], lhsT=C.tri_f[:], rhs=lf[:, 0:4], start=True, stop=True), reads=[G.pfx + '.lf'], writes=[pbkey], signal=False)
    yield k.op('pe', lambda e: e.matmul(pb[:, 4:8], lhsT=C.tri_b[:], rhs=lf[:, 4:8], start=True, stop=True), reads=[G.pfx + '.lf'], writes=[pbkey], signal=False)
    yield k.op('pe', lambda e: e.matmul(pb[:, 8:16], lhsT=C.ones_f[:], rhs=lf[:], start=True, stop=True), reads=[G.pfx + '.lf'], writes=[pbkey])
    pbs = G.pbs
    pk = G.pfx + '.pbs'
    yield k.op('dve', lambda e: e.tensor_copy(out=pbs[:], in_=pb), reads=[pbkey], writes=[pk])
    av = a[:].rearrange("p (d h) -> p d h", d=2)
    yield k.op('dve', lambda e: e.tensor_tensor(out=av, in0=g4[:, :, 0, :], in1=pbs[:, 0:8].rearrange("p (d h) -> p d h", d=2), op=ALU.subtract),
         reads=[G.pfx + '.g', pk], writes=[G.pfx + '.a'])
    yield k.op('act', lambda e: e.activation(out=G.eb[:], in_=pbs[:, 0:8], func=AF.Exp), reads=[pk], writes=[G.pfx + '.eb'])
    yield k.op('act', lambda e: e.activation(out=G.edec[:], in_=pbs[:, 8:16], func=AF.Exp), reads=[pk], writes=[G.pfx + '.edec'])
    yield k.op('pool', lambda e: e.tensor_tensor(out=tmp[:], in0=a[:], in1=pbs[:, 8:16], op=ALU.add), reads=[G.pfx + '.a', pk], writes=[G.pfx + '.tmp'])
    yield k.op('act', lambda e: e.activation(out=G.wend[:], in_=tmp[:], func=AF.Exp), reads=[G.pfx + '.tmp'], writes=[G.pfx + '.wend'])


def mlstm_local(k, C, G, ps_kv, kvkey, pl, plkeys, mv_aug_tile, mvkey):
    kw = G.kw
    yield k.op('act', lambda e: e.activation(out=mv_aug_tile[:, :, 0:64], in_=ps_kv[:, 256:512].rearrange("p (h d) -> p h d", d=64), func=AF.Copy),
         reads=[kvkey], writes=[mvkey])
    for d in range(2):
        yield k.op('dve', lambda e: e.scalar_tensor_tensor(out=kw[:, d], in0=ps_kv[:, 0:256].rearrange("p (h d) -> p h d", d=64), scalar=0.125,
                                                     in1=G.wend[:, d * 4:(d + 1) * 4].unsqueeze(2).to_broadcast([128, 4, 64]),
                                                     op0=ALU.mult, op1=ALU.mult),
             reads=[kvkey, G.pfx + '.wend'], writes=[(G.pfx + '.kw', d)])
        for h in range(4):
            yield k.op('pe', lambda e: e.matmul(pl[d][0:64, h * 80:h * 80 + 65], lhsT=kw[:, d, h, :], rhs=mv_aug_tile[:, h, 0:65], start=True, stop=True),
                 reads=[(G.pfx + '.kw', d), mvkey], writes=[plkeys[d]], signal=(h == 3))


def alloc_gates(nc, es, pfx):
    G = Ctx()
    G.pfx = pfx
    G.g = sbuf(nc, es, pfx + "g", [128, 16], F32)
    G.e1 = sbuf(nc, es, pfx + "e1", [128, 8], F32)
    G.lf = sbuf(nc, es, pfx + "lf", [128, 8], F32)
    G.a = sbuf(nc, es, pfx + "a", [128, 8], F32)
    G.tmp = sbuf(nc, es, pfx + "tmp", [128, 8], F32)
    G.eb = sbuf(nc, es, pfx + "eb", [128, 8], F32)
    G.edec = sbuf(nc, es, pfx + "edec", [128, 8], F32)
    G.wend = sbuf(nc, es, pfx + "wend", [128, 8], F32)
    G.kw = sbuf(nc, es, pfx + "kw", [128, 2, 4, 64], BF16)
    G.pbs = sbuf(nc, es, pfx + "pbs", [128, 16], F32)
    G.loc = sbuf(nc, es, pfx + "loc", [64, 2, 4, 65], F32)
    return G


def p1_mixer(k, nc, C, xT, W, l, X):
    ps = C.ps
    with ExitStack() as es:
        xnb = sbuf(nc, es, "p1_xnb", [128, 8, 512], BF16)
        wkv = sbuf(nc, es, "p1_wkv", [128, 8, 256], BF16)
        wmk = sbuf(nc, es, "p1_wmk", [128, 8, 528], BF16)
        wcv = sbuf(nc, es, "p1_wcv", [128, 8, 512], BF16)
        kTl = sbuf(nc, es, "p1_kTl", [128, NT], BF16)
        vsb = sbuf(nc, es, "p1_vsb", [128, NTILE, 128], BF16)
        halo = sbuf(nc, es, "p1_halo", [128, 2, 30], F32)
        sg = sbuf(nc, es, "p1_sg", [128, 64], F32)
        kr = sbuf(nc, es, "p1_kr", [128, 128], F32)
        mv_aug = sbuf(nc, es, "p1_mvaug", [128, 2, 4, 72], BF16)
        agg = sbuf(nc, es, "p1_agg", [64, 8, 66], F32)
        tmpS = sbuf(nc, es, "p1_tmpS", [64, 4, 65], F32)
        Ts = []
        for sfx in ("a", "b"):
            T = Ctx()
            T.sqv = sbuf(nc, es, "p1_sqv" + sfx, [128, 128], F32)
            T.ssq = sbuf(nc, es, "p1_ssq" + sfx, [128, 8], F32)
            T.t1 = sbuf(nc, es, "p1_t1" + sfx, [128, 64], F32)
            T.t2 = sbuf(nc, es, "p1_t2" + sfx, [128, 64], F32)
            T.kn = sbuf(nc, es, "p1_kn" + sfx, [128, 128], F32)
            Ts.append(T)
        kr2 = sbuf(nc, es, "p1_kr2", [128, 128], F32)
        Gs = [alloc_gates(nc, es, "p1_Ga"), alloc_gates(nc, es, "p1_Gb")]

        k.dma('pool', wkv[:], W.w_kv[l], writes=['wkv'])
        k.dma('pool', wmk[:, :, 0:512], W.w_m[l][:, :, 256:768], writes=['wmk'])
        k.dma('pool', wmk[:, :, 512:528], W.w_m[l][:, :, 1024:1040], writes=['wmk'])
        k.dma('pool', wcv[:], W.w_conv[l], writes=['wcv'])
        k.op('dve', lambda e: e.memset(mv_aug[:], 1.0), writes=[('mv', 0), ('mv', 1)])
        k.op('dve', lambda e: e.memset(agg[:], 0.0), writes=['agg'])
        k.op('dve', lambda e: e.memset(agg[:, :, 65:66], 1.0), reads=[], writes=['agg'])
        gcol = C.pp[:, l, 8:16]
        for tb in range(4):
            t0 = tb * 512
            rms_block(k, C, xT, t0, gcol, lambda kc: xnb[:, kc, :], 'xnb', xkeys_of(t0))
            if tb in (0, 3) and 'halo' not in SKIP:
                c0 = 0 if tb == 0 else 512 - PAD
                for cc in range(4):
                    for kc in range(8):
                        k.op('pe', lambda e: e.matmul(ps[6][:, cc * 16:cc * 16 + PAD], lhsT=wcv[:, kc, cc * 128:(cc + 1) * 128],
                                                      rhs=xnb[:, kc, c0:c0 + PAD], start=(kc == 0), stop=(kc == 7)),
                             reads=['wcv', 'xnb'], writes=['ps6'], signal=(kc == 7))
                h0 = 0 if tb == 0 else PAD
                k.op('act', lambda e: e.activation(out=sg[:, 0:32], in_=ps[6][:, 32:64], func=AF.Sigmoid), reads=['ps6'], writes=['sg'])
                for cc in range(2):
                    k.op('dve', lambda e: e.tensor_tensor(out=halo[:, cc, h0:h0 + PAD], in0=ps[6][:, cc * 16:cc * 16 + PAD],
                                                          in1=sg[:, cc * 16:cc * 16 + PAD], op=ALU.mult),
                         reads=['ps6', 'sg'], writes=['halo'])
            def tile_gen(tt):
                ti = tb * 4 + tt
                par = ti % 2
                G = Gs[par]
                T = Ts[par]
                krp = kr if par == 0 else kr2
                krk = 'kr%d' % par
                pkv, kkv = (ps[0], 'ps0') if par == 0 else (ps[6], 'ps6')
                pmk, kmk = (ps[2], 'ps2') if par == 0 else (ps[1], 'ps1')
                tsl = slice(tt * 128, (tt + 1) * 128)
                for kc in range(8):
                    k.op('pe', lambda e: e.matmul(pkv[:, 0:256], lhsT=xnb[:, kc, tsl], rhs=wkv[:, kc, :], start=(kc == 0), stop=(kc == 7)),
                               reads=['xnb', 'wkv'], writes=[kkv], signal=(kc == 7))
                yield
                yield k.op('act', lambda e: e.activation(out=vsb[:, ti, :], in_=pkv[:, 128:256], func=AF.Copy), reads=[kkv], writes=['vsb'])
                yield from qk_norm_rope(k, C, T, pkv[:, 0:128], kkv, 2, C.rp[:, l, 64:128], 1.0, ti, krp[:], krk, 'p1' + 'ab'[par])
                yield k.op('pe', lambda e: e.transpose(out=pkv[:, 256:384], in_=krp[:], identity=C.ident_f[:]), reads=[krk], writes=[kkv])
                yield k.op('act', lambda e: e.activation(out=kTl[:, ti * 128:(ti + 1) * 128], in_=pkv[:, 256:384], func=AF.Copy),
                           reads=[kkv], writes=['kTl'])
                for kc in range(8):
                    k.op('pe', lambda e: e.matmul(pmk[:], lhsT=xnb[:, kc, tsl], rhs=wmk[:, kc, 0:512], start=(kc == 0), stop=(kc == 7)),
                         reads=['xnb', 'wmk'], writes=[kmk], signal=(kc == 7))
                yield
                for kc in range(8):
                    k.op('pe', lambda e: e.matmul(pkv[:, 384:400], lhsT=xnb[:, kc, tsl], rhs=wmk[:, kc, 512:528], start=(kc == 0), stop=(kc == 7)),
                         reads=['xnb', 'wmk'], writes=[kkv], signal=(kc == 7))
                yield
                yield from mlstm_gates(k, C, G, pkv[:, 384:400], kkv, pkv[:, 416:432], kkv)
                lb = (4, 5) if par == 0 else (3, 7)
                yield from mlstm_local(k, C, G, pmk, kmk, [ps[lb[0]], ps[lb[1]]], ['ps%d' % lb[0], 'ps%d' % lb[1]], mv_aug[:, par], ('mv', par))
                for d in range(2):
                    yield k.op('act', lambda e: e.activation(out=G.loc[:, d], in_=ps[lb[d]][0:64, 0:320].rearrange("p (h e) -> p h e", e=80)[:, :, 0:65], func=AF.Copy),
                               reads=['ps%d' % lb[d]], writes=[(G.pfx + '.loc', d)])

            def agg_update(tt):
                par = (tb * 4 + tt) % 2
                G = Gs[par]
                edf = G.edec[0:64, 0:4]
                edb = G.edec[0:64, 4:8]
                k.op('dve', lambda e: e.tensor_tensor(out=agg[:, 0:4, 0:65], in0=agg[:, 0:4, 0:65],
                                                      in1=edf.unsqueeze(2).to_broadcast([64, 4, 65]), op=ALU.mult),
                     reads=['agg', G.pfx + '.edec'], writes=['agg'])
                k.op('dve', lambda e: e.tensor_tensor(out=agg[:, 0:4, 0:65], in0=agg[:, 0:4, 0:65], in1=G.loc[:, 0], op=ALU.add),
                     reads=['agg', (G.pfx + '.loc', 0)], writes=['agg'])
                k.op('dve', lambda e: e.tensor_tensor(out=agg[:, 0:4, 65:66], in0=agg[:, 0:4, 65:66], in1=edf.unsqueeze(2), op=ALU.mult),
                     reads=['agg', G.pfx + '.edec'], writes=['agg'])
                k.op('dve', lambda e: e.tensor_tensor(out=tmpS[:], in0=G.loc[:, 1], in1=agg[:, 4:8, 65:66].to_broadcast([64, 4, 65]), op=ALU.mult),
                     reads=['agg', (G.pfx + '.loc', 1)], writes=['tmpS'])
                k.op('dve', lambda e: e.tensor_tensor(out=agg[:, 4:8, 0:65], in0=agg[:, 4:8, 0:65], in1=tmpS[:], op=ALU.add),
                     reads=['agg', 'tmpS'], writes=['agg'])
                k.op('dve', lambda e: e.tensor_tensor(out=agg[:, 4:8, 65:66], in0=agg[:, 4:8, 65:66], in1=edb.unsqueeze(2), op=ALU.mult),
                     reads=['agg', G.pfx + '.edec'], writes=['agg'])

            for t2 in (0, 2):
                interleave([tile_gen(t2), tile_gen(t2 + 1)])
                agg_update(t2)
                agg_update(t2 + 1)
        if 'xw' in SKIP:
            k.barrier()
            return
        k.dma('sp', X.kx, kTl[:], reads=['kTl'], writes=['X.kx'])
        k.dma('sp', X.vx.rearrange("(t p) c -> p t c", p=128), vsb[:], reads=['vsb'], writes=['X.vx'])
        k.dma('sp', X.hx, halo[:], reads=['halo'], writes=['X.hx'])
        k.dma('sp', X.ax, agg[:], reads=['agg'], writes=['X.ax'])
        k.barrier()


def p2_conv(k, nc, C, xT, W, l, XA, yT_cm):
    ps = C.ps
    with ExitStack() as es:
        xnb = sbuf(nc, es, "cv_xnb", [128, 8, 512], BF16)
        wcv = sbuf(nc, es, "cv_wcv", [128, 8, 512], BF16)
        uT = sbuf(nc, es, "cv_uT", [128, 2, NT + 2 * PAD], BF16 if CONV_PE else F32)
        acc = sbuf(nc, es, "cv_acc", [128, 2, NT], F32)
        hall = sbuf(nc, es, "cv_hall", [128, 4, 2, 30], F32)
        hacc = sbuf(nc, es, "cv_hacc", [128, 2, 30], F32)
        sg = sbuf(nc, es, "cv_sg", [128, 2, 512], F32)
        sq = sbuf(nc, es, "cv_sq", [128, 512], F32)
        mean = sbuf(nc, es, "cv_mean", [128, 512], F32)
        msq = sbuf(nc, es, "cv_msq", [128, 512], F32)
        rs = sbuf(nc, es, "cv_rs", [128, 512], F32)
        tt_ = sbuf(nc, es, "cv_t", [128, 512], F32)
        cp = C.pp[:, l, 44:112].rearrange("p (c j) -> p c j", j=34)
        k.dma('pool', wcv[:], W.w_conv[l], writes=['wcv'])
        for i in range(4):
            k.dma('sp', hall[:, i], XA.hx_all[i], writes=['hall'])
        if CONV_PE:
            dgw = sbuf(nc, es, "cv_dgw", [128, 2, CONV_W, 128], BF16)
            for cc in range(2):
                for j in range(CONV_W):
                    k.op('pool', lambda e: e.tensor_scalar(out=dgw[:, cc, j, :], in0=C.ident_f[:], scalar1=cp[:, cc, j:j + 1], scalar2=None, op0=ALU.mult),
                         reads=[], writes=[('dgw', cc)])
        k.op('dve', lambda e: e.memset(hacc[:], 0.0), writes=['hacc'])
        for i in range(4):
            k.op('dve', lambda e: e.scalar_tensor_tensor(out=hacc[:, :, 0:PAD], in0=hall[:, i, :, PAD:2 * PAD], scalar=C.sel[:, 8 + i:9 + i],
                                                         in1=hacc[:, :, 0:PAD], op0=ALU.mult, op1=ALU.add),
                 reads=['hall', 'hacc'], writes=['hacc'])
            k.op('dve', lambda e: e.scalar_tensor_tensor(out=hacc[:, :, PAD:2 * PAD], in0=hall[:, i, :, 0:PAD], scalar=C.sel[:, 12 + i:13 + i],
                                                         in1=hacc[:, :, PAD:2 * PAD], op0=ALU.mult, op1=ALU.add),
                 reads=['hall', 'hacc'], writes=['hacc'])
        k.op('dve', lambda e: e.tensor_copy(out=uT[:, :, 0:PAD], in_=hacc[:, :, 0:PAD]), reads=['hacc'], writes=['uT_h'])
        k.op('dve', lambda e: e.tensor_copy(out=uT[:, :, NT + PAD:NT + 2 * PAD], in_=hacc[:, :, PAD:2 * PAD]), reads=['hacc'], writes=['uT_h'])
        gcol = C.pp[:, l, 8:16]
        for tb in range(4):
            t0 = tb * 512
            rms_block(k, C, xT, t0, gcol, lambda kc: xnb[:, kc, :], 'xnb', xkeys_of(t0))
            for cc in range(4):
                for kc in range(8):
                    k.op('pe', lambda e: e.matmul(ps[cc][:], lhsT=wcv[:, kc, cc * 128:(cc + 1) * 128], rhs=xnb[:, kc, :],
                                                  start=(kc == 0), stop=(kc == 7)),
                         reads=['wcv', 'xnb'], writes=['ps%d' % cc], signal=(kc == 7))
            for cc in range(2):
                k.op('act', lambda e: e.activation(out=sg[:, cc, :], in_=ps[2 + cc][:], func=AF.Sigmoid), reads=['ps%d' % (2 + cc)], writes=[('sg', cc)])
                k.op('dve', lambda e: e.tensor_tensor(out=uT[:, cc, PAD + t0:PAD + t0 + 512], in0=ps[cc][:], in1=sg[:, cc, :], op=ALU.mult),
                     reads=['ps%d' % cc, ('sg', cc)], writes=[('uT', tb)])
        ukeys = [('uT', i) for i in range(4)] + ['uT_h']
        if CONV_PE:
            rr = 0
            for tb in range(4):
                t0 = tb * 512
                for cc in range(2):
                    b = 4 + (rr % 4)
                    rr += 1
                    for j in range(CONV_W):
                        k.op('pe', lambda e: e.matmul(ps[b][:], lhsT=dgw[:, cc, j, :], rhs=uT[:, cc, t0 + j:t0 + j + 512], start=(j == 0), stop=(j == CONV_W - 1)),
                             reads=ukeys + [('dgw', cc)], writes=['ps%d' % b], signal=(j == CONV_W - 1))
                    k.op('act', lambda e: e.activation(out=acc[:, cc, t0:t0 + 512], in_=ps[b][:], func=AF.Identity, bias=cp[:, cc, 31:32]),
                         reads=['ps%d' % b], writes=[('acc', cc)])
        else:
            for cc in range(2):
                k.op('dve', lambda e: e.tensor_scalar(out=acc[:, cc, :], in0=uT[:, cc, 0:NT], scalar1=cp[:, cc, 0:1], scalar2=None, op0=ALU.mult),
                     reads=ukeys, writes=[('acc', cc)])
                for j in range(1, CONV_W):
                    k.op('dve', lambda e: e.scalar_tensor_tensor(out=acc[:, cc, :], in0=uT[:, cc, j:j + NT], scalar=cp[:, cc, j:j + 1],
                                                                 in1=acc[:, cc, :], op0=ALU.mult, op1=ALU.add),
                         reads=ukeys + [('acc', cc)], writes=[('acc', cc)])
                k.op('act', lambda e: e.activation(out=acc[:, cc, :], in_=acc[:, cc, :], func=AF.Identity, bias=cp[:, cc, 31:32]),
                     reads=[('acc', cc)], writes=[('acc', cc)])
        for tb in range(4):
            t0 = tb * 512
            for cc in range(2):
                k.op('pe', lambda e: e.matmul(ps[0][:], lhsT=C.ones_f[:], rhs=acc[:, cc, t0:t0 + 512], start=(cc == 0), stop=(cc == 1)),
                     reads=[('acc', cc)], writes=['ps0'], signal=(cc == 1))
            for cc in range(2):
                k.op('act', lambda e: e.activation(out=sq[:], in_=acc[:, cc, t0:t0 + 512], func=AF.Square), reads=[('acc', cc)], writes=['cvsq'])
                k.op('pe', lambda e: e.matmul(ps[1][:], lhsT=C.ones_f[:], rhs=sq[:], start=(cc == 0), stop=(cc == 1)),
                     reads=['cvsq'], writes=['ps1'])
            k.op('dve', lambda e: e.tensor_scalar(out=mean[:], in0=ps[0][:], scalar1=1.0 / 256, scalar2=None, op0=ALU.mult), reads=['ps0'], writes=['mean'])
            k.op('pool', lambda e: e.tensor_tensor(out=msq[:], in0=mean[:], in1=mean[:], op=ALU.mult), reads=['mean'], writes=['msq'])
            k.op('dve', lambda e: e.scalar_tensor_tensor(out=rs[:], in0=ps[1][:], scalar=1.0 / 256, in1=msq[:], op0=ALU.mult, op1=ALU.subtract),
                 reads=['ps1', 'msq'], writes=['rs'])
            k.op('act', lambda e: e.activation(out=rs[:], in_=rs[:], func=AF.Sqrt, bias=C.eps_col[:, 0:1]), reads=['rs'], writes=['rs'])
            k.op('dve', lambda e: e.reciprocal(out=rs[:], in_=rs[:]), reads=['rs'], writes=['rs'])
            for cc in range(2):
                k.op('dve', lambda e: e.tensor_tensor(out=tt_[:], in0=acc[:, cc, t0:t0 + 512], in1=mean[:], op=ALU.subtract),
                     reads=[('acc', cc), 'mean'], writes=['cvt'])
                k.op('dve', lambda e: e.tensor_tensor(out=tt_[:], in0=tt_[:], in1=rs[:], op=ALU.mult), reads=['cvt', 'rs'], writes=['cvt'])
                k.op('act', lambda e: e.activation(out=yT_cm[:, cc, t0:t0 + 512], in_=tt_[:], func=AF.Silu, scale=cp[:, cc, 32:33], bias=cp[:, cc, 33:34]),
                     reads=['cvt'], writes=[('ycm', cc, tb)])
        k.barrier()


def p2_mlstm(k, nc, C, xT, W, l, XA, yT_cm):
    ps = C.ps
    with ExitStack() as es:
        xnb = sbuf(nc, es, "ml_xnb", [128, 8, 512], BF16)
        wm = sbuf(nc, es, "ml_wm", [128, 8, 1040], BF16)
        mqT = sbuf(nc, es, "ml_mqT", [128, 2, NT], BF16)
        mkT = sbuf(nc, es, "ml_mkT", [128, 2, NT], BF16)
        mv_aug = sbuf(nc, es, "ml_mvaug", [128, NTILE, 4, 72], BF16)
        sgo = sbuf(nc, es, "ml_sgo", [128, NTILE, 256], BF16)
        CTf = sbuf(nc, es, "ml_CTf", [128, NTILE, 4, 72], BF16)
        CTb = sbuf(nc, es, "ml_CTb", [128, NTILE, 4, 72], BF16)
        locb = sbuf(nc, es, "ml_locb", [64, NTILE, 4, 65], F32)
        edb_all = sbuf(nc, es, "ml_edb", [64, NTILE, 4], F32)
        a_all = sbuf(nc, es, "ml_a", [128, NTILE, 8], F32)
        eb_all = sbuf(nc, es, "ml_eb", [128, NTILE, 8], F32)
        dg = sbuf(nc, es, "ml_dg", [128, 4, 128], F32)
        b_all = sbuf(nc, es, "ml_ball", [128, NTILE, 8], F32)
        S = sbuf(nc, es, "ml_S", [64, 8, 65], F32)
        al = sbuf(nc, es, "ml_al", [64, 8], F32)
        DT = sbuf(nc, es, "ml_DT", [128, 4, 128], F32)
        SwT = sbuf(nc, es, "ml_SwT", [128, 4, 128], BF16)
        tI = sbuf(nc, es, "ml_tI", [128, 4, 65], F32)
        hn = sbuf(nc, es, "ml_hn", [128, 4, 65], F32)
        den = sbuf(nc, es, "ml_den", [128, 4], F32)
        hs = sbuf(nc, es, "ml_hs", [128, 256], F32)
        hq = sbuf(nc, es, "ml_hq", [128, 256], F32)
        hss = sbuf(nc, es, "ml_hss", [128, 4], F32)
        Gs = [alloc_gates(nc, es, "ml_Ga"), alloc_gates(nc, es, "ml_Gb")]

        k.dma('pool', wm[:], W.w_m[l], writes=['wm'])
        aggs = locb[:, 0:9].rearrange("p a h e -> p (a h e)")[:, 0:4 * 8 * 66].rearrange("p (i j e) -> p i j e", i=4, j=8)
        for i in range(4):
            k.dma('sp', aggs[:, i], XA.ax_all[i], writes=['aggs'])
        k.op('dve', lambda e: e.memset(mv_aug[:], 1.0), writes=['mv_all'])
        k.op('dve', lambda e: e.memset(S[:], 0.0), writes=['S'])
        for d in range(2):
            order = range(4) if d == 0 else range(3, -1, -1)
            for i in order:
                selc = C.sel[0:64, d * 4 + i:d * 4 + i + 1]
                k.op('dve', lambda e: e.tensor_scalar(out=al[:, 0:4], in0=aggs[:, i, d * 4:(d + 1) * 4, 65], scalar1=-1.0, scalar2=selc,
                                                      op0=ALU.add, op1=ALU.mult), reads=['aggs'], writes=['al'])
                k.op('dve', lambda e: e.tensor_scalar(out=al[:, 0:4], in0=al[:, 0:4], scalar1=1.0, scalar2=None, op0=ALU.add), reads=['al'], writes=['al'])
                k.op('dve', lambda e: e.tensor_tensor(out=S[:, d * 4:(d + 1) * 4, :], in0=S[:, d * 4:(d + 1) * 4, :],
                                                      in1=al[:, 0:4].unsqueeze(2).to_broadcast([64, 4, 65]), op=ALU.mult),
                     reads=['S', 'al'], writes=['S'])
                k.op('dve', lambda e: e.scalar_tensor_tensor(out=S[:, d * 4:(d + 1) * 4, :], in0=aggs[:, i, d * 4:(d + 1) * 4, 0:65], scalar=selc,
                                                             in1=S[:, d * 4:(d + 1) * 4, :], op0=ALU.mult, op1=ALU.add),
                     reads=['S', 'aggs'], writes=['S'])
        k.barrier()
        gcol = C.pp[:, l, 8:16]
        for tb in range(4):
            t0 = tb * 512
            rms_block(k, C, xT, t0, gcol, lambda kc: xnb[:, kc, :], 'xnb', xkeys_of(t0))
            for cc in range(4):
                for kc in range(8):
                    k.op('pe', lambda e: e.matmul(ps[cc][:], lhsT=wm[:, kc, cc * 128:(cc + 1) * 128], rhs=xnb[:, kc, :], start=(kc == 0), stop=(kc == 7)),
                         reads=['wm', 'xnb'], writes=['ps%d' % cc], signal=(kc == 7))
            for cc in range(2):
                k.op('act', lambda e: e.activation(out=mqT[:, cc, t0:t0 + 512], in_=ps[cc][:], func=AF.Copy), reads=['ps%d' % cc], writes=[('mqT', tb)])
                k.op('dve', lambda e: e.tensor_scalar(out=mkT[:, cc, t0:t0 + 512], in0=ps[2 + cc][:], scalar1=0.125, scalar2=None, op0=ALU.mult),
                     reads=['ps%d' % (2 + cc)], writes=[('mkT', tb)])
            def banks(ti):
                return (4, 5, 6, 7) if ti % 2 == 0 else (0, 1, 2, 3)

            def tileA_gen(tt):
                ti = tb * 4 + tt
                G = Gs[ti % 2]
                bkv, bg, blf, blb = banks(ti)
                tsl = slice(tt * 128, (tt + 1) * 128)
                for kc in range(8):
                    k.op('pe', lambda e: e.matmul(ps[bkv][:], lhsT=xnb[:, kc, tsl], rhs=wm[:, kc, 256:768], start=(kc == 0), stop=(kc == 7)),
                               reads=['xnb', 'wm'], writes=['ps%d' % bkv], signal=(kc == 7))
                yield
                for kc in range(8):
                    k.op('pe', lambda e: e.matmul(ps[bg][:, 0:272], lhsT=xnb[:, kc, tsl], rhs=wm[:, kc, 768:1040], start=(kc == 0), stop=(kc == 7)),
                               reads=['xnb', 'wm'], writes=['ps%d' % bg], signal=(kc == 7))
                yield
                yield k.op('act', lambda e: e.activation(out=sgo[:, ti, :], in_=ps[bg][:, 0:256], func=AF.Sigmoid), reads=['ps%d' % bg], writes=[('sgo', ti)])
                yield from mlstm_gates(k, C, G, ps[bg][:, 256:272], 'ps%d' % bg, ps[bg][:, 288:304], 'ps%d' % bg)
                yield k.op('pool', lambda e: e.tensor_copy(out=a_all[:, ti, :], in_=G.a[:]), reads=[G.pfx + '.a'], writes=[('a_all', ti)])
                yield k.op('pool', lambda e: e.tensor_copy(out=eb_all[:, ti, :], in_=G.eb[:]), reads=[G.pfx + '.eb'], writes=[('eb_all', ti)])
                yield k.op('pool', lambda e: e.tensor_copy(out=b_all[:, ti, :], in_=G.pbs[:, 0:8]), reads=[G.pfx + '.pbs'], writes=[('b_all', ti)])
                yield from mlstm_local(k, C, G, ps[bkv], 'ps%d' % bkv, [ps[blf], ps[blb]], ['ps%d' % blf, 'ps%d' % blb], mv_aug[:, ti], ('mv', ti))
                yield k.op('act', lambda e: e.activation(out=locb[:, ti], in_=ps[blb][0:64, 0:320].rearrange("p (h e) -> p h e", e=80)[:, :, 0:65], func=AF.Copy),
                           reads=['ps%d' % blb], writes=[('locb', ti)])
                yield k.op('pool', lambda e: e.tensor_copy(out=edb_all[:, ti, :], in_=G.edec[0:64, 4:8]), reads=[G.pfx + '.edec'], writes=[('edb', ti)])

            def fwd_advance(tt):
                ti = tb * 4 + tt
                G = Gs[ti % 2]
                blf = banks(ti)[2]
                k.op('act', lambda e: e.activation(out=CTf[0:64, ti, 0:4:2, 0:65], in_=S[:, 0:4:2, :], func=AF.Copy), reads=['S'], writes=[('CTf', ti)])
                k.op('act', lambda e: e.activation(out=CTf[64:128, ti, 1:4:2, 0:65], in_=S[:, 1:4:2, :], func=AF.Copy), reads=['S'], writes=[('CTf', ti)])
                k.op('dve', lambda e: e.tensor_tensor(out=S[:, 0:4, :], in0=S[:, 0:4, :],
                                                      in1=G.edec[0:64, 0:4].unsqueeze(2).to_broadcast([64, 4, 65]), op=ALU.mult),
                     reads=['S', G.pfx + '.edec'], writes=['S'])
                k.op('dve', lambda e: e.tensor_tensor(out=S[:, 0:4, :], in0=S[:, 0:4, :],
                                                      in1=ps[blf][0:64, 0:320].rearrange("p (h e) -> p h e", e=80)[:, :, 0:65], op=ALU.add),
                     reads=['S', 'ps%d' % blf], writes=['S'])

            for t2 in (0, 2):
                if PASSA_INTERLEAVE:
                    interleave([tileA_gen(t2), tileA_gen(t2 + 1)])
                    fwd_advance(t2)
                    fwd_advance(t2 + 1)
                else:
                    drain(tileA_gen(t2))
                    fwd_advance(t2)
                    drain(tileA_gen(t2 + 1))
                    fwd_advance(t2 + 1)
        for ti in range(NTILE - 1, -1, -1):
            k.op('act', lambda e: e.activation(out=CTb[0:64, ti, 0:4:2, 0:65], in_=S[:, 4:8:2, :], func=AF.Copy), reads=['S'], writes=[('CTb', ti)])
            k.op('act', lambda e: e.activation(out=CTb[64:128, ti, 1:4:2, 0:65], in_=S[:, 5:8:2, :], func=AF.Copy), reads=['S'], writes=[('CTb', ti)])
            if ti > 0:
                k.op('dve', lambda e: e.tensor_tensor(out=S[:, 4:8, :], in0=S[:, 4:8, :],
                                                      in1=edb_all[:, ti, :].unsqueeze(2).to_broadcast([64, 4, 65]), op=ALU.mult),
                     reads=['S', ('edb', ti)], writes=['S'])
                k.op('dve', lambda e: e.tensor_tensor(out=S[:, 4:8, :], in0=S[:, 4:8, :], in1=locb[:, ti], op=ALU.add),
                     reads=['S', ('locb', ti)], writes=['S'])
        for ti in range(NTILE):
            tb = ti // 4
            csl = slice(ti * 128, (ti + 1) * 128)
            for hg in range(2):
                U = []
                for h in (2 * hg, 2 * hg + 1):
                    hp = (h % 2) * 64
                    hc = h // 2
                    pA = ps[h % 2]
                    k.op('pe', lambda e: e.matmul(pA[:, 0:128], lhsT=mkT[hp:hp + 64, hc, csl], rhs=mqT[hp:hp + 64, hc, csl], start=True, stop=True),
                         reads=[('mkT', tb), ('mqT', tb)], writes=['ps%d' % (h % 2)])
                    for d in range(2):
                        sl = (h % 2) * 2 + d
                        bank = 2 + sl
                        U.append((h, hp, hc, pA, d, d * 4 + h, sl, 'ps%d' % bank, ps[bank][:, 0:128], ps[bank][:, 256:321], ps[bank][:, 384:449]))
                for h, hp, hc, pA, d, j, sl, bkey, pL, pN, pI in U:
                    k.op('pool', lambda e: e.tensor_scalar(out=dg[:, sl, :], in0=C.ident_f[:], scalar1=b_all[:, ti, j:j + 1], scalar2=None, op0=ALU.mult),
                         reads=[('b_all', ti)], writes=[('dg', sl)])
                for h, hp, hc, pA, d, j, sl, bkey, pL, pN, pI in U:
                    k.op('pe', lambda e: e.matmul(pL, lhsT=C.ones_f[:], rhs=dg[:, sl, :], start=True, stop=False),
                         reads=[('dg', sl)], writes=[bkey])
                    k.op('pe', lambda e: e.matmul(pL, lhsT=C.ident_bf[:], rhs=C.maskneg[:, d, :], start=False, stop=True),
                         reads=[], writes=[bkey])
                for h, hp, hc, pA, d, j, sl, bkey, pL, pN, pI in U:
                    k.op('act', lambda e: e.activation(out=DT[:, sl, :], in_=pL, func=AF.Exp, bias=a_all[:, ti, j:j + 1]),
                         reads=[bkey, ('a_all', ti)], writes=[('DT', sl)])
                for h, hp, hc, pA, d, j, sl, bkey, pL, pN, pI in U:
                    k.op('dve', lambda e: e.tensor_tensor(out=SwT[:, sl, :], in0=DT[:, sl, :], in1=pA[:, 0:128], op=ALU.mult),
                         reads=[('DT', sl), 'ps%d' % (h % 2)], writes=[('SwT', sl)])
                for h, hp, hc, pA, d, j, sl, bkey, pL, pN, pI in U:
                    CT = CTf if d == 0 else CTb
                    ckey = ('CTf', ti) if d == 0 else ('CTb', ti)
                    k.op('pe', lambda e: e.matmul(pN, lhsT=SwT[:, sl, :], rhs=mv_aug[:, ti, h, 0:65], start=True, stop=True),
                         reads=[('SwT', sl), ('mv', ti), 'mv_all'], writes=[bkey], signal=False)
                    k.op('pe', lambda e: e.matmul(pI, lhsT=mqT[hp:hp + 64, hc, csl], rhs=CT[hp:hp + 64, ti, h, 0:65], start=True, stop=True),
                         reads=[('mqT', tb), ckey], writes=[bkey])
                for h, hp, hc, pA, d, j, sl, bkey, pL, pN, pI in U:
                    k.op('act', lambda e: e.activation(out=tI[:, sl, :], in_=pI, func=AF.Copy, scale=eb_all[:, ti, j:j + 1]),
                         reads=[bkey, ('eb_all', ti)], writes=[('tI', sl)])
                for h, hp, hc, pA, d, j, sl, bkey, pL, pN, pI in U:
                    k.op('dve', lambda e: e.tensor_tensor(out=hn[:, sl, :], in0=tI[:, sl, :], in1=pN, op=ALU.add),
                         reads=[('tI', sl), bkey], writes=[('hn', sl)])
                hk = [('hn', i) for i in range(4)]
                k.op('dve', lambda e: e.tensor_scalar(out=den[:], in0=hn[:, :, 64], scalar1=-1.0, scalar2=None, op0=ALU.mult), reads=hk, writes=['den'])
                k.op('dve', lambda e: e.tensor_tensor(out=den[:], in0=den[:], in1=hn[:, :, 64], op=ALU.max), reads=hk + ['den'], writes=['den'])
                k.op('dve', lambda e: e.tensor_scalar(out=den[:], in0=den[:], scalar1=1.0, scalar2=None, op0=ALU.max), reads=['den'], writes=['den'])
                k.op('dve', lambda e: e.reciprocal(out=den[:], in_=den[:]), reads=['den'], writes=['den'])
                for h in (2 * hg, 2 * hg + 1):
                    s0 = (h % 2) * 2
                    k.op('dve', lambda e: e.tensor_scalar(out=hs[:, h * 64:(h + 1) * 64], in0=hn[:, s0, 0:64], scalar1=den[:, s0:s0 + 1], scalar2=None, op0=ALU.mult),
                         reads=[('hn', s0), 'den'], writes=['hs'])
                    k.op('dve', lambda e: e.scalar_tensor_tensor(out=hs[:, h * 64:(h + 1) * 64], in0=hn[:, s0 + 1, 0:64], scalar=den[:, s0 + 1:s0 + 2],
                                                                 in1=hs[:, h * 64:(h + 1) * 64], op0=ALU.mult, op1=ALU.add),
                         reads=[('hn', s0 + 1), 'den', 'hs'], writes=['hs'])
            k.op('act', lambda e: e.activation(out=hq[:], in_=hs[:], func=AF.Square), reads=['hs'], writes=['hq'])
            k.op('dve', lambda e: e.tensor_reduce(out=hss[:], in_=hq[:].rearrange("p (h d) -> p h d", d=64), axis=AX.X, op=ALU.add),
                 reads=['hq'], writes=['hss'])
            k.op('act', lambda e: e.activation(out=hss[:], in_=hss[:], func=AF.Sqrt, scale=1.0 / 64, bias=C.eps_col[:, 0:1]), reads=['hss'], writes=['hss'])
            k.op('dve', lambda e: e.reciprocal(out=hss[:], in_=hss[:]), reads=['hss'], writes=['hss'])
            k.op('dve', lambda e: e.tensor_tensor(out=hq[:].rearrange("p (h d) -> p h d", d=64), in0=hs[:].rearrange("p (h d) -> p h d", d=64),
                                                  in1=hss[:].unsqueeze(2).to_broadcast([128, 4, 64]), op=ALU.mult),
                 reads=['hs', 'hss', 'hq'], writes=['hq'])
            k.op('pool', lambda e: e.tensor_tensor(out=hq[:], in0=hq[:], in1=C.rp[:, l, 144:400], op=ALU.mult), reads=['hq'], writes=['hq'])
            k.op('dve', lambda e: e.tensor_tensor(out=hq[:], in0=hq[:], in1=sgo[:, ti, :], op=ALU.mult), reads=['hq', ('sgo', ti)], writes=['hq'])
            for cc in range(2):
                k.op('pe', lambda e: e.transpose(out=ps[6 + cc][:, 0:128], in_=hq[:, cc * 128:(cc + 1) * 128], identity=C.ident_f[:]),
                     reads=['hq'], writes=['ps%d' % (6 + cc)])
                k.op('act', lambda e: e.activation(out=yT_cm[:, 2 + cc, csl], in_=ps[6 + cc][:, 0:128], func=AF.Copy),
                     reads=['ps%d' % (6 + cc)], writes=[('ycm', 2 + cc, tb)])
        k.barrier()


def p2_attn(k, nc, C, xT, W, l, XA, yT_at):
    ps = C.ps
    with ExitStack() as es:
        xnb = sbuf(nc, es, "at_xnb", [128, 8, 512], BF16)
        wq = sbuf(nc, es, "at_wq", [128, 8, 512], BF16)
        kTd = sbuf(nc, es, "at_kTd", [128, 2, SEQ], BF16)
        Vo = sbuf(nc, es, "at_Vo", [128, 64, 2, 128], BF16)
        qTb = sbuf(nc, es, "at_qTb", [128, 4, 512], BF16)
        PT = sbuf(nc, es, "at_PT", [128, 2, 2, 512], BF16)
        T = Ctx()
        T.sqv = sbuf(nc, es, "at_sqv", [128, 512], F32)
        T.ssq = sbuf(nc, es, "at_ssq", [128, 8], F32)
        T.t1 = sbuf(nc, es, "at_t1", [128, 256], F32)
        T.t2 = sbuf(nc, es, "at_t2", [128, 256], F32)
        T.kn = T.sqv
        qr = sbuf(nc, es, "at_qr", [128, 512], F32)
        rden = qr
        numB = C.rstd
        k.dma('pool', wq[:], W.w_q[l], writes=['wq'])
        k.op('dve', lambda e: e.memset(Vo[:, :, :, 64:128], 1.0), writes=['Vo1'])
        for i in range(4):
            for kvh in range(2):
                for hf in range(2):
                    k.dma('sp', kTd[hf * 64:(hf + 1) * 64, kvh, i * NT:(i + 1) * NT], XA.kx_all[i, kvh * 64:(kvh + 1) * 64, :], writes=['kTd'])
            for kvh in range(2):
                k.dma('sp', Vo[:, i * NTILE:(i + 1) * NTILE, kvh, 0:64],
                      XA.vx_all[i].rearrange("(t p) c -> p t c", p=128)[:, :, kvh * 64:(kvh + 1) * 64], writes=['Vo'])
        gcol = C.pp[:, l, 8:16]
        for qb in range(4):
            t0 = qb * 512
            rms_block(k, C, xT, t0, gcol, lambda kc: xnb[:, kc, :], 'xnb', xkeys_of(t0), bank=6)
            for tt in range(4):
                ti = qb * 4 + tt
                tsl = slice(tt * 128, (tt + 1) * 128)
                for kc in range(8):
                    k.op('pe', lambda e: e.matmul(ps[6][:], lhsT=xnb[:, kc, tsl], rhs=wq[:, kc, :], start=(kc == 0), stop=(kc == 7)),
                         reads=['xnb', 'wq'], writes=['ps6'], signal=(kc == 7))
                drain(qk_norm_rope(k, C, T, ps[6][:], 'ps6', 8, C.rp[:, l, 0:64], 0.125, ti, qr[:], 'qr', 'at'))
                for pr in range(4):
                    k.op('pe', lambda e: e.transpose(out=ps[7][:, pr * 128:(pr + 1) * 128], in_=qr[:, pr * 128:(pr + 1) * 128], identity=C.ident_f[:]),
                         reads=['qr'], writes=['ps7'])
                k.op('act', lambda e: e.activation(out=qTb[:, :, tsl], in_=ps[7][:].rearrange("p (a t) -> p a t", t=128), func=AF.Copy),
                     reads=['ps7'], writes=['qTb'])
            for pr in range(4):
                kvh = pr // 2
                ab = 4 + 2 * (pr % 2)

                def scores(kc):
                    sb_ = (kc % 2) * 2
                    ksl = slice(kc * 128, (kc + 1) * 128)
                    for hf in range(2):
                        k.op('pe', lambda e: e.matmul(ps[sb_ + hf][:], lhsT=kTd[hf * 64:(hf + 1) * 64, kvh, ksl], rhs=qTb[hf * 64:(hf + 1) * 64, pr, :],
                                                      start=True, stop=True),
                             reads=['kTd', 'qTb'], writes=['ps%d' % (sb_ + hf)])

                scores(0)
                for kc in range(64):
                    sb_ = (kc % 2) * 2
                    if EXP2:
                        k.op('act', lambda e: e.activation(out=PT[:, kc % 2].rearrange("p h n -> p (h n)"), in_=C.psd[kc % 2][:], func=AF.Exp),
                             reads=['ps%d' % sb_, 'ps%d' % (sb_ + 1)], writes=[('PT', kc % 2, 0), ('PT', kc % 2, 1)])
                    else:
                        for hf in range(2):
                            k.op('act', lambda e: e.activation(out=PT[:, kc % 2, hf, :], in_=ps[sb_ + hf][:], func=AF.Exp),
                                 reads=['ps%d' % (sb_ + hf)], writes=[('PT', kc % 2, hf)])
                    if kc + 1 < 64:
                        scores(kc + 1)
                    lhs = Vo[:, kc, kvh, :]
                    for hf in range(2):
                        k.op('pe', lambda e: e.matmul(ps[ab + hf][:], lhsT=lhs, rhs=PT[:, kc % 2, hf, :], start=(kc == 0), stop=(kc == 63)),
                             reads=['Vo', 'Vo1', ('PT', kc % 2, hf)], writes=['ps%d' % (ab + hf)], signal=True)
                pA_, pB_ = ps[ab], ps[ab + 1]
                kA, kB = 'ps%d' % ab, 'ps%d' % (ab + 1)
                k.op('dve', lambda e: e.reciprocal(out=rden[0:64, :], in_=pA_[64:128, :]), reads=[kA], writes=['qr'])
                k.op('dve', lambda e: e.tensor_tensor(out=yT_at[0:64, pr, t0:t0 + 512], in0=pA_[0:64, :], in1=rden[0:64, :], op=ALU.mult),
                     reads=[kA, 'qr'], writes=[('yat', qb)])
                k.op('dve', lambda e: e.reciprocal(out=rden[64:128, :], in_=pB_[64:128, :]), reads=[kB], writes=['qr'])
                k.op('act', lambda e: e.activation(out=numB[64:128, :], in_=pB_[0:64, :], func=AF.Copy), reads=[kB], writes=['rstd'])
                k.op('dve', lambda e: e.tensor_tensor(out=yT_at[64:128, pr, t0:t0 + 512], in0=numB[64:128, :], in1=rden[64:128, :], op=ALU.mult),
                     reads=['rstd', 'qr'], writes=[('yat', qb)])
        k.barrier()


def p2_wout(k, nc, C, xT, W, l, yT_at, yT_cm):
    ps = C.ps
    with ExitStack() as es:
        wo = sbuf(nc, es, "wo_w", [128, 8, DM], BF16)
        k.dma('pool', wo[:], W.w_out[l], writes=['wo'])
        for tb in range(4):
            t0 = tb * 512
            for dc in range(8):
                bank = dc % 4
                for kc in range(8):
                    rhs = yT_at[:, kc, t0:t0 + 512] if kc < 4 else yT_cm[:, kc - 4, t0:t0 + 512]
                    k.op('pe', lambda e: e.matmul(ps[bank][:], lhsT=wo[:, kc, dc * 128:(dc + 1) * 128], rhs=rhs, start=(kc == 0), stop=(kc == 7)),
                         reads=['wo'], writes=['ps%d' % bank], signal=(kc == 7))
                xk = ('xT', tb)
                k.op('dve', lambda e: e.tensor_tensor(out=xT[:, dc, t0:t0 + 512], in0=ps[bank][:], in1=xT[:, dc, t0:t0 + 512], op=ALU.add),
                     reads=['ps%d' % bank, xk], writes=[xk])
        k.barrier()


def p2_xattn(k, nc, C, xT, W, l, mem_in):
    ps = C.ps
    with ExitStack() as es:
        wq = sbuf(nc, es, "xa_wq", [128, 8, DM], BF16)
        wo = sbuf(nc, es, "xa_wo", [128, 8, DM], BF16)
        kxT = sbuf(nc, es, "xa_kxT", [128, 8, N_MEM], BF16)
        vx = sbuf(nc, es, "xa_vx", [128, 2, DM], BF16)
        k.dma('pool', wq[:], W.xwq[l], writes=['wq'])
        k.dma('pool', wo[:], W.xwo[l], writes=['wo'])
        with ExitStack() as esp:
            memT = sbuf(nc, esp, "xa_memT", [128, 8, N_MEM], F32)
            wkv = sbuf(nc, esp, "xa_wkv", [128, 2, 8, 512], BF16)
            mnT = sbuf(nc, esp, "xa_mnT", [128, 8, N_MEM], BF16)
            msq = sbuf(nc, esp, "xa_msq", [128, 2, N_MEM], BF16)
            kraw = sbuf(nc, esp, "xa_kraw", [128, 8, N_MEM], F32)
            ksq = sbuf(nc, esp, "xa_ksq", [128, 2, N_MEM], BF16)
            krs = sbuf(nc, esp, "xa_krs", [128, 2, N_MEM], F32)
            k.dma('sp', memT[:], mem_in, writes=['memT'])
            gm = C.pp[:, l, 24:32]
            for kc in range(8):
                j = kc % 2
                k.op('act', lambda e: e.activation(out=msq[:, j, :], in_=memT[:, kc, :], func=AF.Square), reads=['memT'], writes=[('msq', j)])
                k.op('pe', lambda e: e.matmul(ps[7][:, 0:N_MEM], lhsT=C.ones_bf[:], rhs=msq[:, j, :], start=(kc == 0), stop=(kc == 7)),
                     reads=[('msq', j)], writes=['ps7'])
            k.op('act', lambda e: e.activation(out=krs[:, 0, :], in_=ps[7][:, 0:N_MEM], func=AF.Sqrt, scale=1.0 / DM, bias=C.eps_col[:, 0:1]), reads=['ps7'], writes=[('krs', 0)])
            k.op('dve', lambda e: e.reciprocal(out=krs[:, 0, :], in_=krs[:, 0, :]), reads=[('krs', 0)], writes=[('krs', 0)])
            for kc in range(8):
                k.op('dve', lambda e: e.scalar_tensor_tensor(out=mnT[:, kc, :], in0=memT[:, kc, :], scalar=gm[:, kc:kc + 1], in1=krs[:, 0, :], op0=ALU.mult, op1=ALU.mult),
                     reads=['memT', ('krs', 0)], writes=['mnT'])
            for g4 in range(4):
                wb = g4 % 2
                k.dma('pool', wkv[:, wb], W.xwkv[l][:, :, g4 * 512:(g4 + 1) * 512], writes=[('xwkv', wb)])
                if g4 < 2:
                    for c4 in range(4):
                        cc = g4 * 4 + c4
                        for kc in range(8):
                            k.op('pe', lambda e: e.matmul(ps[c4][:, 0:N_MEM], lhsT=wkv[:, wb, kc, c4 * 128:(c4 + 1) * 128], rhs=mnT[:, kc, :],
                                                          start=(kc == 0), stop=(kc == 7)),
                                 reads=[('xwkv', wb), 'mnT'], writes=['ps%d' % c4], signal=(kc == 7))
                        k.op('act', lambda e: e.activation(out=kraw[:, cc, :], in_=ps[c4][:, 0:N_MEM], func=AF.Copy), reads=['ps%d' % c4], writes=[('kraw', cc)])
                else:
                    for mt in range(2):
                        for kc in range(8):
                            k.op('pe', lambda e: e.matmul(ps[4 + mt][:], lhsT=mnT[:, kc, mt * 128:(mt + 1) * 128], rhs=wkv[:, wb, kc, :],
                                                          start=(kc == 0), stop=(kc == 7)),
                                 reads=[('xwkv', wb), 'mnT'], writes=['ps%d' % (4 + mt)], signal=(kc == 7))
                        k.op('act', lambda e: e.activation(out=vx[:, mt, (g4 - 2) * 512:(g4 - 1) * 512], in_=ps[4 + mt][:], func=AF.Copy),
                             reads=['ps%d' % (4 + mt)], writes=['vx'])
            gk = C.pp[:, l, 42:44]
            for h in range(4):
                hb = h % 2
                for dcc in range(2):
                    cc = h * 2 + dcc
                    k.op('act', lambda e: e.activation(out=ksq[:, dcc, :], in_=kraw[:, cc, :], func=AF.Square), reads=[('kraw', cc)], writes=[('ksq', dcc)])
                    k.op('pe', lambda e: e.matmul(ps[6 + hb][:, 0:N_MEM], lhsT=C.ones_bf[:], rhs=ksq[:, dcc, :], start=(dcc == 0), stop=(dcc == 1)),
                         reads=[('ksq', dcc)], writes=['ps%d' % (6 + hb)])
                k.op('act', lambda e: e.activation(out=krs[:, hb, :], in_=ps[6 + hb][:, 0:N_MEM], func=AF.Sqrt, scale=1.0 / 256, bias=C.eps_col[:, 0:1]),
                     reads=['ps%d' % (6 + hb)], writes=[('krs', hb)])
                k.op('dve', lambda e: e.reciprocal(out=krs[:, hb, :], in_=krs[:, hb, :]), reads=[('krs', hb)], writes=[('krs', hb)])
                for dcc in range(2):
                    cc = h * 2 + dcc
                    k.op('dve', lambda e: e.scalar_tensor_tensor(out=kxT[:, cc, :], in0=kraw[:, cc, :], scalar=gk[:, dcc:dcc + 1], in1=krs[:, hb, :],
                                                                 op0=ALU.mult, op1=ALU.mult),
                         reads=[('kraw', cc), ('krs', hb)], writes=['kxT'])
            k.barrier()
        xnb = sbuf(nc, es, "xa_xnb", [128, 8, 512], BF16)
        qraw = sbuf(nc, es, "xa_qraw", [128, 8, 512], F32)
        qsq = sbuf(nc, es, "xa_qsq", [128, 8, 512], BF16)
        qrs = sbuf(nc, es, "xa_qrs", [128, 4, 512], F32)
        qT = sbuf(nc, es, "xa_qT", [128, 8, 512], BF16)
        PT = sbuf(nc, es, "xa_PT", [128, 4, 2, 512], BF16)
        rden = sbuf(nc, es, "xa_rden", [128, 4, 512], F32)
        oT = sbuf(nc, es, "xa_oT", [128, 8, 512], BF16)
        gx = C.pp[:, l, 16:24]
        gq = C.pp[:, l, 40:42]
        rr = [0]

        def nb():
            b = rr[0] % 7
            rr[0] += 1
            return b

        for tb in range(4):
            t0 = tb * 512
            rms_block(k, C, xT, t0, gx, lambda kc: xnb[:, kc, :], 'xnb', xkeys_of(t0), bank=7)
            for cc in range(8):
                b = nb()
                for kc in range(8):
                    k.op('pe', lambda e: e.matmul(ps[b][:], lhsT=wq[:, kc, cc * 128:(cc + 1) * 128], rhs=xnb[:, kc, :], start=(kc == 0), stop=(kc == 7)),
                         reads=['wq', 'xnb'], writes=['ps%d' % b], signal=(kc == 7))
                k.op('act', lambda e: e.activation(out=qraw[:, cc, :], in_=ps[b][:], func=AF.Copy), reads=['ps%d' % b], writes=[('qraw', cc)])
                k.op('pool', lambda e: e.tensor_tensor(out=qsq[:, cc, :], in0=qraw[:, cc, :], in1=qraw[:, cc, :], op=ALU.mult),
                     reads=[('qraw', cc)], writes=[('qsq', cc)])
            for h in range(4):
                b = nb()
                for dcc in range(2):
                    k.op('pe', lambda e: e.matmul(ps[b][:], lhsT=C.ones_bf[:], rhs=qsq[:, h * 2 + dcc, :], start=(dcc == 0), stop=(dcc == 1)),
                         reads=[('qsq', h * 2 + dcc)], writes=['ps%d' % b], signal=(dcc == 1))
                k.op('act', lambda e: e.activation(out=qrs[:, h, :], in_=ps[b][:], func=AF.Sqrt, scale=1.0 / 256, bias=C.eps_col[:, 0:1]),
                     reads=['ps%d' % b], writes=[('qrs', h)])
                k.op('dve', lambda e: e.reciprocal(out=qrs[:, h, :], in_=qrs[:, h, :]), reads=[('qrs', h)], writes=[('qrs', h)])
                k.op('pool', lambda e: e.tensor_scalar(out=qrs[:, h, :], in0=qrs[:, h, :], scalar1=1.0 / 16, scalar2=None, op0=ALU.mult),
                     reads=[('qrs', h)], writes=[('qrs', h)])
                for dcc in range(2):
                    cc = h * 2 + dcc
                    k.op('dve', lambda e: e.scalar_tensor_tensor(out=qT[:, cc, :], in0=qraw[:, cc, :], scalar=gq[:, dcc:dcc + 1], in1=qrs[:, h, :],
                                                                 op0=ALU.mult, op1=ALU.mult),
                         reads=[('qraw', cc), ('qrs', h)], writes=[('qT', cc)])
            for h in range(4):
                for mt in range(2):
                    b = nb()
                    for dcc in range(2):
                        cc = h * 2 + dcc
                        k.op('pe', lambda e: e.matmul(ps[b][:], lhsT=kxT[:, cc, mt * 128:(mt + 1) * 128], rhs=qT[:, cc, :], start=(dcc == 0), stop=(dcc == 1)),
                             reads=['kxT', ('qT', cc)], writes=['ps%d' % b], signal=(dcc == 1))
                    k.op('act', lambda e: e.activation(out=PT[:, h, mt, :], in_=ps[b][:], func=AF.Exp), reads=['ps%d' % b], writes=[('PT', h, mt)])
            for h in range(4):
                b = nb()
                for mt in range(2):
                    k.op('pe', lambda e: e.matmul(ps[b][:], lhsT=C.ones_bf[:], rhs=PT[:, h, mt, :], start=(mt == 0), stop=(mt == 1)),
                         reads=[('PT', h, mt)], writes=['ps%d' % b], signal=(mt == 1))
                k.op('dve', lambda e: e.reciprocal(out=rden[:, h, :], in_=ps[b][:]), reads=['ps%d' % b], writes=[('rden', h)])
            for cc in range(8):
                h = cc // 2
                b = nb()
                for mt in range(2):
                    k.op('pe', lambda e: e.matmul(ps[b][:], lhsT=vx[:, mt, cc * 128:(cc + 1) * 128], rhs=PT[:, h, mt, :], start=(mt == 0), stop=(mt == 1)),
                         reads=['vx', ('PT', h, mt)], writes=['ps%d' % b], signal=(mt == 1))
                k.op('dve', lambda e: e.tensor_tensor(out=oT[:, cc, :], in0=ps[b][:], in1=rden[:, h, :], op=ALU.mult),
                     reads=['ps%d' % b, ('rden', h)], writes=[('oT', cc)])
            for dc in range(8):
                b = nb()
                for kc in range(8):
                    k.op('pe', lambda e: e.matmul(ps[b][:], lhsT=wo[:, kc, dc * 128:(dc + 1) * 128], rhs=oT[:, kc, :], start=(kc == 0), stop=(kc == 7)),
                         reads=['wo', ('oT', kc)], writes=['ps%d' % b], signal=(kc == 7))
                xk = ('xT', tb)
                k.op('dve', lambda e: e.tensor_tensor(out=xT[:, dc, t0:t0 + 512], in0=ps[b][:], in1=xT[:, dc, t0:t0 + 512], op=ALU.add),
                     reads=['ps%d' % b, xk], writes=[xk])
        k.barrier()


WSPEC = [
    ("f1_w13", [NFC, 128, 2, 8, 128]), ("f1_w2", [8, 128, NFC, 128]),
    ("f2_w13", [NFC, 128, 2, 8, 128]), ("f2_w2", [8, 128, NFC, 128]),
    ("w_q", [128, 8, 512]), ("w_kv", [128, 8, 256]), ("w_conv", [128, 8, 512]), ("w_m", [128, 8, 1040]),
    ("w_out", [128, 8, DM]), ("xwq", [128, 8, DM]), ("xwkv", [128, 8, 2048]), ("xwo", [128, 8, DM]),
]
NPP = 112
NRP = 400


def build(prog, debug=()):
    nc = bass.Bass("TRN2", target_bir_lowering=False)

    def din(name, shape, dt=F32):
        return nc.dram_tensor(name, list(shape), dt, kind="ExternalInput").ap()

    def dout(name, shape, dt=F32):
        return nc.dram_tensor(name, list(shape), dt, kind="ExternalOutput").ap()

    class WL:
        def __init__(self, name, shp):
            self.name, self.shp, self.aps = name, shp, {}

        def __getitem__(self, l):
            if l not in self.aps:
                self.aps[l] = din("%s_%d" % (self.name, l), self.shp)
            return self.aps[l]

    W = Ctx()
    for name, shp in WSPEC:
        setattr(W, name, WL(name, shp))
    FUSED = (prog == 'F')
    if FUSED:
        x_in4 = din("xT_in", [4, 128, 8, NT])
        x_out4 = dout("xT_out", [4, 128, 8, NT])
        cos_in4 = din("cos", [4, 128, NTILE, 32])
        sin_in4 = din("sin", [4, 128, NTILE, 32])
        sel_in4 = din("sel", [4, 128, 16])
        xq = nc.dram_tensor("xq_scr", [4, 128, 8, NT], F32).ap()
        x_in = x_out = cos_in = sin_in = sel_in = None
    else:
        x_in = din("xT_in", [128, 8, NT])
        x_out = dout("xT_out", [128, 8, NT])
        cos_in = din("cos", [128, NTILE, 32])
        sin_in = din("sin", [128, NTILE, 32])
        sel_in = din("sel", [128, 16])
    mem_in = din("memT", [128, 8, N_MEM])
    pp_in = din("pp", [128, 2, NPP])
    rp_in = din("rp", [128, 2, NRP])
    cst_in = din("cst", [128, 5, 128])
    XO = Ctx()
    XA = Ctx()
    if prog in ('A', 'B'):
        XO.kx = dout("kx", [128, NT], BF16)
        XO.vx = dout("vx", [NT, 128], BF16)
        XO.hx = dout("hx", [128, 2, 30])
        XO.ax = dout("ax", [64, 8, 66])
    if FUSED:
        XA.kx_all = nc.dram_tensor("kx_scr", [4, 128, NT], BF16).ap()
        XA.vx_all = nc.dram_tensor("vx_scr", [4, NT, 128], BF16).ap()
        XA.hx_all = nc.dram_tensor("hx_scr", [4, 128, 2, 30], F32).ap()
        XA.ax_all = nc.dram_tensor("ax_scr", [4, 64, 8, 66], F32).ap()
    if prog in ('B', 'C'):
        XA.kx_all = din("kx_all", [4, 128, NT], BF16)
        XA.vx_all = din("vx_all", [4, NT, 128], BF16)
        XA.hx_all = din("hx_all", [4, 128, 2, 30])
        XA.ax_all = din("ax_all", [4, 64, 8, 66])

    with ExitStack() as es:
        k = K(nc, es)
        C = Ctx()
        C.psd = [es.enter_context(nc.psum_tensor("psd%d" % i, [128, 1024], F32)) for i in range(4)]
        C.ps = [C.psd[i // 2][:, (i % 2) * 512:(i % 2 + 1) * 512] for i in range(8)]
        xT = sbuf(nc, es, "xT", [128, 8, NT], F32)
        C.pp = sbuf(nc, es, "pp_s", [128, 2, NPP], F32)
        C.rp = sbuf(nc, es, "rp_s", [128, 2, NRP], F32)
        C.cos = sbuf(nc, es, "cos_s", [128, NTILE, 32], F32)
        C.sin = sbuf(nc, es, "sin_s", [128, NTILE, 32], F32)
        C.sel = sbuf(nc, es, "sel_s", [128, 16], F32)
        cst = sbuf(nc, es, "cst_s", [128, 3, 128], F32)
        C.ident_f = cst[:, 0, :]
        C.tri_f = cst[:, 1, :]
        C.tri_b = cst[:, 2, :]
        C.ones_f = sbuf(nc, es, "ones_f", [128, 128], F32)
        C.ones_bf = sbuf(nc, es, "ones_bf", [128, 128], BF16)
        C.ident_bf = sbuf(nc, es, "ident_bf", [128, 128], BF16)
        C.maskneg = sbuf(nc, es, "maskneg", [128, 2, 128], BF16)
        C.eps_col = sbuf(nc, es, "eps_col", [128, 1], F32)
        C.one_col = sbuf(nc, es, "one_col", [128, 1], F32)
        C.sq = sbuf(nc, es, "c_sq", [128, 2, 512], BF16)
        C.rstd = sbuf(nc, es, "c_rstd", [128, 512], F32)
        C.gateb_bc = None

        def load_x(src):
            for i in range(4):
                k.dma('sp', xT[:, :, i * 512:(i + 1) * 512], src[:, :, i * 512:(i + 1) * 512], writes=[('xT', i)])

        def store_x(dst):
            for i in range(4):
                k.dma('sp', dst[:, :, i * 512:(i + 1) * 512], xT[:, :, i * 512:(i + 1) * 512], reads=[('xT', i)])

        def load_quarter_consts(q):
            k.dma('sp', C.cos[:], cos_in4[q], writes=['c'])
            k.dma('sp', C.sin[:], sin_in4[q], writes=['c'])
            k.dma('sp', C.sel[:], sel_in4[q], writes=['c'])
            k.barrier()

        if not FUSED:
            load_x(x_in)
            k.dma('sp', C.cos[:], cos_in, writes=['c'])
            k.dma('sp', C.sin[:], sin_in, writes=['c'])
            k.dma('sp', C.sel[:], sel_in, writes=['c'])
        k.dma('sp', C.pp[:], pp_in, writes=['c'])
        k.dma('sp', C.rp[:], rp_in, writes=['c'])
        k.dma('sp', cst[:], cst_in[:, 0:3, :], writes=['c'])
        k.op('dve', lambda e: e.memset(C.ones_f[:], 1.0), writes=['c1'])
        k.op('dve', lambda e: e.memset(C.ones_bf[:], 1.0), writes=['c2'])
        k.op('dve', lambda e: e.memset(C.eps_col[:], EPS), writes=['c3'])
        k.op('dve', lambda e: e.memset(C.one_col[:], 1.0), writes=['c4'])
        k.op('dve', lambda e: e.tensor_copy(out=C.ident_bf[:], in_=cst[:, 0, :]), reads=['c'], writes=['c5'])
        with ExitStack() as es0:
            mtmp = sbuf(nc, es0, "mtmp", [128, 2, 128], F32)
            k.dma('sp', mtmp[:], cst_in[:, 3:5, :], writes=['mtmp'])
            k.op('dve', lambda e: e.tensor_copy(out=C.maskneg[:], in_=mtmp[:]), reads=['mtmp'], writes=['c6'])
            k.barrier()

        def P1(l):
            C.gateb_bc = C.rp[:, l, 128:144]
            if 'ffn' not in SKIP:
                ffn(k, nc, C, xT, C.pp[:, l, 0:8], W.f1_w13[l], W.f1_w2[l])
            if 'x1' in debug:
                k.dump("x1_%d" % l, xT[:], [128, 8, NT], F32, [('xT', i) for i in range(4)])
            if 'ffn' in SKIP:
                pass
            if 'mixer' not in SKIP:
                p1_mixer(k, nc, C, xT, W, l, XO)

        def P2(l):
            C.gateb_bc = C.rp[:, l, 128:144]
            with ExitStack() as es2:
                yT_cm = sbuf(nc, es2, "yT_cm", [128, 4, NT], BF16)
                if 'conv' not in SKIP:
                    p2_conv(k, nc, C, xT, W, l, XA, yT_cm)
                if 'mlstm' not in SKIP:
                    p2_mlstm(k, nc, C, xT, W, l, XA, yT_cm)
                with ExitStack() as es3:
                    yT_at = sbuf(nc, es3, "yT_at", [128, 4, NT], BF16)
                    if 'attn' not in SKIP:
                        p2_attn(k, nc, C, xT, W, l, XA, yT_at)
                    if 'y' in debug:
                        k.dump("yat_%d" % l, yT_at[:], [128, 4, NT], BF16, [])
                        k.dump("ycm_%d" % l, yT_cm[:], [128, 4, NT], BF16, [])
                    if 'wout' not in SKIP:
                        p2_wout(k, nc, C, xT, W, l, yT_at, yT_cm)
            if 'x2' in debug:
                k.dump("x2_%d" % l, xT[:], [128, 8, NT], F32, [('xT', i) for i in range(4)])
            if 'xattn' not in SKIP:
                p2_xattn(k, nc, C, xT, W, l, mem_in)
            if 'x3' in debug:
                k.dump("x3_%d" % l, xT[:], [128, 8, NT], F32, [('xT', i) for i in range(4)])
            if 'ffn2' not in SKIP:
                ffn(k, nc, C, xT, C.pp[:, l, 32:40], W.f2_w13[l], W.f2_w2[l])

        if prog == 'A':
            P1(0)
        elif prog == 'B':
            P2(0)
            P1(1)
        elif prog == 'C':
            P2(1)
        else:
            for l in range(2):
                for q in range(4):
                    load_quarter_consts(q)
                    load_x(x_in4[q] if l == 0 else xq[q])
                    XO.kx, XO.vx, XO.hx, XO.ax = XA.kx_all[q], XA.vx_all[q], XA.hx_all[q], XA.ax_all[q]
                    P1(l)
                    store_x(xq[q])
                    k.barrier()
                for q in range(4):
                    load_quarter_consts(q)
                    load_x(xq[q])
                    P2(l)
                    store_x(xq[q] if l == 0 else x_out4[q])
                    k.barrier()
        if not FUSED:
            store_x(x_out)
        k.barrier()
        stats = (k.nops, k.nwaits, list(k.dumps))
    used = []
    for name, _ in WSPEC:
        used += ['%s_%d' % (name, l) for l in getattr(W, name).aps]
    return nc, used, stats


def _perm64():
    return np.concatenate([np.arange(0, 64, 2), np.arange(1, 64, 2)])


def prep_weights(inp):
    f = lambda a: np.ascontiguousarray(a, dtype=np.float32)
    L = 2
    out = {}

    def kmaj(w):
        return f(w.reshape(L, 8, 128, w.shape[-1]).transpose(0, 2, 1, 3))

    for pfx, a, b in (("f1", "ffn1_w13", "ffn1_w2"), ("f2", "ffn2_w13", "ffn2_w2")):
        w13 = inp[a]
        out[pfx + "_w13"] = f(w13.reshape(L, 8, 128, 2, NFC, 128).transpose(0, 4, 2, 3, 1, 5))
        w2 = inp[b]
        out[pfx + "_w2"] = f(w2.reshape(L, NFC, 128, 8, 128).transpose(0, 3, 2, 1, 4))
    w_in = inp["w_in"]
    p64 = _perm64()
    qcols = np.concatenate([h * 64 + p64 for h in range(8)])
    kcols = 512 + np.concatenate([h * 64 + p64 for h in range(2)])
    out["w_q"] = kmaj(w_in[:, :, qcols])
    out["w_kv"] = kmaj(np.concatenate([w_in[:, :, kcols], w_in[:, :, 640:768]], axis=-1))
    out["w_conv"] = kmaj(w_in[:, :, 768:1280])
    out["w_m"] = kmaj(w_in[:, :, 1280:2320])
    out["w_out"] = kmaj(inp["w_out"])
    out["xwq"] = kmaj(inp["xattn_wq"])
    out["xwkv"] = kmaj(inp["xattn_wkv"])
    out["xwo"] = kmaj(inp["xattn_wo"])
    pp = np.zeros((128, L, NPP), np.float32)
    rp = np.zeros((128, L, NRP), np.float32)
    for l in range(L):
        col = lambda v: v.reshape(-1, 128).T
        pp[:, l, 0:8] = col(inp["ffn1_norm"][l])
        pp[:, l, 8:16] = col(inp["mix_norm"][l])
        pp[:, l, 16:24] = col(inp["xattn_norm"][l])
        pp[:, l, 24:32] = col(inp["mem_norm"][l])
        pp[:, l, 32:40] = col(inp["ffn2_norm"][l])
        pp[:, l, 40:42] = col(inp["xattn_q_norm"][l])
        pp[:, l, 42:44] = col(inp["xattn_k_norm"][l])
        cp = np.zeros((128, 2, 34), np.float32)
        cp[:, :, 0:31] = inp["conv_dw_w"][l].T.reshape(2, 128, 31).transpose(1, 0, 2)
        cp[:, :, 31] = col(inp["conv_dw_b"][l])
        cp[:, :, 32] = col(inp["conv_ln_g"][l])
        cp[:, :, 33] = col(inp["conv_ln_b"][l])
        pp[:, l, 44:112] = cp.reshape(128, 68)
        rp[:, l, 0:64] = inp["attn_q_norm"][l][p64][None, :]
        rp[:, l, 64:128] = inp["attn_k_norm"][l][p64][None, :]
        rp[:, l, 128:144] = inp["mlstm_gate_b"][l].reshape(16)[None, :]
        rp[:, l, 144:400] = inp["mlstm_out_norm"][l].reshape(256)[None, :]
    out["pp"] = pp
    out["rp"] = rp
    cst = np.zeros((128, 5, 128), np.float32)
    s = np.arange(128)[:, None]
    t = np.arange(128)[None, :]
    cst[:, 0] = (s == t)
    cst[:, 1] = (s <= t)
    cst[:, 2] = (s >= t)
    cst[:, 3] = np.where(s <= t, 0.0, NEG)
    cst[:, 4] = np.where(s >= t, 0.0, NEG)
    out["cst"] = cst
    return out


def prep_core(inp, c):
    b, p = c // 4, c % 4
    d = {}
    xs = inp["x"][b, p * NT:(p + 1) * NT]
    d["xT_in"] = np.ascontiguousarray(xs.reshape(NT, 8, 128).transpose(2, 1, 0), dtype=np.float32)
    d["memT"] = np.ascontiguousarray(inp["mem"][b].reshape(N_MEM, 8, 128).transpose(2, 1, 0), dtype=np.float32)
    pos = p * NT + np.arange(NT)
    row = (pos // GRID_W).astype(np.float32)
    colp = (pos % GRID_W).astype(np.float32)
    inv_freq = (np.float32(10000.0) ** (-np.arange(16, dtype=np.float32) / np.float32(16))).astype(np.float32)
    ang = np.concatenate([row[:, None] * inv_freq[None, :], colp[:, None] * inv_freq[None, :]], axis=-1).astype(np.float32)
    d["cos"] = np.ascontiguousarray(np.cos(ang).astype(np.float32).reshape(NTILE, 128, 32).transpose(1, 0, 2))
    d["sin"] = np.ascontiguousarray(np.sin(ang).astype(np.float32).reshape(NTILE, 128, 32).transpose(1, 0, 2))
    sel = np.zeros((128, 16), np.float32)
    for i in range(4):
        sel[:, i] = 1.0 if i < p else 0.0
        sel[:, 4 + i] = 1.0 if i > p else 0.0
        sel[:, 8 + i] = 1.0 if i == p - 1 else 0.0
        sel[:, 12 + i] = 1.0 if i == p + 1 else 0.0
    d["sel"] = sel
    return d


def run_prog(prog, inp_w, cores, extra, debug=()):
    nc, used, stats = build(prog, debug)
    in_maps = []
    for c in range(NCORES):
        m = {kk: inp_w[kk] for kk in ("pp", "rp", "cst")}
        for nm in used:
            base, l = nm.rsplit("_", 1)
            m[nm] = inp_w[base][int(l)]
        m.update(cores[c])
        m.update(extra[c])
        in_maps.append(m)
    res = run_bass_kernel_spmd(nc, in_maps, core_ids=list(range(NCORES)))
    return res.results


def gather_exchange(results):
    ex = []
    for c in range(NCORES):
        g0 = (c // 4) * 4
        ex.append({
            "kx_all": np.stack([results[g0 + i]["kx"] for i in range(4)]),
            "vx_all": np.stack([results[g0 + i]["vx"] for i in range(4)]),
            "hx_all": np.stack([results[g0 + i]["hx"] for i in range(4)]),
            "ax_all": np.stack([results[g0 + i]["ax"] for i in range(4)]),
        })
    return ex


def prep_core_fused(inp, b):
    qs = [prep_core(inp, b * 4 + p) for p in range(4)]
    d = {"memT": qs[0]["memT"]}
    for kk in ("xT_in", "cos", "sin", "sel"):
        d[kk] = np.stack([q[kk] for q in qs])
    return d


def kernel_unfused(**inputs):
    inp = {kk: np.asarray(v) for kk, v in inputs.items()}
    w = prep_weights(inp)
    cores = [prep_core(inp, c) for c in range(NCORES)]
    rA = run_prog('A', w, cores, [{} for _ in range(NCORES)])
    ex = gather_exchange(rA)
    for c in range(NCORES):
        ex[c]["xT_in"] = rA[c]["xT_out"]
    rB = run_prog('B', w, cores, ex)
    ex = gather_exchange(rB)
    for c in range(NCORES):
        ex[c]["xT_in"] = rB[c]["xT_out"]
    rC = run_prog('C', w, cores, ex)
    out = np.zeros((2, SEQ, DM), np.float32)
    for c in range(NCORES):
        b, p = c // 4, c % 4
        out[b, p * NT:(p + 1) * NT] = rC[c]["xT_out"].transpose(2, 1, 0).reshape(NT, DM)
    return out


def kernel_fused(**inputs):
    inp = {kk: np.asarray(v) for kk, v in inputs.items()}
    w = prep_weights(inp)
    pb = [prep_core_fused(inp, b) for b in range(2)]
    cores = [pb[c // 4] for c in range(NCORES)]
    r = run_prog('F', w, cores, [{} for _ in range(NCORES)])
    out = np.zeros((2, SEQ, DM), np.float32)
    for b in range(2):
        xo = r[b * 4]["xT_out"]
        for p in range(4):
            out[b, p * NT:(p + 1) * NT] = xo[p].transpose(2, 1, 0).reshape(NT, DM)
    return out


FUSED_DEFAULT = False


def kernel(**inputs):
    return kernel_fused(**inputs) if FUSED_DEFAULT else kernel_unfused(**inputs)
```
